# Optimizing a Trainium2 kernel written in Bass

```python
import math
import jax, jax.numpy as jnp
from jax import lax
import numpy as np

D_MODEL = 2048
BATCH = 4
SEQ = 4096
DEPTH = 2

NORM_EPS = 1e-6
BLOCK = 128

LRU_WIDTH = 1024
LRU_BLOCKS = 8
LRU_BLOCK_DIM = LRU_WIDTH // LRU_BLOCKS
CONV_WIDTH = 4
LRU_C = 8.0
SWA_Q_HEADS = 16
SWA_KV_HEADS = 2
SWA_HEAD_DIM = 64
SWA_GROUP = SWA_Q_HEADS // SWA_KV_HEADS
SWA_WIDTH = SWA_Q_HEADS * SWA_HEAD_DIM
SWA_KV_WIDTH = SWA_KV_HEADS * SWA_HEAD_DIM
WINDOW = 128
EVEN_SPLITS = [LRU_WIDTH, LRU_WIDTH, SWA_WIDTH, SWA_KV_WIDTH, SWA_KV_WIDTH, SWA_WIDTH]
EVEN_IN = sum(EVEN_SPLITS)
EVEN_MIX = LRU_WIDTH + SWA_WIDTH

MLA_HEADS = 8
MLA_Q_RANK = 768
MLA_KV_RANK = 512
MLA_NOPE = 128
MLA_ROPE = 64
MLA_V = 128
MLA_WIDTH = MLA_HEADS * MLA_V
ROPE_THETA = 10000.0
DIFF_HEADS = 8
DIFF_QK = 64
DIFF_V = 2 * DIFF_QK
DIFF_QK_WIDTH = DIFF_HEADS * 2 * DIFF_QK
DIFF_WIDTH = DIFF_HEADS * DIFF_V
DIFF_LAYER_IDX = 1
DIFF_LAMBDA_INIT = 0.8 - 0.6 * math.exp(-0.3 * DIFF_LAYER_IDX)
ODD_SPLITS = [MLA_Q_RANK, MLA_KV_RANK, MLA_ROPE, MLA_WIDTH,
              DIFF_QK_WIDTH, DIFF_QK_WIDTH, DIFF_WIDTH, DIFF_WIDTH]
ODD_IN = sum(ODD_SPLITS)
ODD_MIX = MLA_WIDTH + DIFF_WIDTH

kernel_name = "hybrid_rglru_swa_mla_diff_trunk"


def rmsnorm(x, g):
    xf = x.astype(jnp.float32)
    y = xf * lax.rsqrt(jnp.mean(xf * xf, axis=-1, keepdims=True) + NORM_EPS)
    return (y * g.astype(jnp.float32)).astype(x.dtype)


def split_cols(z, sizes):
    idx = [int(i) for i in np.cumsum(sizes)[:-1]]
    return jnp.split(z, idx, axis=-1)


def rope(x, pos):
    half = x.shape[-1] // 2
    freq = ROPE_THETA ** (-jnp.arange(half, dtype=jnp.float32) / half)
    ang = pos.astype(jnp.float32)[:, None] * freq[None, :]
    cos = jnp.cos(ang)[:, None, :]
    sin = jnp.sin(ang)[:, None, :]
    xf = x.astype(jnp.float32)
    x1, x2 = xf[..., :half], xf[..., half:]
    return jnp.concatenate([x1 * cos - x2 * sin, x1 * sin + x2 * cos], axis=-1).astype(x.dtype)


def causal_depthwise_conv(x, w, b):
    y = lax.conv_general_dilated(
        x, w[:, None, :].astype(x.dtype), window_strides=(1,),
        padding=[(CONV_WIDTH - 1, 0)], dimension_numbers=("NWC", "WIO", "NWC"),
        feature_group_count=x.shape[-1])
    return y + b


def rg_lru(x, gx_w, gx_b, ga_w, ga_b, lru_lambda):
    B, S, _ = x.shape
    xb = x.reshape(B, S, LRU_BLOCKS, LRU_BLOCK_DIM)
    gate_i = jax.nn.sigmoid(jnp.einsum("bsnd,nde->bsne", xb, gx_w) + gx_b).reshape(B, S, LRU_WIDTH)
    gate_r = jax.nn.sigmoid(jnp.einsum("bsnd,nde->bsne", xb, ga_w) + ga_b).reshape(B, S, LRU_WIDTH)
    log_a = -LRU_C * jax.nn.softplus(-lru_lambda.astype(jnp.float32)) * gate_r.astype(jnp.float32)
    a = jnp.exp(log_a)
    mult = jnp.sqrt(-jnp.expm1(2.0 * log_a))
    u = mult * gate_i.astype(jnp.float32) * x.astype(jnp.float32)

    def combine(c1, c2):
        a1, b1 = c1
        a2, b2 = c2
        return a1 * a2, a2 * b1 + b2

    _, h = lax.associative_scan(combine, (a, u), axis=1)
    return h.astype(x.dtype)


def sliding_window_gqa_sinks(q, k, v, sinks):
    B, S = q.shape[:2]
    nb = S // BLOCK
    qb = q.reshape(B, nb, BLOCK, SWA_KV_HEADS, SWA_GROUP, SWA_HEAD_DIM)

    def band(t):
        tp = jnp.pad(t, ((0, 0), (BLOCK, 0), (0, 0), (0, 0)))
        cur = tp[:, BLOCK:].reshape(B, nb, BLOCK, SWA_KV_HEADS, SWA_HEAD_DIM)
        prev = tp[:, :S].reshape(B, nb, BLOCK, SWA_KV_HEADS, SWA_HEAD_DIM)
        return jnp.concatenate([prev, cur], axis=2)

    kb, vb = band(k), band(v)
    s = jnp.einsum("bnqhgd,bnkhd->bnhgqk", qb, kb).astype(jnp.float32) * (SWA_HEAD_DIM ** -0.5)
    n = jnp.arange(nb)[:, None, None]
    qi = jnp.arange(BLOCK)[None, :, None]
    ki = jnp.arange(2 * BLOCK)[None, None, :]
    rel = qi + BLOCK - ki
    kpos = n * BLOCK - BLOCK + ki
    mask = (rel >= 0) & (rel < WINDOW) & (kpos >= 0)
    s = jnp.where(mask[None, :, None, None], s, -jnp.inf)
    sink = sinks.astype(jnp.float32).reshape(1, 1, SWA_KV_HEADS, SWA_GROUP, 1, 1)
    m = jnp.maximum(jnp.max(s, axis=-1, keepdims=True), sink)
    p = jnp.exp(s - m)
    p = p / (jnp.sum(p, axis=-1, keepdims=True) + jnp.exp(sink - m))
    o = jnp.einsum("bnhgqk,bnkhd->bnqhgd", p.astype(vb.dtype), vb)
    return o.reshape(B, S, SWA_WIDTH)


def sweep_query_blocks(block_fn, q):
    B, S = q.shape[:2]
    nb = S // BLOCK
    qb = jnp.moveaxis(q.reshape((B, nb, BLOCK) + q.shape[2:]), 1, 0)

    def body(args):
        qblk, n = args
        return block_fn(qblk, n * BLOCK + jnp.arange(BLOCK))

    o = lax.map(body, (qb, jnp.arange(nb)))
    o = jnp.moveaxis(o, 0, 1)
    return o.reshape((B, S) + o.shape[3:])


def causal_mla_attention(q, k, v):
    S = k.shape[1]
    kpos = jnp.arange(S)
    scale = (MLA_NOPE + MLA_ROPE) ** -0.5

    def block_fn(qblk, qpos):
        s = jnp.einsum("bqhd,bkhd->bhqk", qblk, k).astype(jnp.float32) * scale
        s = jnp.where(kpos[None, :] <= qpos[:, None], s, -jnp.inf)
        p = jax.nn.softmax(s, axis=-1)
        return jnp.einsum("bhqk,bkhd->bqhd", p.astype(v.dtype), v)

    return sweep_query_blocks(block_fn, q)


def causal_diff_attention(q, k, v, lam):
    S = k.shape[1]
    kpos = jnp.arange(S)
    scale = DIFF_QK ** -0.5

    def block_fn(qblk, qpos):
        s = jnp.einsum("bqhcd,bkhcd->bchqk", qblk, k).astype(jnp.float32) * scale
        s = jnp.where(kpos[None, :] <= qpos[:, None], s, -jnp.inf)
        p = jax.nn.softmax(s, axis=-1)
        a = p[:, 0] - lam * p[:, 1]
        return jnp.einsum("bhqk,bkhd->bqhd", a.astype(v.dtype), v)

    return sweep_query_blocks(block_fn, q)


def even_layer(h, w_in, conv_w, conv_b, gx_w, gx_b, ga_w, ga_b, lru_lambda, sinks, w_out):
    B, S, _ = h.shape
    z = h @ w_in
    lru_x, lru_gate, q, k, v, swa_gate = split_cols(z, EVEN_SPLITS)
    lru_x = causal_depthwise_conv(lru_x, conv_w, conv_b)
    y_a = rg_lru(lru_x, gx_w, gx_b, ga_w, ga_b, lru_lambda) * jax.nn.silu(lru_gate)
    q = q.reshape(B, S, SWA_Q_HEADS, SWA_HEAD_DIM)
    k = k.reshape(B, S, SWA_KV_HEADS, SWA_HEAD_DIM)
    v = v.reshape(B, S, SWA_KV_HEADS, SWA_HEAD_DIM)
    y_b = sliding_window_gqa_sinks(q, k, v, sinks) * jax.nn.silu(swa_gate)
    return jnp.concatenate([y_a, y_b], axis=-1) @ w_out


def odd_layer(h, w_in, q_norm, w_uq, kv_norm, w_ukv, lambda_q1, lambda_k1, lambda_q2, lambda_k2,
              subln, w_out):
    B, S, _ = h.shape
    pos = jnp.arange(S)
    z = h @ w_in
    c_q, c_kv, k_rope, mla_gate, dq, dk, dv, diff_gate = split_cols(z, ODD_SPLITS)
    q = (rmsnorm(c_q, q_norm) @ w_uq).reshape(B, S, MLA_HEADS, MLA_NOPE + MLA_ROPE)
    q = jnp.concatenate([q[..., :MLA_NOPE], rope(q[..., MLA_NOPE:], pos)], axis=-1)
    kv = (rmsnorm(c_kv, kv_norm) @ w_ukv).reshape(B, S, MLA_HEADS, MLA_NOPE + MLA_V)
    k_nope, v_c = kv[..., :MLA_NOPE], kv[..., MLA_NOPE:]
    k_r = rope(k_rope.reshape(B, S, 1, MLA_ROPE), pos)
    k_c = jnp.concatenate([k_nope, jnp.broadcast_to(k_r, (B, S, MLA_HEADS, MLA_ROPE))], axis=-1)
    y_c = causal_mla_attention(q, k_c, v_c).reshape(B, S, MLA_WIDTH) * jax.nn.silu(mla_gate)
    qd = dq.reshape(B, S, DIFF_HEADS, 2, DIFF_QK)
    kd = dk.reshape(B, S, DIFF_HEADS, 2, DIFF_QK)
    vd = dv.reshape(B, S, DIFF_HEADS, DIFF_V)
    lam = (jnp.exp(jnp.sum(lambda_q1.astype(jnp.float32) * lambda_k1.astype(jnp.float32)))
           - jnp.exp(jnp.sum(lambda_q2.astype(jnp.float32) * lambda_k2.astype(jnp.float32)))
           + DIFF_LAMBDA_INIT)
    od = causal_diff_attention(qd, kd, vd, lam)
    od = rmsnorm(od, subln) * (1.0 - DIFF_LAMBDA_INIT)
    y_d = od.reshape(B, S, DIFF_WIDTH) * jax.nn.silu(diff_gate)
    return jnp.concatenate([y_c, y_d], axis=-1) @ w_out


def setup_inputs(seed: int = 0) -> dict:
    key = jax.random.key(seed)
    ks = jax.random.split(key, 32)
    f32 = jnp.float32

    def dense(k, shape, fan_in):
        return jax.random.normal(k, shape, f32) * (fan_in ** -0.5)

    def gain(k, shape):
        return 1.0 + 0.05 * jax.random.normal(k, shape, f32)

    def small(k, shape, s=0.02):
        return s * jax.random.normal(k, shape, f32)

    u = jax.random.uniform(ks[10], (LRU_WIDTH,), f32, minval=0.9, maxval=0.999)
    a_base = u ** (1.0 / LRU_C)
    lru_lambda = jnp.log(a_base) - jnp.log1p(-a_base)

    return {
        "x": jax.random.normal(ks[0], (BATCH, SEQ, D_MODEL), f32),
        "norm_gains": gain(ks[1], (DEPTH, D_MODEL)),
        "final_norm_gain": gain(ks[2], (D_MODEL,)),
        "l0_w_in": dense(ks[3], (D_MODEL, EVEN_IN), D_MODEL),
        "l0_conv_w": dense(ks[4], (CONV_WIDTH, LRU_WIDTH), CONV_WIDTH),
        "l0_conv_b": small(ks[5], (LRU_WIDTH,)),
        "l0_gate_x_w": dense(ks[6], (LRU_BLOCKS, LRU_BLOCK_DIM, LRU_BLOCK_DIM), LRU_BLOCK_DIM),
        "l0_gate_x_b": small(ks[7], (LRU_BLOCKS, LRU_BLOCK_DIM)),
        "l0_gate_a_w": dense(ks[8], (LRU_BLOCKS, LRU_BLOCK_DIM, LRU_BLOCK_DIM), LRU_BLOCK_DIM),
        "l0_gate_a_b": small(ks[9], (LRU_BLOCKS, LRU_BLOCK_DIM)),
        "l0_lru_lambda": lru_lambda,
        "l0_sinks": 0.5 * jax.random.normal(ks[11], (SWA_Q_HEADS,), f32),
        "l0_w_out": dense(ks[12], (EVEN_MIX, D_MODEL), EVEN_MIX),
        "l1_w_in": dense(ks[13], (D_MODEL, ODD_IN), D_MODEL),
        "l1_q_norm": gain(ks[14], (MLA_Q_RANK,)),
        "l1_w_uq": dense(ks[15], (MLA_Q_RANK, MLA_HEADS * (MLA_NOPE + MLA_ROPE)), MLA_Q_RANK),
        "l1_kv_norm": gain(ks[16], (MLA_KV_RANK,)),
        "l1_w_ukv": dense(ks[17], (MLA_KV_RANK, MLA_HEADS * (MLA_NOPE + MLA_V)), MLA_KV_RANK),
        "l1_lambda_q1": small(ks[18], (DIFF_QK,), 0.1),
        "l1_lambda_k1": small(ks[19], (DIFF_QK,), 0.1),
        "l1_lambda_q2": small(ks[20], (DIFF_QK,), 0.1),
        "l1_lambda_k2": small(ks[21], (DIFF_QK,), 0.1),
        "l1_subln": gain(ks[22], (DIFF_V,)),
        "l1_w_out": dense(ks[23], (ODD_MIX, D_MODEL), ODD_MIX),
    }


def reference(x, norm_gains, final_norm_gain,
              l0_w_in, l0_conv_w, l0_conv_b, l0_gate_x_w, l0_gate_x_b, l0_gate_a_w, l0_gate_a_b,
              l0_lru_lambda, l0_sinks, l0_w_out,
              l1_w_in, l1_q_norm, l1_w_uq, l1_kv_norm, l1_w_ukv,
              l1_lambda_q1, l1_lambda_k1, l1_lambda_q2, l1_lambda_k2, l1_subln, l1_w_out):
    even_params = (l0_w_in, l0_conv_w, l0_conv_b, l0_gate_x_w, l0_gate_x_b, l0_gate_a_w,
                   l0_gate_a_b, l0_lru_lambda, l0_sinks, l0_w_out)
    odd_params = (l1_w_in, l1_q_norm, l1_w_uq, l1_kv_norm, l1_w_ukv, l1_lambda_q1, l1_lambda_k1,
                  l1_lambda_q2, l1_lambda_k2, l1_subln, l1_w_out)
    for layer in range(DEPTH):
        h = rmsnorm(x, norm_gains[layer])
        if layer % 2 == 0:
            x = x + even_layer(h, *even_params)
        else:
            x = x + odd_layer(h, *odd_params)
    return rmsnorm(x, final_norm_gain)
```

```python
import numpy as np
from concourse.bass_utils import run_bass_kernel_spmd
from contextlib import ExitStack
import concourse.bass as bass
import concourse.mybir as mybir

F32 = mybir.dt.float32
BF16 = mybir.dt.bfloat16
I32 = mybir.dt.int32
U32 = mybir.dt.uint32
AF = mybir.ActivationFunctionType
ALU = mybir.AluOpType
AX = mybir.AxisListType


class Buf:
    def __init__(self, t, name, multi=False):
        self.t = t
        self.name = name
        self.w = {}
        self.r = {}
        self.multi = multi
        self.dsem = None
        self.dcount = 0

    def __getitem__(self, k):
        return self.t[k]


class Eng:
    def __init__(self, name, h, sem):
        self.name = name
        self.h = h
        self.sem = sem
        self.count = 0
        self.seen = {}
        self.stream = []


class Prog:
    def __init__(self):
        self.nc = bass.Bass("TRN2", target_bir_lowering=False)
        self.es = ExitStack()
        nc = self.nc
        self.E = {}
        for name in ["tensor", "vector", "scalar", "gpsimd", "sync"]:
            sem = self.es.enter_context(nc.semaphore("s_" + name))
            self.E[name] = Eng(name, getattr(nc, name), sem)
        self.nbuf = 0
        self.n_inst = 0
        self.dsems_all = {}
        self.dsem_free = []
        self.stack = [self.es]
        self.scope_bufs = [[]]

    def sb(self, name, shape, dtype):
        self.nbuf += 1
        name = f"{name}_{self.nbuf}"
        t = self.stack[-1].enter_context(self.nc.sbuf_tensor(name, list(shape), dtype))
        b = Buf(t, name)
        self.scope_bufs[-1].append(b)
        return b

    def ps(self, name, shape, dtype):
        self.nbuf += 1
        name = f"{name}_{self.nbuf}"
        t = self.stack[-1].enter_context(self.nc.psum_tensor(name, list(shape), dtype))
        b = Buf(t, name)
        self.scope_bufs[-1].append(b)
        return b

    def dram(self, name, shape, dtype, kind="Internal"):
        t = self.nc.dram_tensor(name, list(shape), dtype, kind=kind).ap()
        return Buf(t, name, multi=True)

    def _wait(self, eng, evs):
        for key, (sem, val) in evs.items():
            if sem is eng.sem:
                if val > eng.count:
                    continue
                if eng.name in ("tensor", "sync"):
                    continue
            if eng.seen.get(key, 0) >= val:
                continue
            eng.seen[key] = val
            eng.h.wait_ge(sem, val)

    @staticmethod
    def _merge(d, key, sem, val):
        if key not in d or d[key][1] < val:
            d[key] = (sem, val)

    def _deps(self, eng, reads, writes):
        evs = {}
        for b in reads:
            for k, (s, v) in b.w.items():
                self._merge(evs, k, s, v)
        for b in writes:
            if b.multi:
                continue
            for k, (s, v) in b.w.items():
                self._merge(evs, k, s, v)
            for k, (s, v) in b.r.items():
                self._merge(evs, k, s, v)
        self._wait(eng, evs)

    def _record(self, reads, writes, key, sem, val):
        for b in reads:
            self._merge(b.r, key, sem, val)
        for b in writes:
            if b.multi:
                self._merge(b.w, key, sem, val)
            else:
                b.w = {key: (sem, val)}
                b.r = {}

    def op(self, engname, fn, reads=(), writes=(), signal=True):
        eng = self.E[engname]
        self._deps(eng, reads, writes)
        self.n_inst += 1
        if signal:
            eng.count += 1
            val = eng.count
            sem = eng.sem
            fn(eng.h).then_inc(sem, 1)
        else:
            val = eng.count + 1
            fn(eng.h)
        self._record(reads, writes, id(eng.sem), eng.sem, val)

    def dma(self, qname, fn, owner, reads=(), writes=(), n=1):
        eng = self.E[qname]
        if owner.dsem is None:
            if self.dsem_free:
                owner.dsem = self.dsem_free.pop()
            else:
                owner.dsem = self.es.enter_context(self.nc.semaphore("d%d" % len(self.dsems_all)))
                self.dsems_all[id(owner.dsem)] = (owner.dsem, [0])
        self._deps(eng, reads, writes)
        self.n_inst += n
        cnt = self.dsems_all[id(owner.dsem)][1]
        cnt[0] += 16 * n
        sem = owner.dsem
        val = cnt[0]

        def emit(h, fn=fn, sem=sem, n=n):
            r = fn(h)
            if n == 1 and not isinstance(r, (list, tuple)):
                r = [r]
            assert len(r) == n
            for ins in r:
                ins.then_inc(sem, 16)
        emit(eng.h)
        self._record(reads, writes, id(sem), sem, val)

    def barrier_all_to(self, engname, bufs):
        eng = self.E[engname]
        evs = {}
        for b in bufs:
            for k, (s, v) in b.w.items():
                self._merge(evs, k, s, v)
        self._wait(eng, evs)

    def full_barrier(self):
        evs = {}
        for e in self.E.values():
            if e.count > 0:
                evs[id(e.sem)] = (e.sem, e.count)
        for k, (sem, cnt) in self.dsems_all.items():
            if cnt[0] > 0:
                evs[k] = (sem, cnt[0])
        for e in self.E.values():
            ev2 = {k: v for k, v in evs.items() if v[0] is not e.sem}
            self._wait(e, ev2)

    def scope(self):
        return _Scope(self)

    def finish(self):
        self.es.close()
        return self.nc


class _Scope:
    def __init__(self, P):
        self.P = P

    def __enter__(self):
        self.P.stack.append(ExitStack())
        self.P.scope_bufs.append([])
        return self

    def __exit__(self, *a):
        P = self.P
        P.full_barrier()
        for b in P.scope_bufs.pop():
            if b.dsem is not None:
                P.dsem_free.append(b.dsem)
                b.dsem = None
        P.stack.pop().close()
        return False


D = 2048
FC = 16
EPS = 1e-6


def chunk_cols(w, cols):
    K = w.shape[0]
    sub = w[:, cols]
    return np.ascontiguousarray(sub.reshape(K // 128, 128, len(cols)).transpose(1, 0, 2))


def prep_inputs(inp, core, TR):
    b, j = core // 2, core % 2
    NB = TR // 128
    NOWN = NB // 2
    ar = np.arange
    d = {}
    d["x"] = np.ascontiguousarray(inp["x"][b, :TR])
    d["g0"] = np.ascontiguousarray(inp["norm_gains"][0].reshape(FC, 128).T)
    d["g1"] = np.ascontiguousarray(inp["norm_gains"][1].reshape(FC, 128).T)
    d["gf"] = np.ascontiguousarray(inp["final_norm_gain"].reshape(1, D))
    w = inp["l0_w_in"]
    ch = []
    for n in range(8):
        ch.append(ar(n * 128, (n + 1) * 128))
    for n in range(8):
        ch.append(1024 + ar(n * 128, (n + 1) * 128))
    for n in range(8):
        ch.append(2048 + ar(n * 128, (n + 1) * 128))
    ch.append(3072 + np.concatenate([ar(64), ar(64)]))
    ch.append(3072 + 64 + np.concatenate([ar(64), ar(64)]))
    for n in range(8):
        ch.append(3328 + ar(n * 128, (n + 1) * 128))
    ch.append(3200 + ar(128))
    d["w0v"] = np.stack([chunk_cols(w, c) for c in ch])
    lv = np.zeros((128, 8, 8), np.float32)
    for n in range(8):
        sl = slice(n * 128, (n + 1) * 128)
        lv[:, n, 0:4] = inp["l0_conv_w"][:, sl].T
        lv[:, n, 4] = inp["l0_conv_b"][sl]
        lv[:, n, 5] = inp["l0_gate_x_b"][n]
        lv[:, n, 6] = inp["l0_gate_a_b"][n]
        lv[:, n, 7] = inp["l0_lru_lambda"][sl]
    d["lruvec"] = lv
    d["gxw"] = np.ascontiguousarray(inp["l0_gate_x_w"])
    d["gaw"] = np.ascontiguousarray(inp["l0_gate_a_w"])
    d["sinks"] = np.ascontiguousarray(inp["l0_sinks"].reshape(1, 16))
    d["wo0"] = chunk_cols(inp["l0_w_out"], ar(2048))
    w1 = inp["l1_w_in"]
    o_cq, o_ckv, o_kr, o_mg, o_dq, o_dk, o_dv, o_dg = 0, 768, 1280, 1344, 2368, 3392, 4416, 5440
    chq = [o_cq + ar(n * 128, (n + 1) * 128) for n in range(6)]
    chq += [o_mg + ar(n * 128, (n + 1) * 128) for n in range(8)]
    chq += [o_dq + ar(n * 128, (n + 1) * 128) for n in range(8)]
    chq += [o_dg + ar(n * 128, (n + 1) * 128) for n in range(8)]
    d["w1q"] = np.stack([chunk_cols(w1, c) for c in chq])
    chk = [o_kr + np.concatenate([ar(64), ar(32, 64), ar(0, 32)])]
    chk += [o_dk + ar(n * 128, (n + 1) * 128) for n in range(8)]
    chk += [o_ckv + ar(n * 128, (n + 1) * 128) for n in range(4)]
    d["w1k"] = np.stack([chunk_cols(w1, c) for c in chk])
    d["wdv"] = np.stack([chunk_cols(w1, o_dv + ar(g * 512, (g + 1) * 512)) for g in range(2)])
    d["qn"] = np.ascontiguousarray(inp["l1_q_norm"].reshape(6, 128).T)
    d["kvn"] = np.ascontiguousarray(inp["l1_kv_norm"].reshape(4, 128).T)
    hd = np.arange(8)[:, None] * 256 + np.arange(128)[None, :]
    d["wukv_k"] = chunk_cols(inp["l1_w_ukv"], hd.reshape(-1))
    d["wukv_v"] = chunk_cols(inp["l1_w_ukv"], (hd + 128).reshape(-1))
    fr = (10000.0 ** (-np.arange(32, dtype=np.float32) / 32)).astype(np.float32)
    cst = np.zeros((64, 2), np.float32)
    cst[:, 0] = np.concatenate([fr, fr])
    cst[:, 1] = np.concatenate([-np.ones(32), np.ones(32)])
    d["ropec"] = cst
    d["wuq"] = chunk_cols(inp["l1_w_uq"], ar(1536))
    sw = np.concatenate([h * 192 + 128 + np.concatenate([ar(32, 64), ar(0, 32)]) for h in range(8)])
    d["wuqs"] = chunk_cols(inp["l1_w_uq"], sw)
    d["lams"] = np.ascontiguousarray(np.concatenate([inp["l1_lambda_q1"], inp["l1_lambda_k1"],
                                                    inp["l1_lambda_q2"], inp["l1_lambda_k2"]]).reshape(1, 256))
    d["subln"] = np.ascontiguousarray(inp["l1_subln"].reshape(128, 1))
    d["wo1"] = chunk_cols(inp["l1_w_out"], ar(2048))
    jwv = np.zeros((128, 2), np.float32)
    jwv[:, j] = 1.0
    d["jw"] = jwv
    kk, qq = np.meshgrid(ar(128), ar(128), indexing="ij")
    tri = (kk <= qq).astype(np.float32)
    d["maskab"] = np.ascontiguousarray(np.concatenate(
        [tri if j == 0 else np.ones_like(tri), np.zeros_like(tri) if j == 0 else tri], axis=1))
    d["pos_all"] = ar(TR, dtype=np.float32).reshape(1, TR)
    d["pos_own"] = np.concatenate([(2 * i + j) * 128 + ar(128) for i in range(NOWN)]).astype(np.float32).reshape(1, NOWN * 128)
    return d


def rmsnorm_to_T(P, xb, ncol, gain_bc, xn, ident, pst, dst_fn, stats, junk, gain_cols=None, evac_eng="vector"):
    s = stats
    P.op("scalar", lambda h: h.activation(out=junk[:, 0:ncol], in_=xb[:, 0:ncol], func=AF.Square, accum_out=s[0][:]),
         reads=[xb], writes=[junk, s[0]])
    P.op("vector", lambda h: h.tensor_scalar(out=s[1][:], in0=s[0][:], scalar1=1.0 / ncol, scalar2=EPS,
                                             op0=ALU.mult, op1=ALU.add), reads=[s[0]], writes=[s[1]])
    P.op("scalar", lambda h: h.activation(out=s[2][:], in_=s[1][:], func=AF.Sqrt), reads=[s[1]], writes=[s[2]])
    P.op("vector", lambda h: h.reciprocal(out=s[3][:], in_=s[2][:]), reads=[s[2]], writes=[s[3]])
    if gain_bc is None:
        P.op("scalar", lambda h: h.activation(out=xn[:, 0:ncol], in_=xb[:, 0:ncol], func=AF.Copy, scale=s[3][:]),
             reads=[xb, s[3]], writes=[xn])
    else:
        P.op("vector", lambda h: h.scalar_tensor_tensor(out=xn[:, 0:ncol], in0=xb[:, 0:ncol], scalar=s[3][:],
                                                        in1=gain_bc[:, 0:ncol], op0=ALU.mult, op1=ALU.mult),
             reads=[xb, s[3], gain_bc], writes=[xn])
    nk = ncol // 128
    k0 = 0
    pi = 0
    while k0 < nk:
        n = min(8, nk - k0)
        pt = pst[pi % len(pst)]
        pi += 1
        for kk in range(n):
            k = k0 + kk
            P.op("tensor", lambda h, pt=pt, kk=kk, k=k: h.transpose(out=pt[:, kk, :], in_=xn[:, k * 128:(k + 1) * 128],
                                                                    identity=ident[:]),
                 reads=[xn, ident], writes=[pt], signal=(kk == n - 1))
        dst_fn(k0, n, pt)
        k0 += n


def build(TR=4096, stop=99, dbg=None):
    NB = TR // 128
    NT = TR // 512
    NOWN = NB // 2
    TRo = NOWN * 128
    NTo = TRo // 512
    P = Prog()
    nc = P.nc
    I = {}

    def din(name, shape, dt=F32):
        I[name] = P.dram(name, shape, dt, kind="ExternalInput")
        return I[name]

    x = din("x", [TR, D]); g0 = din("g0", [128, FC]); g1 = din("g1", [128, FC]); gf = din("gf", [1, D])
    w0v = din("w0v", [35, 128, 16, 128]); lruvec = din("lruvec", [128, 8, 8])
    gxw = din("gxw", [8, 128, 128]); gaw = din("gaw", [8, 128, 128]); sinks = din("sinks", [1, 16])
    wo0 = din("wo0", [128, 16, 2048])
    w1q = din("w1q", [30, 128, 16, 128]); w1k = din("w1k", [13, 128, 16, 128])
    wdv = din("wdv", [2, 128, 16, 512]); qn = din("qn", [128, 6]); kvn = din("kvn", [128, 4])
    wuq = din("wuq", [128, 6, 1536]); wuqs = din("wuqs", [128, 6, 512])
    wukv_k = din("wukv_k", [128, 4, 1024]); wukv_v = din("wukv_v", [128, 4, 1024]); ropec = din("ropec", [64, 2])
    lams = din("lams", [1, 256]); subln = din("subln", [128, 1]); wo1 = din("wo1", [128, 16, 2048])
    jw = din("jw", [128, 2]); maskab = din("maskab", [128, 256])
    pos_all = din("pos_all", [1, TR]); pos_own = din("pos_own", [1, TRo])
    out = P.dram("out", [TRo, D], F32, kind="ExternalOutput")

    dkind = "ExternalOutput" if dbg else "Internal"
    yT0_d = P.dram("yT0_d", [16, 128, TR], BF16, kind="ExternalOutput" if dbg == "yT0" else "Internal")
    x1_d = P.dram("x1_d", [TR, D], F32, kind="ExternalOutput" if dbg == "x1" else "Internal")
    h1T_d = P.dram("h1T_d", [TR, 16, 128], BF16, kind="ExternalOutput" if dbg == "x1" else "Internal")

    ident = P.sb("ident", [128, 128], BF16)
    io = P.sb("io", [128, 128], I32)
    P.op("gpsimd", lambda h: h.iota(io[:], pattern=[[1, 128]], base=0, channel_multiplier=-1), writes=[io])
    P.op("vector", lambda h: h.tensor_single_scalar(out=ident[:], in_=io[:], scalar=0.0, op=ALU.is_equal),
         reads=[io], writes=[ident])
    m2 = P.sb("m2", [128, 256], BF16)
    P.op("vector", lambda h: h.tensor_single_scalar(out=m2[:, 0:128], in_=io[:], scalar=0.0, op=ALU.is_ge),
         reads=[io], writes=[m2])
    P.op("vector", lambda h: h.tensor_single_scalar(out=m2[:, 128:256], in_=io[:], scalar=0.0, op=ALU.is_lt),
         reads=[io], writes=[m2])
    ones_bf = P.sb("ones_bf", [128, 128], BF16)
    P.op("vector", lambda h: h.memset(ones_bf[:], 1.0), writes=[ones_bf])
    stats = [[P.sb(f"st{i}_{q}", [128, 1], F32) for q in range(4)] for i in range(2)]

    with P.scope():
        hT = P.sb("hT", [128, FC, TR], BF16)
        with P.scope():
            gt = P.sb("gt", [128, FC], F32)
            P.dma("sync", lambda h: h.dma_start(out=gt[:], in_=g0[:]), gt, writes=[gt])
            xb = [P.sb(f"xb{i}", [128, D], F32) for i in range(2)]
            xn = [P.sb(f"xn{i}", [128, D], BF16) for i in range(2)]
            junk = P.sb("junk", [128, D], BF16)
            pst = [P.ps(f"pst{i}", [128, 8, 128], BF16) for i in range(2)]
            for n in range(NB):
                b = xb[n % 2]
                P.dma("sync", lambda h, b=b, n=n: h.dma_start(out=b[:], in_=x[n * 128:(n + 1) * 128, :]), b, writes=[b])

                def dst(k0, nk, pt, n=n):
                    P.op("vector", lambda h: h.tensor_tensor(
                        out=hT[:, k0:k0 + nk, n * 128:(n + 1) * 128], in0=pt[:, 0:nk, :],
                        in1=gt[:, k0:k0 + nk].unsqueeze(2).to_broadcast([128, nk, 128]), op=ALU.mult),
                        reads=[pt, gt], writes=[hT])
                rmsnorm_to_T(P, b, D, None, xn[n % 2], ident, pst, dst, stats[n % 2], junk)

        wch = [P.sb(f"wch{i}", [128, 16, 128], BF16) for i in range(3)]
        wch_i = [0]

        def load_chunk(src, c):
            t = wch[wch_i[0] % len(wch)]
            wch_i[0] += 1
            P.dma("gpsimd", lambda h: [h.dma_start(out=t[:, 0:8, :], in_=src[c, :, 0:8, :]),
                                       h.dma_start(out=t[:, 8:16, :], in_=src[c, :, 8:16, :])], t, writes=[t], n=2)
            return t

        pz = [P.ps(f"pz{i}", [128, 512], F32) for i in range(5)]
        pz_i = [0]

        def nextpz():
            t = pz[pz_i[0] % len(pz)]
            pz_i[0] += 1
            return t

        def proj_fm(wt, tt, ps, m0=0, m1=128):
            for k in range(16):
                P.op("tensor", lambda h, k=k: h.matmul(ps[0:m1 - m0, :], lhsT=wt[:, k, m0:m1],
                                                       rhs=hT[:, k, tt * 512:(tt + 1) * 512],
                                                       start=(k == 0), stop=(k == 15)),
                     reads=[wt, hT], writes=[ps], signal=(k == 15))

        if True:
            with P.scope():
                lv = P.sb("lv", [128, 8, 8], F32)
                P.dma("sync", lambda h: h.dma_start(out=lv[:], in_=lruvec[:]), lv, writes=[lv])
                cv = P.sb("cv", [128, 8, 2], F32)
                tmpv = P.sb("tmpv", [128, 8], F32)
                P.op("scalar", lambda h: h.activation(out=tmpv[:], in_=lv[:, :, 7], func=AF.Exp, scale=-1.0),
                     reads=[lv], writes=[tmpv])
                P.op("vector", lambda h: h.tensor_scalar_add(out=tmpv[:], in0=tmpv[:], scalar1=1.0), reads=[tmpv], writes=[tmpv])
                P.op("scalar", lambda h: h.activation(out=tmpv[:], in_=tmpv[:], func=AF.Ln), reads=[tmpv], writes=[tmpv])
                P.op("vector", lambda h: h.tensor_scalar_mul(out=cv[:, :, 0], in0=tmpv[:], scalar1=-8.0), reads=[tmpv], writes=[cv])
                P.op("vector", lambda h: h.tensor_scalar_mul(out=cv[:, :, 1], in0=tmpv[:], scalar1=-16.0), reads=[tmpv], writes=[cv])
                gw = [[P.sb(f"gw{i}_{q}", [128, 128], BF16) for q in range(2)] for i in range(2)]
                xbuf = [P.sb(f"xbuf{i}", [128, 515], F32) for i in range(2)]
                hs = [P.sb(f"hs{i}", [128, 512], F32) for i in range(2)]
                ya = [P.sb(f"ya{i}", [128, 512], BF16) for i in range(2)]

                def T(name, dt=F32):
                    return [P.sb(f"{name}{i}", [128, 512], dt) for i in range(2)]
                xc, xcb, gi, gr, av, a2, uu, sg = T("xc"), T("xcb", BF16), T("gi"), T("gr"), T("av"), T("a2"), T("uu"), T("sg")
                it = 0
                for n in range(8):
                    wx = load_chunk(w0v, n)
                    wg = load_chunk(w0v, 8 + n)
                    gwn = gw[n % 2]
                    P.dma("gpsimd", lambda h: h.dma_start(out=gwn[0][:], in_=gxw[n]), gwn[0], writes=[gwn[0]])
                    P.dma("gpsimd", lambda h: h.dma_start(out=gwn[1][:], in_=gaw[n]), gwn[1], writes=[gwn[1]])
                    for tt in range(NT):
                        q = it % 2
                        it += 1
                        yan = ya[q]
                        zx, zg, pgi, pgr = nextpz(), nextpz(), nextpz(), nextpz()
                        proj_fm(wx, tt, zx)
                        proj_fm(wg, tt, zg)
                        xbq, xbp = xbuf[q], xbuf[1 - q]
                        P.op("scalar", lambda h: h.activation(out=xbq[:, 3:515], in_=zx[:], func=AF.Copy),
                             reads=[zx], writes=[xbq])
                        if tt == 0:
                            P.op("gpsimd", lambda h: h.memset(xbq[:, 0:3], 0.0), writes=[xbq])
                        else:
                            P.op("gpsimd", lambda h: h.tensor_copy(out=xbq[:, 0:3], in_=xbp[:, 512:515]),
                                 reads=[xbp], writes=[xbq])
                        xcq = xc[q]
                        P.op("vector", lambda h: h.tensor_scalar(out=xcq[:], in0=xbq[:, 3:515], scalar1=lv[:, n, 3:4],
                                                                 scalar2=lv[:, n, 4:5], op0=ALU.mult, op1=ALU.add),
                             reads=[xbq, lv], writes=[xcq])
                        for k in range(3):
                            P.op("vector", lambda h, k=k: h.scalar_tensor_tensor(
                                out=xcq[:], in0=xbq[:, k:k + 512], scalar=lv[:, n, k:k + 1], in1=xcq[:],
                                op0=ALU.mult, op1=ALU.add), reads=[xbq, lv, xcq], writes=[xcq])
                        xcbq = xcb[q]
                        P.op("scalar", lambda h: h.activation(out=xcbq[:], in_=xcq[:], func=AF.Copy), reads=[xcq], writes=[xcbq])
                        P.op("tensor", lambda h: h.matmul(pgi[:], lhsT=gwn[0][:], rhs=xcbq[:], start=True, stop=True),
                             reads=[gwn[0], xcbq], writes=[pgi])
                        P.op("tensor", lambda h: h.matmul(pgr[:], lhsT=gwn[1][:], rhs=xcbq[:], start=True, stop=True),
                             reads=[gwn[1], xcbq], writes=[pgr])
                        giq, grq, avq, a2q, uq, sgq = gi[q], gr[q], av[q], a2[q], uu[q], sg[q]
                        P.op("scalar", lambda h: h.activation(out=giq[:], in_=pgi[:], func=AF.Sigmoid, bias=lv[:, n, 5:6]),
                             reads=[pgi, lv], writes=[giq])
                        P.op("scalar", lambda h: h.activation(out=grq[:], in_=pgr[:], func=AF.Sigmoid, bias=lv[:, n, 6:7]),
                             reads=[pgr, lv], writes=[grq])
                        P.op("scalar", lambda h: h.activation(out=avq[:], in_=grq[:], func=AF.Exp, scale=cv[:, n, 0:1]),
                             reads=[grq, cv], writes=[avq])
                        P.op("scalar", lambda h: h.activation(out=a2q[:], in_=grq[:], func=AF.Exp, scale=cv[:, n, 1:2]),
                             reads=[grq, cv], writes=[a2q])
                        P.op("gpsimd", lambda h: h.tensor_scalar(out=a2q[:], in0=a2q[:], scalar1=-1.0, scalar2=1.0,
                                                                 op0=ALU.mult, op1=ALU.add), reads=[a2q], writes=[a2q])
                        P.op("scalar", lambda h: h.activation(out=a2q[:], in_=a2q[:], func=AF.Sqrt), reads=[a2q], writes=[a2q])
                        P.op("gpsimd", lambda h: h.tensor_tensor(out=uq[:], in0=a2q[:], in1=giq[:], op=ALU.mult),
                             reads=[a2q, giq], writes=[uq])
                        P.op("vector", lambda h: h.tensor_tensor(out=uq[:], in0=uq[:], in1=xcq[:], op=ALU.mult),
                             reads=[uq, xcq], writes=[uq])
                        hq, hp = hs[q], hs[1 - q]
                        if tt == 0:
                            P.op("vector", lambda h: h.tensor_tensor_scan(out=hq[:], data0=avq[:], data1=uq[:], initial=0.0,
                                                                          op0=ALU.mult, op1=ALU.add),
                                 reads=[avq, uq], writes=[hq])
                        else:
                            P.op("vector", lambda h: h.tensor_tensor_scan(out=hq[:], data0=avq[:], data1=uq[:],
                                                                          initial=hp[:, 511:512], op0=ALU.mult, op1=ALU.add),
                                 reads=[avq, uq, hp], writes=[hq])
                        P.op("scalar", lambda h: h.activation(out=sgq[:], in_=zg[:], func=AF.Silu), reads=[zg], writes=[sgq])
                        P.op("gpsimd", lambda h: h.tensor_tensor(out=yan[:], in0=hq[:], in1=sgq[:],
                                                                 op=ALU.mult), reads=[hq, sgq], writes=[yan])
                        P.dma("sync", lambda h: h.dma_start(out=yT0_d[n, :, tt * 512:(tt + 1) * 512], in_=yan[:]), yan,
                              reads=[yan], writes=[yT0_d])

        if stop >= 2:
            with P.scope():
                kd = [P.sb(f"kd{i}", [128, TR], BF16) for i in range(2)]
                vx = P.sb("vx", [128, NB, 2, 65], BF16)
                P.op("gpsimd", lambda h: h.memset(vx[:], 1.0), writes=[vx])
                sk = P.sb("sk", [128, 16], F32)
                P.dma("sync", lambda h: h.dma_start(out=sk[:], in_=sinks[:].partition_broadcast(128)), sk, writes=[sk])
                P.op("scalar", lambda h: h.activation(out=sk[:], in_=sk[:], func=AF.Exp), reads=[sk], writes=[sk])
                for i in range(2):
                    wk = load_chunk(w0v, 24 + i)
                    for tt in range(NT):
                        ps = nextpz()
                        proj_fm(wk, tt, ps)
                        P.op("scalar", lambda h: h.activation(out=kd[i][:, tt * 512:(tt + 1) * 512], in_=ps[:], func=AF.Copy),
                             reads=[ps], writes=[kd[i]])
                wv = load_chunk(w0v, 34)
                for n in range(NB):
                    ps = nextpz()
                    for k in range(16):
                        P.op("tensor", lambda h, k=k: h.matmul(ps[:, 0:128], lhsT=hT[:, k, n * 128:(n + 1) * 128], rhs=wv[:, k, :],
                                                               start=(k == 0), stop=(k == 15)),
                             reads=[hT, wv], writes=[ps], signal=(k == 15))
                    P.op("vector", lambda h: h.tensor_copy(out=vx[:, n, :, 0:64], in_=ps[:, 0:128].rearrange("p (a b) -> p a b", a=2)),
                         reads=[ps], writes=[vx])
                qT = [P.sb(f"qT{i}", [128, TR], BF16) for i in range(1)]
                sgT = [P.sb(f"sgT{i}", [128, 512], BF16) for i in range(2)]
                yb = [P.sb(f"yb{i}", [128, 512], BF16) for i in range(2)]
                ptr_t = P.ps("ptr", [128, 2, 128], BF16)
                ptr = [Buf(ptr_t[:, i, :], f"ptr{i}") for i in range(2)]
                PT = [[P.sb(f"PT{i}_{q}", [128, 256], BF16) for q in range(2)] for i in range(3)]
                den = [P.sb(f"den{i}", [128, 2], F32) for i in range(2)]
                ybt = [P.sb(f"ybt{i}", [128, 2, 64], BF16) for i in range(2)]
                pso_t = P.ps("pso", [128, 2, 2, 65], F32)
                pso = [Buf(pso_t[:, i], f"pso{i}") for i in range(2)]
                for c in range(8):
                    wq = load_chunk(w0v, 16 + c)
                    wg = load_chunk(w0v, 26 + c)
                    qTc = qT[0]
                    for tt in range(NT):
                        ps = nextpz()
                        proj_fm(wq, tt, ps)
                        P.op("scalar", lambda h: h.activation(out=qTc[:, tt * 512:(tt + 1) * 512], in_=ps[:], func=AF.Copy),
                             reads=[ps], writes=[qTc])
                    K = kd[c // 4]
                    kvh = c // 4
                    for m in range(NB):
                        if m % 4 == 0:
                            tt = m // 4
                            sgTc, ybc = sgT[tt % 2], yb[tt % 2]
                            ps2 = nextpz()
                            proj_fm(wg, tt, ps2)
                            P.op("scalar", lambda h: h.activation(out=sgTc[:], in_=ps2[:], func=AF.Silu),
                                 reads=[ps2], writes=[sgTc])
                        ncol = 256 if m < NB - 1 else 128
                        PTm = PT[m % 3]
                        PTp = PT[(m - 1) % 3]
                        for hh in range(2):
                            ps = nextpz()
                            P.op("tensor", lambda h: h.matmul(ps[:, 0:ncol], lhsT=K[64 * hh:64 * hh + 64, m * 128:(m + 1) * 128],
                                                              rhs=qTc[64 * hh:64 * hh + 64, m * 128:m * 128 + ncol],
                                                              start=True, stop=True), reads=[K, qTc], writes=[ps])
                            P.op("scalar", lambda h: h.activation(out=PTm[hh][:, 0:ncol], in_=ps[:, 0:ncol], func=AF.Exp, scale=0.125),
                                 reads=[ps], writes=[PTm[hh]])
                            P.op("gpsimd", lambda h: h.tensor_tensor(out=PTm[hh][:, 0:ncol], in0=PTm[hh][:, 0:ncol],
                                                                     in1=m2[:, 0:ncol], op=ALU.mult),
                                 reads=[PTm[hh], m2], writes=[PTm[hh]])
                        po = pso[m % 2]
                        for hh in range(2):
                            if m > 0:
                                P.op("tensor", lambda h: h.matmul(po[:, hh, :], lhsT=PTp[hh][:, 128:256], rhs=vx[:, m - 1, kvh, :],
                                                                  start=True, stop=False), reads=[PTp[hh], vx], writes=[po], signal=False)
                            P.op("tensor", lambda h: h.matmul(po[:, hh, :], lhsT=PTm[hh][:, 0:128], rhs=vx[:, m, kvh, :],
                                                              start=(m == 0), stop=True), reads=[PTm[hh], vx], writes=[po])
                        dn = den[m % 2]
                        ybm = ybt[m % 2]
                        P.op("vector", lambda h: h.tensor_tensor(out=dn[:], in0=po[:, :, 64], in1=sk[:, 2 * c:2 * c + 2], op=ALU.add),
                             reads=[po, sk], writes=[dn])
                        P.op("vector", lambda h: h.reciprocal(out=dn[:], in_=dn[:]), reads=[dn], writes=[dn])
                        P.op("vector", lambda h: h.tensor_tensor(out=ybm[:], in0=po[:, :, 0:64],
                                                                 in1=dn[:].unsqueeze(2).to_broadcast([128, 2, 64]), op=ALU.mult),
                             reads=[po, dn], writes=[ybm])
                        pt = ptr[m % 2]
                        mm = m % 4
                        P.op("tensor", lambda h: h.transpose(out=pt[:], in_=ybm[:].rearrange("p a b -> p (a b)"), identity=ident[:]),
                             reads=[ybm, ident], writes=[pt])
                        P.op("vector", lambda h: h.tensor_tensor(out=ybc[:, mm * 128:(mm + 1) * 128], in0=pt[:],
                                                                 in1=sgTc[:, mm * 128:(mm + 1) * 128], op=ALU.mult),
                             reads=[pt, sgTc], writes=[ybc])
                        if mm == 3:
                            P.dma("sync", lambda h: h.dma_start(out=yT0_d[8 + c, :, (m - 3) * 128:(m + 1) * 128], in_=ybc[:]), ybc,
                                  reads=[ybc], writes=[yT0_d])

    if stop <= 2:
        P.full_barrier()
        return P.finish(), I

    def outproj_phase(wo_d, yT_tile_fn, ntile, resid_fn, emit_fn, need_xn=True):
        with P.scope():
            wo = P.sb("wo", [128, 16, 2048], BF16)
            for k in range(16):
                P.dma("gpsimd", lambda h, k=k: h.dma_start(out=wo[:, k, :], in_=wo_d[:, k, :]), wo, writes=[wo])
            pb = [P.ps(f"pb{i}", [128, 512], F32) for i in range(4)]
            pst = [P.ps(f"pstB{i}", [128, 8, 128], BF16) for i in range(2)]
            xr = [P.sb(f"xr{i}", [128, D], F32) for i in range(2)]
            x1 = [P.sb(f"x1{i}", [128, D], F32) for i in range(2)]
            xn = [P.sb(f"xnB{i}", [128, D], BF16) for i in range(2)] if need_xn else [None, None]
            junk = P.sb("junkB", [128, D], BF16)
            for t4 in range(ntile):
                yt = yT_tile_fn(t4)
                for bi in range(4):
                    n = t4 * 4 + bi
                    xrn = xr[n % 2]
                    resid_fn(n, xrn)
                    for k in range(16):
                        for g in range(4):
                            P.op("tensor", lambda h, k=k, g=g: h.matmul(pb[g][:], lhsT=yt[:, k, bi * 128:(bi + 1) * 128],
                                                                        rhs=wo[:, k, g * 512:(g + 1) * 512],
                                                                        start=(k == 0), stop=(k == 15)),
                                 reads=[yt, wo], writes=[pb[g]], signal=(k == 15))
                    x1n = x1[n % 2]
                    for g in range(4):
                        P.op("vector", lambda h, g=g: h.tensor_tensor(out=x1n[:, g * 512:(g + 1) * 512], in0=pb[g][:],
                                                                      in1=xrn[:, g * 512:(g + 1) * 512], op=ALU.add),
                             reads=[pb[g], xrn], writes=[x1n])
                    emit_fn(n, x1n, xn[n % 2], junk, pst)

    with P.scope():
        ytile = [P.sb(f"ytile{i}", [128, 16, 512], BF16) for i in range(2)]
        g1t = P.sb("g1t", [128, FC], F32)
        P.dma("sync", lambda h: h.dma_start(out=g1t[:], in_=g1[:]), g1t, writes=[g1t])
        hst = [P.sb(f"hst{i}", [128, 16, 128], BF16) for i in range(2)]

        def ytf(t4):
            yt = ytile[t4 % 2]
            P.dma("sync", lambda h: h.dma_start(out=yt[:], in_=yT0_d[:, :, t4 * 512:(t4 + 1) * 512].rearrange("c p t -> p c t")),
                  yt, reads=[yT0_d], writes=[yt])
            return yt

        def resid(n, xrn):
            P.dma("sync", lambda h: h.dma_start(out=xrn[:], in_=x[n * 128:(n + 1) * 128, :]), xrn, writes=[xrn])

        def emit(n, x1n, xnn, junk, pst):
            P.dma("sync", lambda h: h.dma_start(out=x1_d[n * 128:(n + 1) * 128, :], in_=x1n[:]), x1n, reads=[x1n], writes=[x1_d])
            hs_ = hst[n % 2]

            def dst(k0, nk, pt):
                P.op("vector", lambda h: h.tensor_tensor(
                    out=hs_[:, k0:k0 + nk, :], in0=pt[:, 0:nk, :],
                    in1=g1t[:, k0:k0 + nk].unsqueeze(2).to_broadcast([128, nk, 128]), op=ALU.mult),
                    reads=[pt, g1t], writes=[hs_])
            rmsnorm_to_T(P, x1n, D, None, xnn, ident, pst, dst, stats[n % 2], junk)
            P.dma("sync", lambda h: h.dma_start(out=h1T_d[n * 128:(n + 1) * 128, :, :], in_=hs_[:]),
                  hs_, reads=[hs_], writes=[h1T_d])
        outproj_phase(wo0, ytf, NT, resid, emit)

    if stop <= 3:
        P.full_barrier()
        return P.finish(), I

    ckvnT_d = P.dram("ckvnT_d", [4, 128, TR], BF16, kind=dkind)
    krT_d = P.dram("krT_d", [64, TR], BF16, kind=dkind)
    dkT_d = P.dram("dkT_d", [8, 128, TR], BF16, kind=dkind)
    dv_d = P.dram("dv_d", [8, 128, NB, 128], BF16, kind=dkind)
    KT_d = P.dram("KT_d", [8, 128, TR], BF16, kind=dkind)
    V_d = P.dram("V_d", [8, 128, NB, 128], BF16, kind=dkind)
    qT_d = P.dram("qT_d", [8, 128, TRo], BF16, kind=dkind)
    qrT_d = P.dram("qrT_d", [8, 64, TRo], BF16, kind=dkind)
    mg_d = P.dram("mg_d", [8, 128, TRo], BF16, kind=dkind)
    dq_d = P.dram("dq_d", [8, 128, TRo], BF16, kind=dkind)
    dg_d = P.dram("dg_d", [8, 128, TRo], BF16, kind=dkind)

    ones_f = P.sb("ones_f", [128, 128], F32)
    P.op("vector", lambda h: h.memset(ones_f[:], 1.0), writes=[ones_f])
    rc = P.sb("rc", [64, 2], F32)
    P.dma("sync", lambda h: h.dma_start(out=rc[:], in_=ropec[:]), rc, writes=[rc])
    TWO_PI = 6.283185307179586
    MAGIC = 12582912.0

    def rope_tables(pos_d, t0, pos_t, ang, kf, cosT, sinT):
        P.dma("sync", lambda h: h.dma_start(out=pos_t[:], in_=pos_d[0:1, t0:t0 + 512].partition_broadcast(64)), pos_t, writes=[pos_t])
        for which, dst in ((0, sinT), (1, cosT)):
            P.op("vector", lambda h: h.tensor_scalar(out=ang[:], in0=pos_t[:], scalar1=rc[:, 0:1],
                                                     scalar2=(1.5707963267948966 if which else 0.0),
                                                     op0=ALU.mult, op1=ALU.add), reads=[pos_t, rc], writes=[ang])
            P.op("vector", lambda h: h.tensor_scalar(out=kf[:], in0=ang[:], scalar1=1.0 / TWO_PI, scalar2=MAGIC,
                                                     op0=ALU.mult, op1=ALU.add), reads=[ang], writes=[kf])
            P.op("vector", lambda h: h.tensor_scalar_add(out=kf[:], in0=kf[:], scalar1=-MAGIC), reads=[kf], writes=[kf])
            P.op("vector", lambda h: h.scalar_tensor_tensor(out=ang[:], in0=kf[:], scalar=-TWO_PI, in1=ang[:],
                                                            op0=ALU.mult, op1=ALU.add), reads=[kf, ang], writes=[ang])
            P.op("vector", lambda h: h.tensor_scalar(out=ang[:], in0=ang[:], scalar1=3.14159, scalar2=-3.14159,
                                                     op0=ALU.min, op1=ALU.max), reads=[ang], writes=[ang])
            P.op("scalar", lambda h: h.activation(out=dst[:], in_=ang[:], func=AF.Sin), reads=[ang], writes=[dst])
        P.op("vector", lambda h: h.tensor_scalar_mul(out=sinT[:], in0=sinT[:], scalar1=rc[:, 1:2]), reads=[sinT, rc], writes=[sinT])

    def rope_apply(pa, pb_, cosT, sinT, t1, t2, dst):
        P.op("vector", lambda h: h.tensor_tensor(out=t1[:], in0=pa[0:64, :], in1=cosT[:], op=ALU.mult), reads=[pa, cosT], writes=[t1])
        P.op("vector", lambda h: h.tensor_tensor(out=t2[:], in0=pb_[0:64, :], in1=sinT[:], op=ALU.mult), reads=[pb_, sinT], writes=[t2])
        P.op("gpsimd", lambda h: h.tensor_tensor(out=dst[0:64, :], in0=t1[:], in1=t2[:], op=ALU.add), reads=[t1, t2], writes=[dst])

    def make_loader(nslots):
        wch = [P.sb(f"wc{P.n_inst}_{i}", [128, 16, 128], BF16) for i in range(nslots)]
        ctr = [0]

        def load_chunk(src, c):
            t = wch[ctr[0] % len(wch)]
            ctr[0] += 1
            P.dma("gpsimd", lambda h: [h.dma_start(out=t[:, 0:8, :], in_=src[c, :, 0:8, :]),
                                       h.dma_start(out=t[:, 8:16, :], in_=src[c, :, 8:16, :])], t, writes=[t], n=2)
            return t
        return load_chunk

    def make_pz(n):
        pz = [P.ps(f"pq{P.n_inst}_{i}", [128, 512], F32) for i in range(n)]
        ctr = [0]

        def nxt():
            t = pz[ctr[0] % len(pz)]
            ctr[0] += 1
            return t
        return nxt

    def proj(wt, src, c0, ps, m0=0, m1=128, nk=16):
        for k in range(nk):
            P.op("tensor", lambda h, k=k: h.matmul(ps[0:m1 - m0, :], lhsT=wt[:, k, m0:m1], rhs=src[:, k, c0:c0 + 512],
                                                   start=(k == 0), stop=(k == nk - 1)),
                 reads=[wt, src], writes=[ps], signal=(k == nk - 1))

    norm_pss = [None]

    def norm_fm(src, nch, nxt, wts, gcol, ncols_total, cf, sq, rst, dst_fn, tcol, post_fn=None):
        pss = norm_pss[0]
        for c in range(nch):
            ps = nxt()
            proj(wts[c], src, tcol, ps)
            P.op("scalar", lambda h: h.activation(out=cf[:, c, :], in_=ps[:], func=AF.Copy), reads=[ps], writes=[cf])
            sqc = sq[c % 2]
            P.op("vector", lambda h: h.tensor_tensor(out=sqc[:], in0=cf[:, c, :], in1=cf[:, c, :], op=ALU.mult), reads=[cf], writes=[sqc])
            P.op("tensor", lambda h: h.matmul(pss[:], lhsT=ones_f[:], rhs=sqc[:], start=(c == 0), stop=(c == nch - 1)),
                 reads=[ones_f, sqc], writes=[pss], signal=(c == nch - 1))
        P.op("vector", lambda h: h.tensor_scalar(out=rst[:], in0=pss[:], scalar1=1.0 / ncols_total, scalar2=EPS,
                                                 op0=ALU.mult, op1=ALU.add), reads=[pss], writes=[rst])
        P.op("scalar", lambda h: h.activation(out=rst[:], in_=rst[:], func=AF.Sqrt), reads=[rst], writes=[rst])
        P.op("vector", lambda h: h.reciprocal(out=rst[:], in_=rst[:]), reads=[rst], writes=[rst])
        for c in range(nch):
            o_ap, ob = dst_fn(c)
            P.op("vector", lambda h: h.scalar_tensor_tensor(out=o_ap, in0=cf[:, c, :], scalar=gcol[:, c:c + 1], in1=rst[:],
                                                            op0=ALU.mult, op1=ALU.mult), reads=[cf, gcol, rst], writes=[ob])
            if post_fn is not None:
                post_fn(c, ob)

    with P.scope():
        h1T = P.sb("h1T", [128, 16, TR], BF16)
        for n in range(NB):
            P.dma("sync", lambda h: h.dma_start(out=h1T[:, :, n * 128:(n + 1) * 128], in_=h1T_d[n * 128:(n + 1) * 128, :, :]),
                  h1T, reads=[h1T_d], writes=[h1T])
        nxt = make_pz(5)
        norm_pss[0] = P.ps("npss1", [128, 512], F32)
        stg = [P.sb(f"stg{i}", [128, 512], BF16) for i in range(3)]
        stg_i = [0]

        def stage():
            t = stg[stg_i[0] % 3]
            stg_i[0] += 1
            return t
        with P.scope():
            ld = make_loader(4)
            wts = [ld(w1k, 9 + c) for c in range(4)]
            kvg = P.sb("kvg", [128, 4], F32)
            P.dma("sync", lambda h: h.dma_start(out=kvg[:], in_=kvn[:]), kvg, writes=[kvg])
            cf = P.sb("cf", [128, 4, 512], F32)
            sq = [P.sb(f"sq{i}", [128, 512], F32) for i in range(2)]
            rst = P.sb("rst", [128, 512], F32)
            for tt in range(NT):
                def dst(c):
                    t = stage()
                    return t[:], t

                def post(c, t):
                    P.dma("sync", lambda h: h.dma_start(out=ckvnT_d[c, :, tt * 512:(tt + 1) * 512], in_=t[:]), t, reads=[t], writes=[ckvnT_d])
                norm_fm(h1T, 4, nxt, wts, kvg, 512.0, cf, sq, rst, dst, tt * 512, post)
        with P.scope():
            ld = make_loader(3)
            wkr = ld(w1k, 0)
            pos_t = P.sb("pos_t", [64, 512], F32); ang = P.sb("ang", [64, 512], F32); kf = P.sb("kf", [64, 512], F32)
            cosT = P.sb("cosT", [64, 512], F32); sinT = P.sb("sinT", [64, 512], F32)
            t1 = P.sb("t1", [64, 512], F32); t2 = P.sb("t2", [64, 512], F32)
            for tt in range(NT):
                rope_tables(pos_all, tt * 512, pos_t, ang, kf, cosT, sinT)
                pa, pb_ = nxt(), nxt()
                proj(wkr, h1T, tt * 512, pa, 0, 64)
                proj(wkr, h1T, tt * 512, pb_, 64, 128)
                t = stage()
                rope_apply(pa, pb_, cosT, sinT, t1, t2, t)
                P.dma("sync", lambda h: h.dma_start(out=krT_d[:, tt * 512:(tt + 1) * 512], in_=t[0:64, :]), t, reads=[t], writes=[krT_d])
            for hh in range(8):
                wt = ld(w1k, 1 + hh)
                for tt in range(NT):
                    ps = nxt()
                    proj(wt, h1T, tt * 512, ps)
                    t = stage()
                    P.op("scalar", lambda h: h.activation(out=t[:], in_=ps[:], func=AF.Copy), reads=[ps], writes=[t])
                    P.dma("sync", lambda h: h.dma_start(out=dkT_d[hh, :, tt * 512:(tt + 1) * 512], in_=t[:]), t, reads=[t], writes=[dkT_d])
        with P.scope():
            wdvt = P.sb("wdvt", [128, 16, 512], BF16)
            for g in range(2):
                for k4 in range(4):
                    P.dma("gpsimd", lambda h: h.dma_start(out=wdvt[:, k4 * 4:(k4 + 1) * 4, :], in_=wdv[g, :, k4 * 4:(k4 + 1) * 4, :]),
                          wdvt, writes=[wdvt])
                for n in range(NB):
                    ps = nxt()
                    for k in range(16):
                        P.op("tensor", lambda h, k=k: h.matmul(ps[:], lhsT=h1T[:, k, n * 128:(n + 1) * 128], rhs=wdvt[:, k, :],
                                                               start=(k == 0), stop=(k == 15)), reads=[h1T, wdvt], writes=[ps], signal=(k == 15))
                    t = stage()
                    P.op("scalar", lambda h: h.activation(out=t[:], in_=ps[:], func=AF.Copy), reads=[ps], writes=[t])
                    P.dma("sync", lambda h: h.dma_start(out=dv_d[g * 4:(g + 1) * 4, :, n, :].rearrange("h p d -> p h d"),
                                                        in_=t[:].rearrange("p (h d) -> p h d", h=4)), t, reads=[t], writes=[dv_d])

    with P.scope():
        ckT = P.sb("ckT", [128, 4, TR], BF16)
        P.dma("sync", lambda h: h.dma_start(out=ckT[:], in_=ckvnT_d[:].rearrange("c p t -> p c t")), ckT, reads=[ckvnT_d], writes=[ckT])
        wk_ = P.sb("wk_", [128, 4, 1024], BF16); wv_ = P.sb("wv_", [128, 4, 1024], BF16)
        for kc in range(4):
            P.dma("gpsimd", lambda h: h.dma_start(out=wk_[:, kc, :], in_=wukv_k[:, kc, :]), wk_, writes=[wk_])
            P.dma("gpsimd", lambda h: h.dma_start(out=wv_[:, kc, :], in_=wukv_v[:, kc, :]), wv_, writes=[wv_])
        nxt = make_pz(4)
        stg = [P.sb(f"stgc{i}", [128, 512], BF16) for i in range(3)]
        si = 0
        for hh in range(8):
            for tt in range(NT):
                ps = nxt()
                proj(wk_, ckT, tt * 512, ps, hh * 128, (hh + 1) * 128, nk=4)
                t = stg[si % 3]; si += 1
                P.op("scalar", lambda h: h.activation(out=t[:], in_=ps[:], func=AF.Copy), reads=[ps], writes=[t])
                P.dma("sync", lambda h: h.dma_start(out=KT_d[hh, :, tt * 512:(tt + 1) * 512], in_=t[:]), t, reads=[t], writes=[KT_d])
        for g in range(2):
            for n in range(NB):
                ps = nxt()
                for k in range(4):
                    P.op("tensor", lambda h, k=k: h.matmul(ps[:], lhsT=ckT[:, k, n * 128:(n + 1) * 128], rhs=wv_[:, k, g * 512:(g + 1) * 512],
                                                           start=(k == 0), stop=(k == 3)), reads=[ckT, wv_], writes=[ps], signal=(k == 3))
                t = stg[si % 3]; si += 1
                P.op("scalar", lambda h: h.activation(out=t[:], in_=ps[:], func=AF.Copy), reads=[ps], writes=[t])
                P.dma("sync", lambda h: h.dma_start(out=V_d[g * 4:(g + 1) * 4, :, n, :].rearrange("h p d -> p h d"),
                                                    in_=t[:].rearrange("p (h d) -> p h d", h=4)), t, reads=[t], writes=[V_d])

    jwt = P.sb("jwt", [128, 2], F32)
    P.dma("sync", lambda h: h.dma_start(out=jwt[:], in_=jw[:]), jwt, writes=[jwt])
    with P.scope():
        h1o = P.sb("h1o", [128, 16, TRo], BF16)
        with P.scope():
            ga = [P.sb(f"ga{i}", [128, 16, 128], BF16) for i in range(2)]
            gb = [P.sb(f"gb{i}", [128, 16, 128], BF16) for i in range(2)]
            for i in range(NOWN):
                a_, b_ = ga[i % 2], gb[i % 2]
                P.dma("sync", lambda h: h.dma_start(out=a_[:], in_=h1T_d[(2 * i) * 128:(2 * i + 1) * 128, :, :]), a_, reads=[h1T_d], writes=[a_])
                P.dma("sync", lambda h: h.dma_start(out=b_[:], in_=h1T_d[(2 * i + 1) * 128:(2 * i + 2) * 128, :, :]), b_, reads=[h1T_d], writes=[b_])
                P.op("vector", lambda h: h.tensor_scalar_mul(out=a_[:], in0=a_[:], scalar1=jwt[:, 0:1]), reads=[a_, jwt], writes=[a_])
                P.op("vector", lambda h: h.scalar_tensor_tensor(out=h1o[:, :, i * 128:(i + 1) * 128], in0=b_[:], scalar=jwt[:, 1:2], in1=a_[:],
                                                                op0=ALU.mult, op1=ALU.add), reads=[b_, jwt, a_], writes=[h1o])
        nxt = make_pz(5)
        norm_pss[0] = P.ps("npss2", [128, 512], F32)
        stg = [P.sb(f"stgq{i}", [128, 512], BF16) for i in range(3)]
        stg_i = [0]

        def stage():
            t = stg[stg_i[0] % 3]
            stg_i[0] += 1
            return t
        with P.scope():
            cqn = P.sb("cqn", [128, 6, TRo], BF16)
            with P.scope():
                ld = make_loader(6)
                wts = [ld(w1q, c) for c in range(6)]
                qg = P.sb("qg", [128, 6], F32)
                P.dma("sync", lambda h: h.dma_start(out=qg[:], in_=qn[:]), qg, writes=[qg])
                cf = P.sb("cfq", [128, 6, 512], F32)
                sq = [P.sb(f"sqq{i}", [128, 512], F32) for i in range(2)]
                rst = P.sb("rstq", [128, 512], F32)
                for tt in range(NTo):
                    norm_fm(h1o, 6, nxt, wts, qg, 768.0, cf, sq, rst, lambda c: (cqn[:, c, tt * 512:(tt + 1) * 512], cqn), tt * 512)
            with P.scope():
                wq_ = P.sb("wq_", [128, 6, 1536], BF16); wqs_ = P.sb("wqs_", [128, 6, 512], BF16)
                for kc in range(6):
                    P.dma("gpsimd", lambda h: h.dma_start(out=wq_[:, kc, :], in_=wuq[:, kc, :]), wq_, writes=[wq_])
                    P.dma("gpsimd", lambda h: h.dma_start(out=wqs_[:, kc, :], in_=wuqs[:, kc, :]), wqs_, writes=[wqs_])
                pos_t = P.sb("pos_tq", [64, 512], F32); ang = P.sb("angq", [64, 512], F32); kf = P.sb("kfq", [64, 512], F32)
                cosT = P.sb("cosTq", [64, 512], F32); sinT = P.sb("sinTq", [64, 512], F32)
                t1 = P.sb("t1q", [64, 512], F32); t2 = P.sb("t2q", [64, 512], F32)
                for tt in range(NTo):
                    rope_tables(pos_own, tt * 512, pos_t, ang, kf, cosT, sinT)
                    for hh in range(8):
                        ps = nxt()
                        proj(wq_, cqn, tt * 512, ps, hh * 192, hh * 192 + 128, nk=6)
                        t = stage()
                        P.op("scalar", lambda h: h.activation(out=t[:], in_=ps[:], func=AF.Copy), reads=[ps], writes=[t])
                        P.dma("sync", lambda h: h.dma_start(out=qT_d[hh, :, tt * 512:(tt + 1) * 512], in_=t[:]), t, reads=[t], writes=[qT_d])
                        pa, pb_ = nxt(), nxt()
                        proj(wq_, cqn, tt * 512, pa, hh * 192 + 128, hh * 192 + 192, nk=6)
                        proj(wqs_, cqn, tt * 512, pb_, hh * 64, hh * 64 + 64, nk=6)
                        t = stage()
                        rope_apply(pa, pb_, cosT, sinT, t1, t2, t)
                        P.dma("sync", lambda h: h.dma_start(out=qrT_d[hh, :, tt * 512:(tt + 1) * 512], in_=t[0:64, :]), t, reads=[t], writes=[qrT_d])
        with P.scope():
            ld = make_loader(3)
            for (base, dstd, fn) in ((6, mg_d, AF.Silu), (14, dq_d, AF.Copy), (22, dg_d, AF.Silu)):
                for hh in range(8):
                    wt = ld(w1q, base + hh)
                    for tt in range(NTo):
                        ps = nxt()
                        proj(wt, h1o, tt * 512, ps)
                        t = stage()
                        P.op("scalar", lambda h: h.activation(out=t[:], in_=ps[:], func=fn), reads=[ps], writes=[t])
                        P.dma("sync", lambda h: h.dma_start(out=dstd[hh, :, tt * 512:(tt + 1) * 512], in_=t[:]), t, reads=[t], writes=[dstd])

    with P.scope():
        yT1 = P.sb("yT1", [128, 16, TRo], BF16)
        mab_f = P.sb("mab_f", [128, 256], F32)
        mab = P.sb("mab", [128, 256], BF16)
        P.dma("sync", lambda h: h.dma_start(out=mab_f[:], in_=maskab[:]), mab_f, writes=[mab_f])
        P.op("vector", lambda h: h.tensor_copy(out=mab[:], in_=mab_f[:]), reads=[mab_f], writes=[mab])

        def attn_tile(qt, qk_fn, Vt, scale, ps2, po, pl, PTs):
            i0 = 4 * qt
            mlast = 2 * i0 + 7
            for m in range(mlast + 1):
                r0 = max(0, m // 2 - i0)
                c0 = r0 * 128
                ps = ps2[m % 2]
                PT = PTs[m % 3]
                qk_fn(ps, m, qt * 512 + c0, c0)
                P.op("scalar", lambda h: h.activation(out=PT[:, c0:512], in_=ps[:, c0:512], func=AF.Exp, scale=scale),
                     reads=[ps], writes=[PT])
                if m >= 2 * i0:
                    mo = (m % 2) * 128
                    P.op("gpsimd", lambda h: h.tensor_tensor(out=PT[:, c0:c0 + 128], in0=PT[:, c0:c0 + 128],
                                                             in1=mab[:, mo:mo + 128], op=ALU.mult), reads=[PT, mab], writes=[PT])
                P.op("tensor", lambda h: h.matmul(po[:, c0:512], lhsT=Vt[:, m, :], rhs=PT[:, c0:512], start=(m == 0), stop=(m == mlast)),
                     reads=[Vt, PT], writes=[po], signal=(m == mlast))
                P.op("tensor", lambda h: h.matmul(pl[:, c0:512], lhsT=ones_bf[:], rhs=PT[:, c0:512], start=(m == 0), stop=(m == mlast)),
                     reads=[ones_bf, PT], writes=[pl], signal=(m == mlast))

        with P.scope():
            krT = P.sb("krT", [64, TR], BF16)
            P.dma("sync", lambda h: h.dma_start(out=krT[:], in_=krT_d[:]), krT, reads=[krT_d], writes=[krT])
            KT = [P.sb(f"KT{i}", [128, TR], BF16) for i in range(2)]
            Vt = [P.sb(f"Vt{i}", [128, NB, 128], BF16) for i in range(2)]
            qT = [P.sb(f"qTh{i}", [128, TRo], BF16) for i in range(2)]
            qr = [P.sb(f"qrh{i}", [64, TRo], BF16) for i in range(2)]
            mg = [P.sb(f"mgh{i}", [128, TRo], BF16) for i in range(2)]
            PTs = [P.sb(f"PTd{i}", [128, 512], BF16) for i in range(3)]
            rl = [P.sb(f"rl{i}", [128, 512], F32) for i in range(2)]
            ps2 = [P.ps(f"psS{i}", [128, 512], F32) for i in range(2)]
            poo = [P.ps(f"poo{i}", [128, 512], F32) for i in range(2)]
            pll = [P.ps(f"pll{i}", [128, 512], F32) for i in range(2)]
            it = 0
            for hh in range(8):
                q = hh % 2
                P.dma("sync", lambda h: h.dma_start(out=KT[q][:], in_=KT_d[hh]), KT[q], reads=[KT_d], writes=[KT[q]])
                P.dma("sync", lambda h: h.dma_start(out=Vt[q][:], in_=V_d[hh]), Vt[q], reads=[V_d], writes=[Vt[q]])
                P.dma("sync", lambda h: h.dma_start(out=qT[q][:], in_=qT_d[hh]), qT[q], reads=[qT_d], writes=[qT[q]])
                P.dma("sync", lambda h: h.dma_start(out=qr[q][:], in_=qrT_d[hh]), qr[q], reads=[qrT_d], writes=[qr[q]])
                P.dma("sync", lambda h: h.dma_start(out=mg[q][:], in_=mg_d[hh]), mg[q], reads=[mg_d], writes=[mg[q]])
                for qt in range(NTo):
                    po, pl = poo[it % 2], pll[it % 2]
                    rlt = rl[it % 2]
                    it += 1

                    def qk(ps, m, qc, c0):
                        P.op("tensor", lambda h: h.matmul(ps[:, c0:512], lhsT=KT[q][:, m * 128:(m + 1) * 128], rhs=qT[q][:, qc:qt * 512 + 512],
                                                          start=True, stop=False), reads=[KT[q], qT[q]], writes=[ps], signal=False)
                        P.op("tensor", lambda h: h.matmul(ps[:, c0:512], lhsT=krT[:, m * 128:(m + 1) * 128], rhs=qr[q][:, qc:qt * 512 + 512],
                                                          start=False, stop=True), reads=[krT, qr[q]], writes=[ps])
                    attn_tile(qt, qk, Vt[q], 192.0 ** -0.5, ps2, po, pl, PTs)
                    P.op("vector", lambda h: h.reciprocal(out=rlt[:], in_=pl[:]), reads=[pl], writes=[rlt])
                    P.op("vector", lambda h: h.tensor_tensor(out=rlt[:], in0=po[:], in1=rlt[:], op=ALU.mult), reads=[po, rlt], writes=[rlt])
                    P.op("gpsimd", lambda h: h.tensor_tensor(out=yT1[:, hh, qt * 512:(qt + 1) * 512], in0=rlt[:],
                                                             in1=mg[q][:, qt * 512:(qt + 1) * 512], op=ALU.mult),
                         reads=[rlt, mg[q]], writes=[yT1])

        with P.scope():
            LINIT = 0.8 - 0.6 * float(np.exp(-0.3 * 1))
            lm = P.sb("lm", [128, 256], F32)
            P.dma("sync", lambda h: h.dma_start(out=lm[:], in_=lams[:].partition_broadcast(128)), lm, writes=[lm])
            lp = P.sb("lp", [128, 2, 64], F32)
            ls = P.sb("ls", [128, 2], F32)
            nlam = P.sb("nlam", [128, 1], F32)
            for a in range(2):
                P.op("vector", lambda h: h.tensor_tensor(out=lp[:, a, :], in0=lm[:, a * 128:a * 128 + 64], in1=lm[:, a * 128 + 64:a * 128 + 128],
                                                         op=ALU.mult), reads=[lm], writes=[lp])
            P.op("vector", lambda h: h.tensor_reduce(out=ls[:], in_=lp[:], axis=AX.X, op=ALU.add), reads=[lp], writes=[ls])
            P.op("scalar", lambda h: h.activation(out=ls[:], in_=ls[:], func=AF.Exp), reads=[ls], writes=[ls])
            P.op("vector", lambda h: h.tensor_tensor(out=nlam[:], in0=ls[:, 1:2], in1=ls[:, 0:1], op=ALU.subtract), reads=[ls], writes=[nlam])
            P.op("vector", lambda h: h.tensor_scalar_add(out=nlam[:], in0=nlam[:], scalar1=-LINIT), reads=[nlam], writes=[nlam])
            sln = P.sb("sln", [128, 1], F32)
            P.dma("sync", lambda h: h.dma_start(out=sln[:], in_=subln[:]), sln, writes=[sln])
            P.op("vector", lambda h: h.tensor_scalar_mul(out=sln[:], in0=sln[:], scalar1=1.0 - LINIT), reads=[sln], writes=[sln])
            dk = [P.sb(f"dk{i}", [128, TR], BF16) for i in range(2)]
            dvt = [P.sb(f"dvt{i}", [128, NB, 128], BF16) for i in range(2)]
            dq = [P.sb(f"dq{i}", [128, TRo], BF16) for i in range(2)]
            dg = [P.sb(f"dg{i}", [128, TRo], BF16) for i in range(2)]
            PTs = [P.sb(f"PTe{i}", [128, 512], BF16) for i in range(3)]
            r0t = P.sb("r0t", [128, 512], F32); r1t = P.sb("r1t", [128, 512], F32); sqt = P.sb("sqt", [128, 512], F32)
            ps2 = [P.ps(f"peS{i}", [128, 512], F32) for i in range(2)]
            poo = [P.ps(f"peo{i}", [128, 512], F32) for i in range(2)]
            pll = [P.ps(f"pel{i}", [128, 512], F32) for i in range(2)]
            pss = P.ps("pess", [128, 512], F32)
            for hh in range(8):
                q = hh % 2
                P.dma("sync", lambda h: h.dma_start(out=dk[q][:], in_=dkT_d[hh]), dk[q], reads=[dkT_d], writes=[dk[q]])
                P.dma("sync", lambda h: h.dma_start(out=dvt[q][:], in_=dv_d[hh]), dvt[q], reads=[dv_d], writes=[dvt[q]])
                P.dma("sync", lambda h: h.dma_start(out=dq[q][:], in_=dq_d[hh]), dq[q], reads=[dq_d], writes=[dq[q]])
                P.dma("sync", lambda h: h.dma_start(out=dg[q][:], in_=dg_d[hh]), dg[q], reads=[dg_d], writes=[dg[q]])
                for qt in range(NTo):
                    for c in range(2):
                        def qk(ps, m, qc, c0):
                            P.op("tensor", lambda h: h.matmul(ps[:, c0:512], lhsT=dk[q][64 * c:64 * c + 64, m * 128:(m + 1) * 128],
                                                              rhs=dq[q][64 * c:64 * c + 64, qc:qt * 512 + 512], start=True, stop=True),
                                 reads=[dk[q], dq[q]], writes=[ps])
                        attn_tile(qt, qk, dvt[q], 0.125, ps2, poo[c], pll[c], PTs)
                    P.op("vector", lambda h: h.reciprocal(out=r0t[:], in_=pll[0][:]), reads=[pll[0]], writes=[r0t])
                    P.op("vector", lambda h: h.tensor_tensor(out=r0t[:], in0=poo[0][:], in1=r0t[:], op=ALU.mult), reads=[poo[0], r0t], writes=[r0t])
                    P.op("vector", lambda h: h.reciprocal(out=r1t[:], in_=pll[1][:]), reads=[pll[1]], writes=[r1t])
                    P.op("vector", lambda h: h.tensor_tensor(out=r1t[:], in0=poo[1][:], in1=r1t[:], op=ALU.mult), reads=[poo[1], r1t], writes=[r1t])
                    P.op("vector", lambda h: h.scalar_tensor_tensor(out=r0t[:], in0=r1t[:], scalar=nlam[:, 0:1], in1=r0t[:],
                                                                    op0=ALU.mult, op1=ALU.add), reads=[r1t, nlam, r0t], writes=[r0t])
                    P.op("scalar", lambda h: h.activation(out=sqt[:], in_=r0t[:], func=AF.Square), reads=[r0t], writes=[sqt])
                    P.op("tensor", lambda h: h.matmul(pss[:], lhsT=ones_f[:], rhs=sqt[:], start=True, stop=True), reads=[ones_f, sqt], writes=[pss])
                    P.op("vector", lambda h: h.tensor_scalar(out=r1t[:], in0=pss[:], scalar1=1.0 / 128, scalar2=EPS, op0=ALU.mult, op1=ALU.add),
                         reads=[pss], writes=[r1t])
                    P.op("scalar", lambda h: h.activation(out=r1t[:], in_=r1t[:], func=AF.Sqrt), reads=[r1t], writes=[r1t])
                    P.op("vector", lambda h: h.reciprocal(out=r1t[:], in_=r1t[:]), reads=[r1t], writes=[r1t])
                    P.op("vector", lambda h: h.tensor_tensor(out=r0t[:], in0=r0t[:], in1=r1t[:], op=ALU.mult), reads=[r0t, r1t], writes=[r0t])
                    P.op("vector", lambda h: h.scalar_tensor_tensor(out=yT1[:, 8 + hh, qt * 512:(qt + 1) * 512], in0=r0t[:], scalar=sln[:, 0:1],
                                                                    in1=dg[q][:, qt * 512:(qt + 1) * 512], op0=ALU.mult, op1=ALU.mult),
                         reads=[r0t, sln, dg[q]], writes=[yT1])

        with P.scope():
            gfb = P.sb("gfb", [128, D], F32)
            P.dma("sync", lambda h: h.dma_start(out=gfb[:], in_=gf[:].partition_broadcast(128)), gfb, writes=[gfb])

            def ytf(t4):
                return Buf(yT1.t[:, :, t4 * 512:(t4 + 1) * 512], "ytv")

            xra = [P.sb(f"xra{i}", [128, D], F32) for i in range(1)]

            def resid(n, xrn):
                a_ = xra[0]
                P.dma("sync", lambda h: h.dma_start(out=a_[:], in_=x1_d[(2 * n) * 128:(2 * n + 1) * 128, :]), a_, reads=[x1_d], writes=[a_])
                P.dma("sync", lambda h: h.dma_start(out=xrn[:], in_=x1_d[(2 * n + 1) * 128:(2 * n + 2) * 128, :]), xrn, reads=[x1_d], writes=[xrn])
                P.op("gpsimd", lambda h: h.tensor_scalar(out=a_[:], in0=a_[:], scalar1=jwt[:, 0:1], scalar2=None, op0=ALU.mult), reads=[a_, jwt], writes=[a_])
                P.op("vector", lambda h: h.scalar_tensor_tensor(out=xrn[:], in0=xrn[:], scalar=jwt[:, 1:2], in1=a_[:],
                                                                op0=ALU.mult, op1=ALU.add), reads=[xrn, jwt, a_], writes=[xrn])

            def emit(n, x2, xnn, junk, pst):
                s_ = stats[n % 2]
                P.op("scalar", lambda h: h.activation(out=junk[:], in_=x2[:], func=AF.Square, accum_out=s_[0][:]), reads=[x2], writes=[junk, s_[0]])
                P.op("vector", lambda h: h.tensor_scalar(out=s_[1][:], in0=s_[0][:], scalar1=1.0 / D, scalar2=EPS, op0=ALU.mult, op1=ALU.add),
                     reads=[s_[0]], writes=[s_[1]])
                P.op("scalar", lambda h: h.activation(out=s_[2][:], in_=s_[1][:], func=AF.Sqrt), reads=[s_[1]], writes=[s_[2]])
                P.op("vector", lambda h: h.reciprocal(out=s_[3][:], in_=s_[2][:]), reads=[s_[2]], writes=[s_[3]])
                o = x2
                P.op("vector", lambda h: h.scalar_tensor_tensor(out=o[:], in0=x2[:], scalar=s_[3][:], in1=gfb[:], op0=ALU.mult, op1=ALU.mult),
                     reads=[x2, s_[3], gfb], writes=[o])
                P.dma("sync", lambda h: h.dma_start(out=out[n * 128:(n + 1) * 128, :], in_=o[:]), o, reads=[o], writes=[out])
            outproj_phase(wo1, ytf, NTo, resid, emit, need_xn=False)

    P.full_barrier()
    return P.finish(), I


_CACHE = {}


def run_full(inputs, TR=4096):
    inputs = {k: np.asarray(v) for k, v in inputs.items()}
    if TR not in _CACHE:
        _CACHE[TR] = build(TR=TR)
    nc, I = _CACHE[TR]
    ins = []
    for c in range(8):
        d = prep_inputs(inputs, c, TR)
        ins.append({k: v for k, v in d.items() if k in I})
    res = run_bass_kernel_spmd(nc, ins, core_ids=list(range(8)))
    B = inputs["x"].shape[0]
    outp = np.zeros((B, TR, D), np.float32)
    NOWN = TR // 256
    for c in range(8):
        b, j = c // 2, c % 2
        o = np.asarray(res.results[c]["out"])
        for i in range(NOWN):
            g = 2 * i + j
            outp[b, g * 128:(g + 1) * 128] = o[i * 128:(i + 1) * 128]
    return outp


def kernel(**inputs):
    return run_full(inputs, 4096)
```

```python
import numpy as np
from concourse.bass_utils import run_bass_kernel_spmd
from contextlib import ExitStack
import concourse.bass as bass
import concourse.mybir as mybir

F32 = mybir.dt.float32
BF16 = mybir.dt.bfloat16
I32 = mybir.dt.int32
U32 = mybir.dt.uint32
AF = mybir.ActivationFunctionType
ALU = mybir.AluOpType
AX = mybir.AxisListType


class Buf:
    def __init__(self, t, name, multi=False):
        self.t = t
        self.name = name
        self.w = {}
        self.r = {}
        self.multi = multi
        self.dsem = None
        self.dcount = 0

    def __getitem__(self, k):
        return self.t[k]


class Eng:
    def __init__(self, name, h, sem):
        self.name = name
        self.h = h
        self.sem = sem
        self.count = 0
        self.seen = {}
        self.stream = []


class Prog:
    def __init__(self):
        self.nc = bass.Bass("TRN2", target_bir_lowering=False)
        self.es = ExitStack()
        nc = self.nc
        self.E = {}
        for name in ["tensor", "vector", "scalar", "gpsimd", "sync"]:
            sem = self.es.enter_context(nc.semaphore("s_" + name))
            self.E[name] = Eng(name, getattr(nc, name), sem)
        self.nbuf = 0
        self.n_inst = 0
        self.dsems_all = {}
        self.dsem_free = []
        self.stack = [self.es]
        self.scope_bufs = [[]]

    def sb(self, name, shape, dtype):
        self.nbuf += 1
        name = f"{name}_{self.nbuf}"
        t = self.stack[-1].enter_context(self.nc.sbuf_tensor(name, list(shape), dtype))
        b = Buf(t, name)
        self.scope_bufs[-1].append(b)
        return b

    def ps(self, name, shape, dtype):
        self.nbuf += 1
        name = f"{name}_{self.nbuf}"
        t = self.stack[-1].enter_context(self.nc.psum_tensor(name, list(shape), dtype))
        b = Buf(t, name)
        self.scope_bufs[-1].append(b)
        return b

    def dram(self, name, shape, dtype, kind="Internal"):
        t = self.nc.dram_tensor(name, list(shape), dtype, kind=kind).ap()
        return Buf(t, name, multi=True)

    def _wait(self, eng, evs):
        for key, (sem, val) in evs.items():
            if sem is eng.sem:
                if val > eng.count:
                    continue
                if eng.name in ("tensor", "sync"):
                    continue
            if eng.seen.get(key, 0) >= val:
                continue
            eng.seen[key] = val
            eng.h.wait_ge(sem, val)

    @staticmethod
    def _merge(d, key, sem, val):
        if key not in d or d[key][1] < val:
            d[key] = (sem, val)

    def _deps(self, eng, reads, writes):
        evs = {}
        for b in reads:
            for k, (s, v) in b.w.items():
                self._merge(evs, k, s, v)
        for b in writes:
            if b.multi:
                continue
            for k, (s, v) in b.w.items():
                self._merge(evs, k, s, v)
            for k, (s, v) in b.r.items():
                self._merge(evs, k, s, v)
        self._wait(eng, evs)

    def _record(self, reads, writes, key, sem, val):
        for b in reads:
            self._merge(b.r, key, sem, val)
        for b in writes:
            if b.multi:
                self._merge(b.w, key, sem, val)
            else:
                b.w = {key: (sem, val)}
                b.r = {}

    def op(self, engname, fn, reads=(), writes=(), signal=True):
        eng = self.E[engname]
        self._deps(eng, reads, writes)
        self.n_inst += 1
        if signal:
            eng.count += 1
            val = eng.count
            sem = eng.sem
            fn(eng.h).then_inc(sem, 1)
        else:
            val = eng.count + 1
            fn(eng.h)
        self._record(reads, writes, id(eng.sem), eng.sem, val)

    def dma(self, qname, fn, owner, reads=(), writes=(), n=1):
        eng = self.E[qname]
        fresh = qname == "gpsimd" and getattr(self, "fresh_swdge", False)
        self._deps(eng, reads, writes)
        self.n_inst += n
        if fresh:
            sems = []
            for _ in range(n):
                sm = self.es.enter_context(self.nc.semaphore("f%d" % len(self.dsems_all)))
                self.dsems_all[id(sm)] = (sm, [16])
                sems.append(sm)
            r = fn(eng.h)
            if not isinstance(r, (list, tuple)):
                r = [r]
            assert len(r) == n
            for ins, sm in zip(r, sems):
                ins.then_inc(sm, 16)
            for i, sm in enumerate(sems):
                if i == 0:
                    self._record(reads, writes, id(sm), sm, 16)
                else:
                    for b in reads:
                        self._merge(b.r, id(sm), sm, 16)
                    for b in writes:
                        self._merge(b.w, id(sm), sm, 16)
            return
        if owner.dsem is None:
            if self.dsem_free:
                owner.dsem = self.dsem_free.pop()
            else:
                owner.dsem = self.es.enter_context(self.nc.semaphore("d%d" % len(self.dsems_all)))
                self.dsems_all[id(owner.dsem)] = (owner.dsem, [0])
        cnt = self.dsems_all[id(owner.dsem)][1]
        cnt[0] += 16 * n
        sem = owner.dsem
        val = cnt[0]
        r = fn(eng.h)
        if not isinstance(r, (list, tuple)):
            r = [r]
        assert len(r) == n
        for ins in r:
            ins.then_inc(sem, 16)
        self._record(reads, writes, id(sem), sem, val)

    def barrier_all_to(self, engname, bufs):
        eng = self.E[engname]
        evs = {}
        for b in bufs:
            for k, (s, v) in b.w.items():
                self._merge(evs, k, s, v)
        self._wait(eng, evs)

    def full_barrier(self):
        evs = {}
        for e in self.E.values():
            if e.count > 0:
                evs[id(e.sem)] = (e.sem, e.count)
        for k, (sem, cnt) in self.dsems_all.items():
            if cnt[0] > 0:
                evs[k] = (sem, cnt[0])
        for e in self.E.values():
            ev2 = {k: v for k, v in evs.items() if v[0] is not e.sem}
            self._wait(e, ev2)

    def scope(self):
        return _Scope(self)

    def finish(self):
        self.es.close()
        return self.nc


class _Scope:
    def __init__(self, P):
        self.P = P

    def __enter__(self):
        self.P.stack.append(ExitStack())
        self.P.scope_bufs.append([])
        return self

    def __exit__(self, *a):
        P = self.P
        P.full_barrier()
        for b in P.scope_bufs.pop():
            if b.dsem is not None:
                P.dsem_free.append(b.dsem)
                b.dsem = None
        P.stack.pop().close()
        return False


D = 2048
FC = 16
EPS = 1e-6


def chunk_cols(w, cols):
    K = w.shape[0]
    sub = w[:, cols]
    return np.ascontiguousarray(sub.reshape(K // 128, 128, len(cols)).transpose(1, 0, 2))


def prep_inputs(inp, core, TR):
    b, j = core // 2, core % 2
    NB = TR // 128
    NOWN = NB // 2
    ar = np.arange
    d = {}
    d["x"] = np.ascontiguousarray(inp["x"][b, :TR])
    d["g0"] = np.ascontiguousarray(inp["norm_gains"][0].reshape(FC, 128).T)
    d["g1"] = np.ascontiguousarray(inp["norm_gains"][1].reshape(FC, 128).T)
    d["gf"] = np.ascontiguousarray(inp["final_norm_gain"].reshape(1, D))
    w = inp["l0_w_in"]
    ch = []
    for n in range(8):
        ch.append(ar(n * 128, (n + 1) * 128))
    for n in range(8):
        ch.append(1024 + ar(n * 128, (n + 1) * 128))
    for n in range(8):
        ch.append(2048 + ar(n * 128, (n + 1) * 128))
    ch.append(3072 + np.concatenate([ar(64), ar(64)]))
    ch.append(3072 + 64 + np.concatenate([ar(64), ar(64)]))
    for n in range(8):
        ch.append(3328 + ar(n * 128, (n + 1) * 128))
    ch.append(3200 + ar(128))
    d["w0v"] = np.stack([chunk_cols(w, c) for c in ch])
    lv = np.zeros((128, 8, 8), np.float32)
    for n in range(8):
        sl = slice(n * 128, (n + 1) * 128)
        lv[:, n, 0:4] = inp["l0_conv_w"][:, sl].T
        lv[:, n, 4] = inp["l0_conv_b"][sl]
        lv[:, n, 5] = inp["l0_gate_x_b"][n]
        lv[:, n, 6] = inp["l0_gate_a_b"][n]
        lv[:, n, 7] = inp["l0_lru_lambda"][sl]
    d["lruvec"] = lv
    d["gxw"] = np.ascontiguousarray(inp["l0_gate_x_w"])
    d["gaw"] = np.ascontiguousarray(inp["l0_gate_a_w"])
    d["sinks"] = np.ascontiguousarray(inp["l0_sinks"].reshape(1, 16))
    d["wo0"] = chunk_cols(inp["l0_w_out"], ar(2048))
    w1 = inp["l1_w_in"]
    o_cq, o_ckv, o_kr, o_mg, o_dq, o_dk, o_dv, o_dg = 0, 768, 1280, 1344, 2368, 3392, 4416, 5440
    chq = [o_cq + ar(n * 128, (n + 1) * 128) for n in range(6)]
    chq += [o_mg + ar(n * 128, (n + 1) * 128) for n in range(8)]
    chq += [o_dq + ar(n * 128, (n + 1) * 128) for n in range(8)]
    chq += [o_dg + ar(n * 128, (n + 1) * 128) for n in range(8)]
    d["w1q"] = np.stack([chunk_cols(w1, c) for c in chq])
    chk = [o_kr + np.concatenate([ar(64), ar(32, 64), ar(0, 32)])]
    chk += [o_dk + ar(n * 128, (n + 1) * 128) for n in range(8)]
    chk += [o_ckv + ar(n * 128, (n + 1) * 128) for n in range(4)]
    d["w1k"] = np.stack([chunk_cols(w1, c) for c in chk])
    d["wdv"] = np.stack([chunk_cols(w1, o_dv + ar(g * 512, (g + 1) * 512)) for g in range(2)])
    d["qn"] = np.ascontiguousarray(inp["l1_q_norm"].reshape(6, 128).T)
    d["kvn"] = np.ascontiguousarray(inp["l1_kv_norm"].reshape(4, 128).T)
    hd = np.arange(8)[:, None] * 256 + np.arange(128)[None, :]
    d["wukv_k"] = chunk_cols(inp["l1_w_ukv"], hd.reshape(-1))
    d["wukv_v"] = chunk_cols(inp["l1_w_ukv"], (hd + 128).reshape(-1))
    fr = (10000.0 ** (-np.arange(32, dtype=np.float32) / 32)).astype(np.float32)
    cst = np.zeros((64, 2), np.float32)
    cst[:, 0] = np.concatenate([fr, fr])
    cst[:, 1] = np.concatenate([-np.ones(32), np.ones(32)])
    d["ropec"] = cst
    d["wuq"] = chunk_cols(inp["l1_w_uq"], ar(1536))
    sw = np.concatenate([h * 192 + 128 + np.concatenate([ar(32, 64), ar(0, 32)]) for h in range(8)])
    d["wuqs"] = chunk_cols(inp["l1_w_uq"], sw)
    d["lams"] = np.ascontiguousarray(np.concatenate([inp["l1_lambda_q1"], inp["l1_lambda_k1"],
                                                    inp["l1_lambda_q2"], inp["l1_lambda_k2"]]).reshape(1, 256))
    d["subln"] = np.ascontiguousarray(inp["l1_subln"].reshape(128, 1))
    d["wo1"] = chunk_cols(inp["l1_w_out"], ar(2048))
    jwv = np.zeros((128, 2), np.float32)
    jwv[:, j] = 1.0
    d["jw"] = jwv
    kk, qq = np.meshgrid(ar(128), ar(128), indexing="ij")
    tri = (kk <= qq).astype(np.float32)
    d["maskab"] = np.ascontiguousarray(np.concatenate(
        [tri if j == 0 else np.ones_like(tri), np.zeros_like(tri) if j == 0 else tri], axis=1))
    d["pos_all"] = ar(TR, dtype=np.float32).reshape(1, TR)
    d["pos_own"] = np.concatenate([(2 * i + j) * 128 + ar(128) for i in range(NOWN)]).astype(np.float32).reshape(1, NOWN * 128)
    return d


def rmsnorm_to_T(P, xb, ncol, gain_bc, xn, ident, pst, dst_fn, stats, junk, gain_cols=None, evac_eng="vector"):
    s = stats
    P.op("scalar", lambda h: h.activation(out=junk[:, 0:ncol], in_=xb[:, 0:ncol], func=AF.Square, accum_out=s[0][:]),
         reads=[xb], writes=[junk, s[0]])
    P.op("vector", lambda h: h.tensor_scalar(out=s[1][:], in0=s[0][:], scalar1=1.0 / ncol, scalar2=EPS,
                                             op0=ALU.mult, op1=ALU.add), reads=[s[0]], writes=[s[1]])
    P.op("scalar", lambda h: h.activation(out=s[2][:], in_=s[1][:], func=AF.Sqrt), reads=[s[1]], writes=[s[2]])
    P.op("vector", lambda h: h.reciprocal(out=s[3][:], in_=s[2][:]), reads=[s[2]], writes=[s[3]])
    if gain_bc is None:
        P.op("scalar", lambda h: h.activation(out=xn[:, 0:ncol], in_=xb[:, 0:ncol], func=AF.Copy, scale=s[3][:]),
             reads=[xb, s[3]], writes=[xn])
    else:
        P.op("vector", lambda h: h.scalar_tensor_tensor(out=xn[:, 0:ncol], in0=xb[:, 0:ncol], scalar=s[3][:],
                                                        in1=gain_bc[:, 0:ncol], op0=ALU.mult, op1=ALU.mult),
             reads=[xb, s[3], gain_bc], writes=[xn])
    nk = ncol // 128
    k0 = 0
    pi = 0
    while k0 < nk:
        n = min(8, nk - k0)
        pt = pst[pi % len(pst)]
        pi += 1
        for kk in range(n):
            k = k0 + kk
            P.op("tensor", lambda h, pt=pt, kk=kk, k=k: h.transpose(out=pt[:, kk, :], in_=xn[:, k * 128:(k + 1) * 128],
                                                                    identity=ident[:]),
                 reads=[xn, ident], writes=[pt], signal=(kk == n - 1))
        dst_fn(k0, n, pt)
        k0 += n


def build(TR=4096, stop=99, dbg=None):
    NB = TR // 128
    NT = TR // 512
    NOWN = NB // 2
    TRo = NOWN * 128
    NTo = TRo // 512
    P = Prog()
    import os
    P.fresh_swdge = bool(os.environ.get('FRESH_SWDGE'))
    nc = P.nc
    I = {}

    def din(name, shape, dt=F32):
        I[name] = P.dram(name, shape, dt, kind="ExternalInput")
        return I[name]

    x = din("x", [TR, D]); g0 = din("g0", [128, FC]); g1 = din("g1", [128, FC]); gf = din("gf", [1, D])
    w0v = din("w0v", [35, 128, 16, 128]); lruvec = din("lruvec", [128, 8, 8])
    gxw = din("gxw", [8, 128, 128]); gaw = din("gaw", [8, 128, 128]); sinks = din("sinks", [1, 16])
    wo0 = din("wo0", [128, 16, 2048])
    w1q = din("w1q", [30, 128, 16, 128]); w1k = din("w1k", [13, 128, 16, 128])
    wdv = din("wdv", [2, 128, 16, 512]); qn = din("qn", [128, 6]); kvn = din("kvn", [128, 4])
    wuq = din("wuq", [128, 6, 1536]); wuqs = din("wuqs", [128, 6, 512])
    wukv_k = din("wukv_k", [128, 4, 1024]); wukv_v = din("wukv_v", [128, 4, 1024]); ropec = din("ropec", [64, 2])
    lams = din("lams", [1, 256]); subln = din("subln", [128, 1]); wo1 = din("wo1", [128, 16, 2048])
    jw = din("jw", [128, 2]); maskab = din("maskab", [128, 256])
    pos_all = din("pos_all", [1, TR]); pos_own = din("pos_own", [1, TRo])
    out = P.dram("out", [TRo, D], F32, kind="ExternalOutput")

    dkind = "ExternalOutput" if dbg else "Internal"
    yT0_d = P.dram("yT0_d", [16, 128, TR], BF16, kind="ExternalOutput" if dbg == "yT0" else "Internal")
    x1_d = P.dram("x1_d", [TR, D], F32, kind="ExternalOutput" if dbg == "x1" else "Internal")
    h1T_d = P.dram("h1T_d", [TR, 16, 128], BF16, kind="ExternalOutput" if dbg == "x1" else "Internal")

    ident = P.sb("ident", [128, 128], BF16)
    io = P.sb("io", [128, 128], I32)
    P.op("gpsimd", lambda h: h.iota(io[:], pattern=[[1, 128]], base=0, channel_multiplier=-1), writes=[io])
    P.op("vector", lambda h: h.tensor_single_scalar(out=ident[:], in_=io[:], scalar=0.0, op=ALU.is_equal),
         reads=[io], writes=[ident])
    m2 = P.sb("m2", [128, 256], BF16)
    P.op("vector", lambda h: h.tensor_single_scalar(out=m2[:, 0:128], in_=io[:], scalar=0.0, op=ALU.is_ge),
         reads=[io], writes=[m2])
    P.op("vector", lambda h: h.tensor_single_scalar(out=m2[:, 128:256], in_=io[:], scalar=0.0, op=ALU.is_lt),
         reads=[io], writes=[m2])
    ones_bf = P.sb("ones_bf", [128, 128], BF16)
    P.op("vector", lambda h: h.memset(ones_bf[:], 1.0), writes=[ones_bf])
    stats = [[P.sb(f"st{i}_{q}", [128, 1], F32) for q in range(4)] for i in range(2)]
    cstage = [P.sb(f"cst{i}", [128, 1024], F32) for i in range(3)]
    cst_i = [0]
    cast_eng = ["gpsimd"]

    def cast_load(dst, dst_ap, src_ap, nel, a=None):
        st = cstage[cst_i[0] % len(cstage)]
        cst_i[0] += 1
        sv = st[:, 0:nel] if a is None else st[:, 0:nel].rearrange("p (a b) -> p a b", a=a)
        P.dma("sync", lambda h: h.dma_start(out=sv, in_=src_ap), st, writes=[st])
        P.op(cast_eng[0], lambda h: h.tensor_copy(out=dst_ap, in_=sv), reads=[st], writes=[dst])

    with P.scope():
        hT = P.sb("hT", [128, FC, TR], BF16)
        with P.scope():
            gt = P.sb("gt", [128, FC], F32)
            P.dma("sync", lambda h: h.dma_start(out=gt[:], in_=g0[:]), gt, writes=[gt])
            xb = [P.sb(f"xb{i}", [128, D], F32) for i in range(2)]
            xn = [P.sb(f"xn{i}", [128, D], BF16) for i in range(2)]
            junk = P.sb("junk", [128, D], BF16)
            pst = [P.ps(f"pst{i}", [128, 8, 128], BF16) for i in range(2)]
            for n in range(NB):
                b = xb[n % 2]
                P.dma("sync", lambda h, b=b, n=n: h.dma_start(out=b[:], in_=x[n * 128:(n + 1) * 128, :]), b, writes=[b])

                def dst(k0, nk, pt, n=n):
                    P.op("vector", lambda h: h.tensor_tensor(
                        out=hT[:, k0:k0 + nk, n * 128:(n + 1) * 128], in0=pt[:, 0:nk, :],
                        in1=gt[:, k0:k0 + nk].unsqueeze(2).to_broadcast([128, nk, 128]), op=ALU.mult),
                        reads=[pt, gt], writes=[hT])
                rmsnorm_to_T(P, b, D, None, xn[n % 2], ident, pst, dst, stats[n % 2], junk)

        wch = [P.sb(f"wch{i}", [128, 16, 128], BF16) for i in range(3)]
        wch_i = [0]

        def load_chunk(src, c):
            t = wch[wch_i[0] % len(wch)]
            wch_i[0] += 1
            cast_load(t, t[:, 0:8, :], src[c, :, 0:8, :], 1024, a=8)
            cast_load(t, t[:, 8:16, :], src[c, :, 8:16, :], 1024, a=8)
            return t

        pz = [P.ps(f"pz{i}", [128, 512], F32) for i in range(4)]
        pz_i = [0]

        def nextpz():
            t = pz[pz_i[0] % len(pz)]
            pz_i[0] += 1
            return t

        def proj_fm(wt, tt, ps, m0=0, m1=128):
            for k in range(16):
                P.op("tensor", lambda h, k=k: h.matmul(ps[0:m1 - m0, :], lhsT=wt[:, k, m0:m1],
                                                       rhs=hT[:, k, tt * 512:(tt + 1) * 512],
                                                       start=(k == 0), stop=(k == 15)),
                     reads=[wt, hT], writes=[ps], signal=(k == 15))

        if True:
            with P.scope():
                lv = P.sb("lv", [128, 8, 8], F32)
                P.dma("sync", lambda h: h.dma_start(out=lv[:], in_=lruvec[:]), lv, writes=[lv])
                cv = P.sb("cv", [128, 8, 2], F32)
                tmpv = P.sb("tmpv", [128, 8], F32)
                P.op("scalar", lambda h: h.activation(out=tmpv[:], in_=lv[:, :, 7], func=AF.Exp, scale=-1.0),
                     reads=[lv], writes=[tmpv])
                P.op("vector", lambda h: h.tensor_scalar_add(out=tmpv[:], in0=tmpv[:], scalar1=1.0), reads=[tmpv], writes=[tmpv])
                P.op("scalar", lambda h: h.activation(out=tmpv[:], in_=tmpv[:], func=AF.Ln), reads=[tmpv], writes=[tmpv])
                P.op("vector", lambda h: h.tensor_scalar_mul(out=cv[:, :, 0], in0=tmpv[:], scalar1=-8.0), reads=[tmpv], writes=[cv])
                P.op("vector", lambda h: h.tensor_scalar_mul(out=cv[:, :, 1], in0=tmpv[:], scalar1=-16.0), reads=[tmpv], writes=[cv])
                gw = [[P.sb(f"gw{i}_{q}", [128, 128], BF16) for q in range(2)] for i in range(2)]
                xbuf = [P.sb(f"xbuf{i}", [128, 515], F32) for i in range(2)]
                hs = [P.sb(f"hs{i}", [128, 512], F32) for i in range(2)]
                ya = [P.sb(f"ya{i}", [128, 512], BF16) for i in range(2)]

                def T(name, dt=F32):
                    return [P.sb(f"{name}{i}", [128, 512], dt) for i in range(2)]
                xc, xcb, gi, gr, av, a2, uu, sg = T("xc"), T("xcb", BF16), T("gi"), T("gr"), T("av"), T("a2"), T("uu"), T("sg")
                zxb = [nextpz(), nextpz()]
                zgb = [nextpz(), nextpz()]
                pgi, pgr = P.ps("pgi_x", [128, 512], F32), P.ps("pgr_x", [128, 512], F32)
                tiles = [(n, tt) for n in range(8) for tt in range(NT)]
                wcur = {}

                def stA(i):
                    n, tt = tiles[i]
                    q = i % 2
                    if tt == 0:
                        wcur["wx"] = load_chunk(w0v, n)
                        wcur["wg"] = load_chunk(w0v, 8 + n)
                        gwn = gw[n % 2]
                        cast_load(gwn[0], gwn[0][:], gxw[n], 128)
                        cast_load(gwn[1], gwn[1][:], gaw[n], 128)
                    zx, zg = zxb[q], zgb[q]
                    proj_fm(wcur["wx"], tt, zx)
                    proj_fm(wcur["wg"], tt, zg)
                    xbq, xbp = xbuf[q], xbuf[1 - q]
                    P.op("scalar", lambda h: h.activation(out=xbq[:, 3:515], in_=zx[:], func=AF.Copy),
                         reads=[zx], writes=[xbq])
                    if tt == 0:
                        P.op("gpsimd", lambda h: h.memset(xbq[:, 0:3], 0.0), writes=[xbq])
                    else:
                        P.op("gpsimd", lambda h: h.tensor_copy(out=xbq[:, 0:3], in_=xbp[:, 512:515]),
                             reads=[xbp], writes=[xbq])
                    xcq = xc[q]
                    P.op("vector", lambda h: h.tensor_scalar(out=xcq[:], in0=xbq[:, 3:515], scalar1=lv[:, n, 3:4],
                                                             scalar2=lv[:, n, 4:5], op0=ALU.mult, op1=ALU.add),
                         reads=[xbq, lv], writes=[xcq])
                    for k in range(3):
                        P.op("vector", lambda h, k=k: h.scalar_tensor_tensor(
                            out=xcq[:], in0=xbq[:, k:k + 512], scalar=lv[:, n, k:k + 1], in1=xcq[:],
                            op0=ALU.mult, op1=ALU.add), reads=[xbq, lv, xcq], writes=[xcq])
                    xcbq = xcb[q]
                    P.op("scalar", lambda h: h.activation(out=xcbq[:], in_=xcq[:], func=AF.Copy), reads=[xcq], writes=[xcbq])
                    sgq = sg[q]
                    P.op("scalar", lambda h: h.activation(out=sgq[:], in_=zg[:], func=AF.Silu), reads=[zg], writes=[sgq])

                def stB(i):
                    n, tt = tiles[i]
                    q = i % 2
                    gwn = gw[n % 2]
                    xcq, xcbq = xc[q], xcb[q]
                    P.op("tensor", lambda h: h.matmul(pgi[:], lhsT=gwn[0][:], rhs=xcbq[:], start=True, stop=True),
                         reads=[gwn[0], xcbq], writes=[pgi])
                    P.op("tensor", lambda h: h.matmul(pgr[:], lhsT=gwn[1][:], rhs=xcbq[:], start=True, stop=True),
                         reads=[gwn[1], xcbq], writes=[pgr])
                    giq, grq, avq, a2q, uq, sgq = gi[q], gr[q], av[q], a2[q], uu[q], sg[q]
                    P.op("scalar", lambda h: h.activation(out=giq[:], in_=pgi[:], func=AF.Sigmoid, bias=lv[:, n, 5:6]),
                         reads=[pgi, lv], writes=[giq])
                    P.op("scalar", lambda h: h.activation(out=grq[:], in_=pgr[:], func=AF.Sigmoid, bias=lv[:, n, 6:7]),
                         reads=[pgr, lv], writes=[grq])
                    P.op("scalar", lambda h: h.activation(out=avq[:], in_=grq[:], func=AF.Exp, scale=cv[:, n, 0:1]),
                         reads=[grq, cv], writes=[avq])
                    P.op("scalar", lambda h: h.activation(out=a2q[:], in_=grq[:], func=AF.Exp, scale=cv[:, n, 1:2]),
                         reads=[grq, cv], writes=[a2q])
                    P.op("gpsimd", lambda h: h.tensor_scalar(out=a2q[:], in0=a2q[:], scalar1=-1.0, scalar2=1.0,
                                                             op0=ALU.mult, op1=ALU.add), reads=[a2q], writes=[a2q])
                    P.op("scalar", lambda h: h.activation(out=a2q[:], in_=a2q[:], func=AF.Sqrt), reads=[a2q], writes=[a2q])
                    P.op("gpsimd", lambda h: h.tensor_tensor(out=uq[:], in0=a2q[:], in1=giq[:], op=ALU.mult),
                         reads=[a2q, giq], writes=[uq])
                    P.op("vector", lambda h: h.tensor_tensor(out=uq[:], in0=uq[:], in1=xcq[:], op=ALU.mult),
                         reads=[uq, xcq], writes=[uq])
                    hq, hp = hs[q], hs[1 - q]
                    if tt == 0:
                        P.op("vector", lambda h: h.tensor_tensor_scan(out=hq[:], data0=avq[:], data1=uq[:], initial=0.0,
                                                                      op0=ALU.mult, op1=ALU.add),
                             reads=[avq, uq], writes=[hq])
                    else:
                        P.op("vector", lambda h: h.tensor_tensor_scan(out=hq[:], data0=avq[:], data1=uq[:],
                                                                      initial=hp[:, 511:512], op0=ALU.mult, op1=ALU.add),
                             reads=[avq, uq, hp], writes=[hq])
                    yan = ya[q]
                    P.op("gpsimd", lambda h: h.tensor_tensor(out=yan[:], in0=hq[:], in1=sgq[:],
                                                             op=ALU.mult), reads=[hq, sgq], writes=[yan])
                    P.dma("sync", lambda h: h.dma_start(out=yT0_d[n, :, tt * 512:(tt + 1) * 512], in_=yan[:]), yan,
                          reads=[yan], writes=[yT0_d])
                stA(0)
                for i in range(len(tiles)):
                    if i + 1 < len(tiles):
                        stA(i + 1)
                    stB(i)

        if stop >= 2:
            with P.scope():
                kd = [P.sb(f"kd{i}", [128, TR], BF16) for i in range(2)]
                vx = P.sb("vx", [128, NB, 2, 65], BF16)
                P.op("gpsimd", lambda h: h.memset(vx[:], 1.0), writes=[vx])
                sk = P.sb("sk", [128, 16], F32)
                P.dma("sync", lambda h: h.dma_start(out=sk[:], in_=sinks[:].partition_broadcast(128)), sk, writes=[sk])
                P.op("scalar", lambda h: h.activation(out=sk[:], in_=sk[:], func=AF.Exp), reads=[sk], writes=[sk])
                for i in range(2):
                    wk = load_chunk(w0v, 24 + i)
                    for tt in range(NT):
                        ps = nextpz()
                        proj_fm(wk, tt, ps)
                        P.op("scalar", lambda h: h.activation(out=kd[i][:, tt * 512:(tt + 1) * 512], in_=ps[:], func=AF.Copy),
                             reads=[ps], writes=[kd[i]])
                wv = load_chunk(w0v, 34)
                for n in range(NB):
                    ps = nextpz()
                    for k in range(16):
                        P.op("tensor", lambda h, k=k: h.matmul(ps[:, 0:128], lhsT=hT[:, k, n * 128:(n + 1) * 128], rhs=wv[:, k, :],
                                                               start=(k == 0), stop=(k == 15)),
                             reads=[hT, wv], writes=[ps], signal=(k == 15))
                    P.op("vector", lambda h: h.tensor_copy(out=vx[:, n, :, 0:64], in_=ps[:, 0:128].rearrange("p (a b) -> p a b", a=2)),
                         reads=[ps], writes=[vx])
                qT = [P.sb(f"qT{i}", [128, TR], BF16) for i in range(1)]
                sgT = [P.sb(f"sgT{i}", [128, 512], BF16) for i in range(2)]
                yb = [P.sb(f"yb{i}", [128, 512], BF16) for i in range(2)]
                ptr = [P.ps(f"ptr{i}", [128, 128], BF16) for i in range(2)]
                PT = [[P.sb(f"PT{i}_{q}", [128, 256], BF16) for q in range(2)] for i in range(4)]
                den = [P.sb(f"den{i}", [128, 2], F32) for i in range(2)]
                ybt = [P.sb(f"ybt{i}", [128, 2, 64], BF16) for i in range(3)]
                pso = [P.ps(f"pso{i}", [128, 2, 65], F32) for i in range(2)]
                for c in range(8):
                    wq = load_chunk(w0v, 16 + c)
                    wg = load_chunk(w0v, 26 + c)
                    qTc = qT[0]
                    for tt in range(NT):
                        ps = nextpz()
                        proj_fm(wq, tt, ps)
                        P.op("scalar", lambda h: h.activation(out=qTc[:, tt * 512:(tt + 1) * 512], in_=ps[:], func=AF.Copy),
                             reads=[ps], writes=[qTc])
                    K = kd[c // 4]
                    kvh = c // 4

                    def stA(m):
                        if m % 4 == 0:
                            tt = m // 4
                            sgTc = sgT[tt % 2]
                            ps2 = nextpz()
                            proj_fm(wg, tt, ps2)
                            P.op("scalar", lambda h: h.activation(out=sgTc[:], in_=ps2[:], func=AF.Silu),
                                 reads=[ps2], writes=[sgTc])
                        ncol = 256 if m < NB - 1 else 128
                        PTm = PT[m % 4]
                        pss_ = [nextpz(), nextpz()]
                        for hh in range(2):
                            P.op("tensor", lambda h: h.matmul(pss_[hh][:, 0:ncol], lhsT=K[64 * hh:64 * hh + 64, m * 128:(m + 1) * 128],
                                                              rhs=qTc[64 * hh:64 * hh + 64, m * 128:m * 128 + ncol],
                                                              start=True, stop=True), reads=[K, qTc], writes=[pss_[hh]])
                        for hh in range(2):
                            P.op("scalar", lambda h: h.activation(out=PTm[hh][:, 0:ncol], in_=pss_[hh][:, 0:ncol], func=AF.Exp, scale=0.125),
                                 reads=[pss_[hh]], writes=[PTm[hh]])
                            P.op("gpsimd", lambda h: h.tensor_tensor(out=PTm[hh][:, 0:ncol], in0=PTm[hh][:, 0:ncol],
                                                                     in1=m2[:, 0:ncol], op=ALU.mult),
                                 reads=[PTm[hh], m2], writes=[PTm[hh]])

                    def stB(m):
                        PTm = PT[m % 4]
                        PTp = PT[(m - 1) % 4]
                        po = pso[m % 2]
                        for hh in range(2):
                            if m > 0:
                                P.op("tensor", lambda h: h.matmul(po[:, hh, :], lhsT=PTp[hh][:, 128:256], rhs=vx[:, m - 1, kvh, :],
                                                                  start=True, stop=False), reads=[PTp[hh], vx], writes=[po], signal=False)
                            P.op("tensor", lambda h: h.matmul(po[:, hh, :], lhsT=PTm[hh][:, 0:128], rhs=vx[:, m, kvh, :],
                                                              start=(m == 0), stop=True), reads=[PTm[hh], vx], writes=[po])
                        dn = den[m % 2]
                        ybm = ybt[m % 3]
                        P.op("vector", lambda h: h.tensor_tensor(out=dn[:], in0=po[:, :, 64], in1=sk[:, 2 * c:2 * c + 2], op=ALU.add),
                             reads=[po, sk], writes=[dn])
                        P.op("vector", lambda h: h.reciprocal(out=dn[:], in_=dn[:]), reads=[dn], writes=[dn])
                        P.op("vector", lambda h: h.tensor_tensor(out=ybm[:], in0=po[:, :, 0:64],
                                                                 in1=dn[:].unsqueeze(2).to_broadcast([128, 2, 64]), op=ALU.mult),
                             reads=[po, dn], writes=[ybm])

                    def stC(m):
                        ybm = ybt[m % 3]
                        tt = m // 4
                        sgTc, ybc = sgT[tt % 2], yb[tt % 2]
                        pt = ptr[m % 2]
                        mm = m % 4
                        P.op("tensor", lambda h: h.transpose(out=pt[:], in_=ybm[:].rearrange("p a b -> p (a b)"), identity=ident[:]),
                             reads=[ybm, ident], writes=[pt])
                        P.op("vector", lambda h: h.tensor_tensor(out=ybc[:, mm * 128:(mm + 1) * 128], in0=pt[:],
                                                                 in1=sgTc[:, mm * 128:(mm + 1) * 128], op=ALU.mult),
                             reads=[pt, sgTc], writes=[ybc])
                        if mm == 3:
                            P.dma("sync", lambda h: h.dma_start(out=yT0_d[8 + c, :, (m - 3) * 128:(m + 1) * 128], in_=ybc[:]), ybc,
                                  reads=[ybc], writes=[yT0_d])
                    for st in range(NB + 2):
                        if st < NB:
                            stA(st)
                        if 1 <= st <= NB:
                            stB(st - 1)
                        if st >= 2:
                            stC(st - 2)

    if stop <= 2:
        P.full_barrier()
        return P.finish(), I

    def outproj_phase(wo_d, yT_tile_fn, ntile, resid_fn, combine_fn, emit_fn, need_xn=True):
        with P.scope():
            wo = P.sb("wo", [128, 16, 2048], BF16)
            for k in range(16):
                for hf in range(2):
                    cast_load(wo, wo[:, k, hf * 1024:(hf + 1) * 1024], wo_d[:, k, hf * 1024:(hf + 1) * 1024], 1024)
            pb = [P.ps(f"pb{i}", [128, 512], F32) for i in range(4)]
            pst = [P.ps(f"pstB{i}", [128, 8, 128], BF16) for i in range(2)] if need_xn else None
            x1 = [P.sb(f"x1{i}", [128, D], F32) for i in range(2)]
            xn = [P.sb(f"xnB{i}", [128, D], BF16) for i in range(2)] if need_xn else [None, None]
            junk = P.sb("junkB", [128, D], BF16)
            hc = 0
            for t4 in range(ntile):
                yt = yT_tile_fn(t4)
                for bi in range(4):
                    n = t4 * 4 + bi
                    rh = resid_fn(n)
                    x1n = x1[n % 2]
                    for half in range(2):
                        banks = pb[0:2] if hc % 2 == 0 else pb[2:4]
                        hc += 1
                        for k in range(16):
                            for g2 in range(2):
                                g = half * 2 + g2
                                P.op("tensor", lambda h, k=k, g=g, g2=g2: h.matmul(banks[g2][:], lhsT=yt[:, k, bi * 128:(bi + 1) * 128],
                                                                                  rhs=wo[:, k, g * 512:(g + 1) * 512],
                                                                                  start=(k == 0), stop=(k == 15)),
                                     reads=[yt, wo], writes=[banks[g2]], signal=(k == 15))
                        for g2 in range(2):
                            combine_fn(half * 2 + g2, banks[g2], x1n, rh)
                    emit_fn(n, x1n, xn[n % 2], junk, pst)

    with P.scope():
        ytile = [P.sb(f"ytile{i}", [128, 16, 512], BF16) for i in range(2)]
        g1t = P.sb("g1t", [128, FC], F32)
        P.dma("sync", lambda h: h.dma_start(out=g1t[:], in_=g1[:]), g1t, writes=[g1t])
        hst = [P.sb(f"hst{i}", [128, 16, 128], BF16) for i in range(2)]

        def ytf(t4):
            yt = ytile[t4 % 2]
            P.dma("sync", lambda h: h.dma_start(out=yt[:], in_=yT0_d[:, :, t4 * 512:(t4 + 1) * 512].rearrange("c p t -> p c t")),
                  yt, reads=[yT0_d], writes=[yt])
            return yt

        xrB = [P.sb(f"xrB{i}", [128, D], F32) for i in range(2)]

        def resid(n):
            xrn = xrB[n % 2]
            P.dma("sync", lambda h: h.dma_start(out=xrn[:], in_=x[n * 128:(n + 1) * 128, :]), xrn, writes=[xrn])
            return xrn

        def combine(g, pbank, x1n, xrn):
            P.op("vector", lambda h: h.tensor_tensor(out=x1n[:, g * 512:(g + 1) * 512], in0=pbank[:],
                                                     in1=xrn[:, g * 512:(g + 1) * 512], op=ALU.add),
                 reads=[pbank, xrn], writes=[x1n])

        def emit(n, x1n, xnn, junk, pst):
            P.dma("sync", lambda h: h.dma_start(out=x1_d[n * 128:(n + 1) * 128, :], in_=x1n[:]), x1n, reads=[x1n], writes=[x1_d])
            hs_ = hst[n % 2]

            def dst(k0, nk, pt):
                P.op("vector", lambda h: h.tensor_tensor(
                    out=hs_[:, k0:k0 + nk, :], in0=pt[:, 0:nk, :],
                    in1=g1t[:, k0:k0 + nk].unsqueeze(2).to_broadcast([128, nk, 128]), op=ALU.mult),
                    reads=[pt, g1t], writes=[hs_])
            rmsnorm_to_T(P, x1n, D, None, xnn, ident, pst, dst, stats[n % 2], junk)
            P.dma("sync", lambda h: h.dma_start(out=h1T_d[n * 128:(n + 1) * 128, :, :], in_=hs_[:]),
                  hs_, reads=[hs_], writes=[h1T_d])
        outproj_phase(wo0, ytf, NT, resid, combine, emit)

    if stop <= 3:
        P.full_barrier()
        return P.finish(), I

    ckvnT_d = P.dram("ckvnT_d", [4, 128, TR], BF16, kind=dkind)
    krT_d = P.dram("krT_d", [64, TR], BF16, kind=dkind)
    dkT_d = P.dram("dkT_d", [8, 128, TR], BF16, kind=dkind)
    dv_d = P.dram("dv_d", [8, 128, NB, 128], BF16, kind=dkind)
    KT_d = P.dram("KT_d", [8, 128, TR], BF16, kind=dkind)
    V_d = P.dram("V_d", [8, 128, NB, 128], BF16, kind=dkind)
    qT_d = P.dram("qT_d", [8, 128, TRo], BF16, kind=dkind)
    qrT_d = P.dram("qrT_d", [8, 64, TRo], BF16, kind=dkind)
    mg_d = P.dram("mg_d", [8, 128, TRo], BF16, kind=dkind)
    dq_d = P.dram("dq_d", [8, 128, TRo], BF16, kind=dkind)
    dg_d = P.dram("dg_d", [8, 128, TRo], BF16, kind=dkind)

    ones_f = P.sb("ones_f", [128, 128], F32)
    P.op("vector", lambda h: h.memset(ones_f[:], 1.0), writes=[ones_f])
    rc = P.sb("rc", [64, 2], F32)
    P.dma("sync", lambda h: h.dma_start(out=rc[:], in_=ropec[:]), rc, writes=[rc])
    TWO_PI = 6.283185307179586
    MAGIC = 12582912.0

    def rope_tables(pos_d, t0, pos_t, ang, kf, cosT, sinT):
        P.dma("sync", lambda h: h.dma_start(out=pos_t[:], in_=pos_d[0:1, t0:t0 + 512].partition_broadcast(64)), pos_t, writes=[pos_t])
        for which, dst in ((0, sinT), (1, cosT)):
            P.op("vector", lambda h: h.tensor_scalar(out=ang[:], in0=pos_t[:], scalar1=rc[:, 0:1],
                                                     scalar2=(1.5707963267948966 if which else 0.0),
                                                     op0=ALU.mult, op1=ALU.add), reads=[pos_t, rc], writes=[ang])
            P.op("vector", lambda h: h.tensor_scalar(out=kf[:], in0=ang[:], scalar1=1.0 / TWO_PI, scalar2=MAGIC,
                                                     op0=ALU.mult, op1=ALU.add), reads=[ang], writes=[kf])
            P.op("vector", lambda h: h.tensor_scalar_add(out=kf[:], in0=kf[:], scalar1=-MAGIC), reads=[kf], writes=[kf])
            P.op("vector", lambda h: h.scalar_tensor_tensor(out=ang[:], in0=kf[:], scalar=-TWO_PI, in1=ang[:],
                                                            op0=ALU.mult, op1=ALU.add), reads=[kf, ang], writes=[ang])
            P.op("vector", lambda h: h.tensor_scalar(out=ang[:], in0=ang[:], scalar1=3.14159, scalar2=-3.14159,
                                                     op0=ALU.min, op1=ALU.max), reads=[ang], writes=[ang])
            P.op("scalar", lambda h: h.activation(out=dst[:], in_=ang[:], func=AF.Sin), reads=[ang], writes=[dst])
        P.op("vector", lambda h: h.tensor_scalar_mul(out=sinT[:], in0=sinT[:], scalar1=rc[:, 1:2]), reads=[sinT, rc], writes=[sinT])

    def rope_apply(pa, pb_, cosT, sinT, t1, t2, dst):
        P.op("vector", lambda h: h.tensor_tensor(out=t1[:], in0=pa[0:64, :], in1=cosT[:], op=ALU.mult), reads=[pa, cosT], writes=[t1])
        P.op("vector", lambda h: h.tensor_tensor(out=t2[:], in0=pb_[0:64, :], in1=sinT[:], op=ALU.mult), reads=[pb_, sinT], writes=[t2])
        P.op("gpsimd", lambda h: h.tensor_tensor(out=dst[0:64, :], in0=t1[:], in1=t2[:], op=ALU.add), reads=[t1, t2], writes=[dst])

    def make_loader(nslots):
        wch = [P.sb(f"wc{P.n_inst}_{i}", [128, 16, 128], BF16) for i in range(nslots)]
        ctr = [0]

        def load_chunk(src, c):
            t = wch[ctr[0] % len(wch)]
            ctr[0] += 1
            cast_load(t, t[:, 0:8, :], src[c, :, 0:8, :], 1024, a=8)
            cast_load(t, t[:, 8:16, :], src[c, :, 8:16, :], 1024, a=8)
            return t
        return load_chunk

    def make_pz(n):
        pz = [P.ps(f"pq{P.n_inst}_{i}", [128, 512], F32) for i in range(n)]
        ctr = [0]

        def nxt():
            t = pz[ctr[0] % len(pz)]
            ctr[0] += 1
            return t
        return nxt

    def proj(wt, src, c0, ps, m0=0, m1=128, nk=16):
        for k in range(nk):
            P.op("tensor", lambda h, k=k: h.matmul(ps[0:m1 - m0, :], lhsT=wt[:, k, m0:m1], rhs=src[:, k, c0:c0 + 512],
                                                   start=(k == 0), stop=(k == nk - 1)),
                 reads=[wt, src], writes=[ps], signal=(k == nk - 1))

    norm_pss = [None]

    def norm_fm(src, nch, nxt, wts, gcol, ncols_total, cf, sq, rst, dst_fn, tcol, post_fn=None):
        pss = norm_pss[0]
        for c in range(nch):
            ps = nxt()
            proj(wts[c], src, tcol, ps)
            P.op("scalar", lambda h: h.activation(out=cf[:, c, :], in_=ps[:], func=AF.Copy), reads=[ps], writes=[cf])
            sqc = sq[c % 2]
            P.op("vector", lambda h: h.tensor_tensor(out=sqc[:], in0=cf[:, c, :], in1=cf[:, c, :], op=ALU.mult), reads=[cf], writes=[sqc])
            P.op("tensor", lambda h: h.matmul(pss[:], lhsT=ones_f[:], rhs=sqc[:], start=(c == 0), stop=(c == nch - 1)),
                 reads=[ones_f, sqc], writes=[pss], signal=(c == nch - 1))
        P.op("vector", lambda h: h.tensor_scalar(out=rst[:], in0=pss[:], scalar1=1.0 / ncols_total, scalar2=EPS,
                                                 op0=ALU.mult, op1=ALU.add), reads=[pss], writes=[rst])
        P.op("scalar", lambda h: h.activation(out=rst[:], in_=rst[:], func=AF.Sqrt), reads=[rst], writes=[rst])
        P.op("vector", lambda h: h.reciprocal(out=rst[:], in_=rst[:]), reads=[rst], writes=[rst])
        for c in range(nch):
            o_ap, ob = dst_fn(c)
            P.op("vector", lambda h: h.scalar_tensor_tensor(out=o_ap, in0=cf[:, c, :], scalar=gcol[:, c:c + 1], in1=rst[:],
                                                            op0=ALU.mult, op1=ALU.mult), reads=[cf, gcol, rst], writes=[ob])
            if post_fn is not None:
                post_fn(c, ob)

    with P.scope():
        h1T = P.sb("h1T", [128, 16, TR], BF16)
        for n in range(NB):
            P.dma("sync", lambda h: h.dma_start(out=h1T[:, :, n * 128:(n + 1) * 128], in_=h1T_d[n * 128:(n + 1) * 128, :, :]),
                  h1T, reads=[h1T_d], writes=[h1T])
        nxt = make_pz(5)
        norm_pss[0] = P.ps("npss1", [128, 512], F32)
        stg = [P.sb(f"stg{i}", [128, 512], BF16) for i in range(3)]
        stg_i = [0]

        def stage():
            t = stg[stg_i[0] % 3]
            stg_i[0] += 1
            return t
        with P.scope():
            ld = make_loader(4)
            wts = [ld(w1k, 9 + c) for c in range(4)]
            kvg = P.sb("kvg", [128, 4], F32)
            P.dma("sync", lambda h: h.dma_start(out=kvg[:], in_=kvn[:]), kvg, writes=[kvg])
            cf = P.sb("cf", [128, 4, 512], F32)
            sq = [P.sb(f"sq{i}", [128, 512], F32) for i in range(2)]
            rst = P.sb("rst", [128, 512], F32)
            for tt in range(NT):
                def dst(c):
                    t = stage()
                    return t[:], t

                def post(c, t):
                    P.dma("sync", lambda h: h.dma_start(out=ckvnT_d[c, :, tt * 512:(tt + 1) * 512], in_=t[:]), t, reads=[t], writes=[ckvnT_d])
                norm_fm(h1T, 4, nxt, wts, kvg, 512.0, cf, sq, rst, dst, tt * 512, post)
        with P.scope():
            ld = make_loader(3)
            wkr = ld(w1k, 0)
            pos_t = P.sb("pos_t", [64, 512], F32); ang = P.sb("ang", [64, 512], F32); kf = P.sb("kf", [64, 512], F32)
            cosT = P.sb("cosT", [64, 512], F32); sinT = P.sb("sinT", [64, 512], F32)
            t1 = P.sb("t1", [64, 512], F32); t2 = P.sb("t2", [64, 512], F32)
            for tt in range(NT):
                rope_tables(pos_all, tt * 512, pos_t, ang, kf, cosT, sinT)
                pa, pb_ = nxt(), nxt()
                proj(wkr, h1T, tt * 512, pa, 0, 64)
                proj(wkr, h1T, tt * 512, pb_, 64, 128)
                t = stage()
                rope_apply(pa, pb_, cosT, sinT, t1, t2, t)
                P.dma("sync", lambda h: h.dma_start(out=krT_d[:, tt * 512:(tt + 1) * 512], in_=t[0:64, :]), t, reads=[t], writes=[krT_d])
            for hh in range(8):
                wt = ld(w1k, 1 + hh)
                for tt in range(NT):
                    ps = nxt()
                    proj(wt, h1T, tt * 512, ps)
                    t = stage()
                    P.op("scalar", lambda h: h.activation(out=t[:], in_=ps[:], func=AF.Copy), reads=[ps], writes=[t])
                    P.dma("sync", lambda h: h.dma_start(out=dkT_d[hh, :, tt * 512:(tt + 1) * 512], in_=t[:]), t, reads=[t], writes=[dkT_d])
        with P.scope():
            wdvt = P.sb("wdvt", [128, 16, 512], BF16)
            for g in range(2):
                for k2 in range(8):
                    cast_load(wdvt, wdvt[:, k2 * 2:(k2 + 1) * 2, :], wdv[g, :, k2 * 2:(k2 + 1) * 2, :], 1024, a=2)
                for n in range(NB):
                    ps = nxt()
                    for k in range(16):
                        P.op("tensor", lambda h, k=k: h.matmul(ps[:], lhsT=h1T[:, k, n * 128:(n + 1) * 128], rhs=wdvt[:, k, :],
                                                               start=(k == 0), stop=(k == 15)), reads=[h1T, wdvt], writes=[ps], signal=(k == 15))
                    t = stage()
                    P.op("scalar", lambda h: h.activation(out=t[:], in_=ps[:], func=AF.Copy), reads=[ps], writes=[t])
                    P.dma("sync", lambda h: h.dma_start(out=dv_d[g * 4:(g + 1) * 4, :, n, :].rearrange("h p d -> p h d"),
                                                        in_=t[:].rearrange("p (h d) -> p h d", h=4)), t, reads=[t], writes=[dv_d])

    with P.scope():
        ckT = P.sb("ckT", [128, 4, TR], BF16)
        P.dma("sync", lambda h: h.dma_start(out=ckT[:], in_=ckvnT_d[:].rearrange("c p t -> p c t")), ckT, reads=[ckvnT_d], writes=[ckT])
        wk_ = P.sb("wk_", [128, 4, 1024], BF16); wv_ = P.sb("wv_", [128, 4, 1024], BF16)
        for kc in range(4):
            cast_load(wk_, wk_[:, kc, :], wukv_k[:, kc, :], 1024)
            cast_load(wv_, wv_[:, kc, :], wukv_v[:, kc, :], 1024)
        nxt = make_pz(4)
        stg = [P.sb(f"stgc{i}", [128, 512], BF16) for i in range(3)]
        si = 0
        for hh in range(8):
            for tt in range(NT):
                ps = nxt()
                proj(wk_, ckT, tt * 512, ps, hh * 128, (hh + 1) * 128, nk=4)
                t = stg[si % 3]; si += 1
                P.op("scalar", lambda h: h.activation(out=t[:], in_=ps[:], func=AF.Copy), reads=[ps], writes=[t])
                P.dma("sync", lambda h: h.dma_start(out=KT_d[hh, :, tt * 512:(tt + 1) * 512], in_=t[:]), t, reads=[t], writes=[KT_d])
        for g in range(2):
            for n in range(NB):
                ps = nxt()
                for k in range(4):
                    P.op("tensor", lambda h, k=k: h.matmul(ps[:], lhsT=ckT[:, k, n * 128:(n + 1) * 128], rhs=wv_[:, k, g * 512:(g + 1) * 512],
                                                           start=(k == 0), stop=(k == 3)), reads=[ckT, wv_], writes=[ps], signal=(k == 3))
                t = stg[si % 3]; si += 1
                P.op("scalar", lambda h: h.activation(out=t[:], in_=ps[:], func=AF.Copy), reads=[ps], writes=[t])
                P.dma("sync", lambda h: h.dma_start(out=V_d[g * 4:(g + 1) * 4, :, n, :].rearrange("h p d -> p h d"),
                                                    in_=t[:].rearrange("p (h d) -> p h d", h=4)), t, reads=[t], writes=[V_d])

    jwt = P.sb("jwt", [128, 2], F32)
    P.dma("sync", lambda h: h.dma_start(out=jwt[:], in_=jw[:]), jwt, writes=[jwt])
    with P.scope():
        h1o = P.sb("h1o", [128, 16, TRo], BF16)
        with P.scope():
            ga = [P.sb(f"ga{i}", [128, 16, 128], BF16) for i in range(2)]
            gb = [P.sb(f"gb{i}", [128, 16, 128], BF16) for i in range(2)]
            for i in range(NOWN):
                a_, b_ = ga[i % 2], gb[i % 2]
                P.dma("sync", lambda h: h.dma_start(out=a_[:], in_=h1T_d[(2 * i) * 128:(2 * i + 1) * 128, :, :]), a_, reads=[h1T_d], writes=[a_])
                P.dma("sync", lambda h: h.dma_start(out=b_[:], in_=h1T_d[(2 * i + 1) * 128:(2 * i + 2) * 128, :, :]), b_, reads=[h1T_d], writes=[b_])
                P.op("vector", lambda h: h.tensor_scalar_mul(out=a_[:], in0=a_[:], scalar1=jwt[:, 0:1]), reads=[a_, jwt], writes=[a_])
                P.op("vector", lambda h: h.scalar_tensor_tensor(out=h1o[:, :, i * 128:(i + 1) * 128], in0=b_[:], scalar=jwt[:, 1:2], in1=a_[:],
                                                                op0=ALU.mult, op1=ALU.add), reads=[b_, jwt, a_], writes=[h1o])
        nxt = make_pz(5)
        norm_pss[0] = P.ps("npss2", [128, 512], F32)
        stg = [P.sb(f"stgq{i}", [128, 512], BF16) for i in range(3)]
        stg_i = [0]

        def stage():
            t = stg[stg_i[0] % 3]
            stg_i[0] += 1
            return t
        with P.scope():
            cqn = P.sb("cqn", [128, 6, TRo], BF16)
            with P.scope():
                ld = make_loader(6)
                wts = [ld(w1q, c) for c in range(6)]
                qg = P.sb("qg", [128, 6], F32)
                P.dma("sync", lambda h: h.dma_start(out=qg[:], in_=qn[:]), qg, writes=[qg])
                cf = P.sb("cfq", [128, 6, 512], F32)
                sq = [P.sb(f"sqq{i}", [128, 512], F32) for i in range(2)]
                rst = P.sb("rstq", [128, 512], F32)
                for tt in range(NTo):
                    norm_fm(h1o, 6, nxt, wts, qg, 768.0, cf, sq, rst, lambda c: (cqn[:, c, tt * 512:(tt + 1) * 512], cqn), tt * 512)
            with P.scope():
                wq_ = P.sb("wq_", [128, 6, 1536], BF16); wqs_ = P.sb("wqs_", [128, 6, 512], BF16)
                for kc in range(6):
                    cast_load(wq_, wq_[:, kc, 0:768], wuq[:, kc, 0:768], 768)
                    cast_load(wq_, wq_[:, kc, 768:1536], wuq[:, kc, 768:1536], 768)
                    cast_load(wqs_, wqs_[:, kc, :], wuqs[:, kc, :], 512)
                pos_t = P.sb("pos_tq", [64, 512], F32); ang = P.sb("angq", [64, 512], F32); kf = P.sb("kfq", [64, 512], F32)
                cosT = P.sb("cosTq", [64, 512], F32); sinT = P.sb("sinTq", [64, 512], F32)
                t1 = P.sb("t1q", [64, 512], F32); t2 = P.sb("t2q", [64, 512], F32)
                for tt in range(NTo):
                    rope_tables(pos_own, tt * 512, pos_t, ang, kf, cosT, sinT)
                    for hh in range(8):
                        ps = nxt()
                        proj(wq_, cqn, tt * 512, ps, hh * 192, hh * 192 + 128, nk=6)
                        t = stage()
                        P.op("scalar", lambda h: h.activation(out=t[:], in_=ps[:], func=AF.Copy), reads=[ps], writes=[t])
                        P.dma("sync", lambda h: h.dma_start(out=qT_d[hh, :, tt * 512:(tt + 1) * 512], in_=t[:]), t, reads=[t], writes=[qT_d])
                        pa, pb_ = nxt(), nxt()
                        proj(wq_, cqn, tt * 512, pa, hh * 192 + 128, hh * 192 + 192, nk=6)
                        proj(wqs_, cqn, tt * 512, pb_, hh * 64, hh * 64 + 64, nk=6)
                        t = stage()
                        rope_apply(pa, pb_, cosT, sinT, t1, t2, t)
                        P.dma("sync", lambda h: h.dma_start(out=qrT_d[hh, :, tt * 512:(tt + 1) * 512], in_=t[0:64, :]), t, reads=[t], writes=[qrT_d])
        with P.scope():
            ld = make_loader(3)
            for (base, dstd, fn) in ((6, mg_d, AF.Silu), (14, dq_d, AF.Copy), (22, dg_d, AF.Silu)):
                for hh in range(8):
                    wt = ld(w1q, base + hh)
                    for tt in range(NTo):
                        ps = nxt()
                        proj(wt, h1o, tt * 512, ps)
                        t = stage()
                        P.op("scalar", lambda h: h.activation(out=t[:], in_=ps[:], func=fn), reads=[ps], writes=[t])
                        P.dma("sync", lambda h: h.dma_start(out=dstd[hh, :, tt * 512:(tt + 1) * 512], in_=t[:]), t, reads=[t], writes=[dstd])

    with P.scope():
        yT1 = P.sb("yT1", [128, 16, TRo], BF16)
        mab_f = P.sb("mab_f", [128, 256], F32)
        mab = P.sb("mab", [128, 256], BF16)
        P.dma("sync", lambda h: h.dma_start(out=mab_f[:], in_=maskab[:]), mab_f, writes=[mab_f])
        P.op("vector", lambda h: h.tensor_copy(out=mab[:], in_=mab_f[:]), reads=[mab_f], writes=[mab])

        lacc = [P.sb(f"lacc{i}", [128, 512], F32) for i in range(2)]

        def attn_tile(qt, qk_fn, Vt, scale, ps2, po, pl, PTs):
            i0 = 4 * qt
            mlast = 2 * i0 + 7
            NPT = len(PTs)

            def A(m):
                c0 = max(0, m // 2 - i0) * 128
                ps = ps2[m % 2]
                PT = PTs[m % NPT]
                qk_fn(ps, m, qt * 512 + c0, c0)
                P.op("scalar", lambda h: h.activation(out=PT[:, c0:512], in_=ps[:, c0:512], func=AF.Exp, scale=scale),
                     reads=[ps], writes=[PT])
                if m >= 2 * i0:
                    mo = (m % 2) * 128
                    P.op("gpsimd", lambda h: h.tensor_tensor(out=PT[:, c0:c0 + 128], in0=PT[:, c0:c0 + 128],
                                                             in1=mab[:, mo:mo + 128], op=ALU.mult), reads=[PT, mab], writes=[PT])

            def B(m):
                c0 = max(0, m // 2 - i0) * 128
                PT = PTs[m % NPT]
                P.op("tensor", lambda h: h.matmul(po[:, c0:512], lhsT=Vt[:, m, :], rhs=PT[:, c0:512], start=(m == 0), stop=(m == mlast)),
                     reads=[Vt, PT], writes=[po], signal=(m == mlast))
                if m == 0:
                    P.op("vector", lambda h: h.tensor_copy(out=lacc[0][:], in_=PT[:]), reads=[PT], writes=[lacc[0]])
                elif m % 3 != 2:
                    P.op("vector", lambda h: h.tensor_tensor(out=lacc[0][:, c0:512], in0=lacc[0][:, c0:512], in1=PT[:, c0:512], op=ALU.add),
                         reads=[lacc[0], PT], writes=[lacc[0]])
                else:
                    P.op("gpsimd", lambda h: h.tensor_tensor(out=lacc[1][:, c0:512], in0=lacc[1][:, c0:512], in1=PT[:, c0:512], op=ALU.add),
                         reads=[lacc[1], PT], writes=[lacc[1]])
            P.op("gpsimd", lambda h: h.memset(lacc[1][:], 0.0), writes=[lacc[1]])
            A(0)
            for m in range(1, mlast + 1):
                A(m)
                B(m - 1)
            B(mlast)
            P.op("tensor", lambda h: h.matmul(pl[:], lhsT=ones_f[:], rhs=lacc[0][:], start=True, stop=False), reads=[ones_f, lacc[0]], writes=[pl], signal=False)
            P.op("tensor", lambda h: h.matmul(pl[:], lhsT=ones_f[:], rhs=lacc[1][:], start=False, stop=True), reads=[ones_f, lacc[1]], writes=[pl])

        with P.scope():
            krT = P.sb("krT", [64, TR], BF16)
            P.dma("sync", lambda h: h.dma_start(out=krT[:], in_=krT_d[:]), krT, reads=[krT_d], writes=[krT])
            KT = [P.sb(f"KT{i}", [128, TR], BF16) for i in range(2)]
            Vt = [P.sb(f"Vt{i}", [128, NB, 128], BF16) for i in range(2)]
            qT = [P.sb(f"qTh{i}", [128, TRo], BF16) for i in range(2)]
            qr = [P.sb(f"qrh{i}", [64, TRo], BF16) for i in range(2)]
            mg = [P.sb(f"mgh{i}", [128, TRo], BF16) for i in range(2)]
            PTs = [P.sb(f"PTd{i}", [128, 512], BF16) for i in range(4)]
            rl = [P.sb(f"rl{i}", [128, 512], F32) for i in range(2)]
            ps2 = [P.ps(f"psS{i}", [128, 512], F32) for i in range(2)]
            poo = [P.ps(f"poo{i}", [128, 512], F32) for i in range(2)]
            pll = [P.ps(f"pll{i}", [128, 512], F32) for i in range(2)]
            it = 0
            for hh in range(8):
                q = hh % 2
                P.dma("sync", lambda h: h.dma_start(out=KT[q][:], in_=KT_d[hh]), KT[q], reads=[KT_d], writes=[KT[q]])
                P.dma("sync", lambda h: h.dma_start(out=Vt[q][:], in_=V_d[hh]), Vt[q], reads=[V_d], writes=[Vt[q]])
                P.dma("sync", lambda h: h.dma_start(out=qT[q][:], in_=qT_d[hh]), qT[q], reads=[qT_d], writes=[qT[q]])
                P.dma("sync", lambda h: h.dma_start(out=qr[q][:], in_=qrT_d[hh]), qr[q], reads=[qrT_d], writes=[qr[q]])
                P.dma("sync", lambda h: h.dma_start(out=mg[q][:], in_=mg_d[hh]), mg[q], reads=[mg_d], writes=[mg[q]])
                for qt in range(NTo):
                    po, pl = poo[it % 2], pll[it % 2]
                    rlt = rl[it % 2]
                    it += 1

                    def qk(ps, m, qc, c0):
                        P.op("tensor", lambda h: h.matmul(ps[:, c0:512], lhsT=KT[q][:, m * 128:(m + 1) * 128], rhs=qT[q][:, qc:qt * 512 + 512],
                                                          start=True, stop=False), reads=[KT[q], qT[q]], writes=[ps], signal=False)
                        P.op("tensor", lambda h: h.matmul(ps[:, c0:512], lhsT=krT[:, m * 128:(m + 1) * 128], rhs=qr[q][:, qc:qt * 512 + 512],
                                                          start=False, stop=True), reads=[krT, qr[q]], writes=[ps])
                    attn_tile(qt, qk, Vt[q], 192.0 ** -0.5, ps2, po, pl, PTs)
                    P.op("vector", lambda h: h.reciprocal(out=rlt[:], in_=pl[:]), reads=[pl], writes=[rlt])
                    P.op("vector", lambda h: h.tensor_tensor(out=rlt[:], in0=po[:], in1=rlt[:], op=ALU.mult), reads=[po, rlt], writes=[rlt])
                    P.op("gpsimd", lambda h: h.tensor_tensor(out=yT1[:, hh, qt * 512:(qt + 1) * 512], in0=rlt[:],
                                                             in1=mg[q][:, qt * 512:(qt + 1) * 512], op=ALU.mult),
                         reads=[rlt, mg[q]], writes=[yT1])

        with P.scope():
            LINIT = 0.8 - 0.6 * float(np.exp(-0.3 * 1))
            lm = P.sb("lm", [128, 256], F32)
            P.dma("sync", lambda h: h.dma_start(out=lm[:], in_=lams[:].partition_broadcast(128)), lm, writes=[lm])
            lp = P.sb("lp", [128, 2, 64], F32)
            ls = P.sb("ls", [128, 2], F32)
            nlam = P.sb("nlam", [128, 1], F32)
            for a in range(2):
                P.op("vector", lambda h: h.tensor_tensor(out=lp[:, a, :], in0=lm[:, a * 128:a * 128 + 64], in1=lm[:, a * 128 + 64:a * 128 + 128],
                                                         op=ALU.mult), reads=[lm], writes=[lp])
            P.op("vector", lambda h: h.tensor_reduce(out=ls[:], in_=lp[:], axis=AX.X, op=ALU.add), reads=[lp], writes=[ls])
            P.op("scalar", lambda h: h.activation(out=ls[:], in_=ls[:], func=AF.Exp), reads=[ls], writes=[ls])
            P.op("vector", lambda h: h.tensor_tensor(out=nlam[:], in0=ls[:, 1:2], in1=ls[:, 0:1], op=ALU.subtract), reads=[ls], writes=[nlam])
            P.op("vector", lambda h: h.tensor_scalar_add(out=nlam[:], in0=nlam[:], scalar1=-LINIT), reads=[nlam], writes=[nlam])
            sln = P.sb("sln", [128, 1], F32)
            P.dma("sync", lambda h: h.dma_start(out=sln[:], in_=subln[:]), sln, writes=[sln])
            P.op("vector", lambda h: h.tensor_scalar_mul(out=sln[:], in0=sln[:], scalar1=1.0 - LINIT), reads=[sln], writes=[sln])
            dk = [P.sb(f"dk{i}", [128, TR], BF16) for i in range(2)]
            dvt = [P.sb(f"dvt{i}", [128, NB, 128], BF16) for i in range(2)]
            dq = [P.sb(f"dq{i}", [128, TRo], BF16) for i in range(2)]
            dg = [P.sb(f"dg{i}", [128, TRo], BF16) for i in range(2)]
            PTs = [P.sb(f"PTe{i}", [128, 512], BF16) for i in range(4)]
            r0t = P.sb("r0t", [128, 512], F32); r1t = P.sb("r1t", [128, 512], F32); sqt = P.sb("sqt", [128, 512], F32)
            ps2 = [P.ps(f"peS{i}", [128, 512], F32) for i in range(2)]
            poo = [P.ps(f"peo{i}", [128, 512], F32) for i in range(2)]
            pll = [P.ps(f"pel{i}", [128, 512], F32) for i in range(2)]
            pss = P.ps("pess", [128, 512], F32)
            for hh in range(8):
                q = hh % 2
                P.dma("sync", lambda h: h.dma_start(out=dk[q][:], in_=dkT_d[hh]), dk[q], reads=[dkT_d], writes=[dk[q]])
                P.dma("sync", lambda h: h.dma_start(out=dvt[q][:], in_=dv_d[hh]), dvt[q], reads=[dv_d], writes=[dvt[q]])
                P.dma("sync", lambda h: h.dma_start(out=dq[q][:], in_=dq_d[hh]), dq[q], reads=[dq_d], writes=[dq[q]])
                P.dma("sync", lambda h: h.dma_start(out=dg[q][:], in_=dg_d[hh]), dg[q], reads=[dg_d], writes=[dg[q]])
                for qt in range(NTo):
                    for c in range(2):
                        def qk(ps, m, qc, c0):
                            P.op("tensor", lambda h: h.matmul(ps[:, c0:512], lhsT=dk[q][64 * c:64 * c + 64, m * 128:(m + 1) * 128],
                                                              rhs=dq[q][64 * c:64 * c + 64, qc:qt * 512 + 512], start=True, stop=True),
                                 reads=[dk[q], dq[q]], writes=[ps])
                        attn_tile(qt, qk, dvt[q], 0.125, ps2, poo[c], pll[c], PTs)
                    P.op("vector", lambda h: h.reciprocal(out=r0t[:], in_=pll[0][:]), reads=[pll[0]], writes=[r0t])
                    P.op("vector", lambda h: h.tensor_tensor(out=r0t[:], in0=poo[0][:], in1=r0t[:], op=ALU.mult), reads=[poo[0], r0t], writes=[r0t])
                    P.op("vector", lambda h: h.reciprocal(out=r1t[:], in_=pll[1][:]), reads=[pll[1]], writes=[r1t])
                    P.op("vector", lambda h: h.tensor_tensor(out=r1t[:], in0=poo[1][:], in1=r1t[:], op=ALU.mult), reads=[poo[1], r1t], writes=[r1t])
                    P.op("vector", lambda h: h.scalar_tensor_tensor(out=r0t[:], in0=r1t[:], scalar=nlam[:, 0:1], in1=r0t[:],
                                                                    op0=ALU.mult, op1=ALU.add), reads=[r1t, nlam, r0t], writes=[r0t])
                    P.op("scalar", lambda h: h.activation(out=sqt[:], in_=r0t[:], func=AF.Square), reads=[r0t], writes=[sqt])
                    P.op("tensor", lambda h: h.matmul(pss[:], lhsT=ones_f[:], rhs=sqt[:], start=True, stop=True), reads=[ones_f, sqt], writes=[pss])
                    P.op("vector", lambda h: h.tensor_scalar(out=r1t[:], in0=pss[:], scalar1=1.0 / 128, scalar2=EPS, op0=ALU.mult, op1=ALU.add),
                         reads=[pss], writes=[r1t])
                    P.op("scalar", lambda h: h.activation(out=r1t[:], in_=r1t[:], func=AF.Sqrt), reads=[r1t], writes=[r1t])
                    P.op("vector", lambda h: h.reciprocal(out=r1t[:], in_=r1t[:]), reads=[r1t], writes=[r1t])
                    P.op("vector", lambda h: h.tensor_tensor(out=r0t[:], in0=r0t[:], in1=r1t[:], op=ALU.mult), reads=[r0t, r1t], writes=[r0t])
                    P.op("vector", lambda h: h.scalar_tensor_tensor(out=yT1[:, 8 + hh, qt * 512:(qt + 1) * 512], in0=r0t[:], scalar=sln[:, 0:1],
                                                                    in1=dg[q][:, qt * 512:(qt + 1) * 512], op0=ALU.mult, op1=ALU.mult),
                         reads=[r0t, sln, dg[q]], writes=[yT1])

        with P.scope():
            gfb = P.sb("gfb", [128, D], F32)
            P.dma("sync", lambda h: h.dma_start(out=gfb[:], in_=gf[:].partition_broadcast(128)), gfb, writes=[gfb])

            def ytf(t4):
                return Buf(yT1.t[:, :, t4 * 512:(t4 + 1) * 512], "ytv")

            xra = [P.sb(f"xra{i}", [128, D], F32) for i in range(2)]
            xrb = [P.sb(f"xrb{i}", [128, D], F32) for i in range(2)]

            def resid(n):
                a_, b_ = xra[n % 2], xrb[n % 2]
                P.dma("sync", lambda h: h.dma_start(out=a_[:], in_=x1_d[(2 * n) * 128:(2 * n + 1) * 128, :]), a_, reads=[x1_d], writes=[a_])
                P.dma("sync", lambda h: h.dma_start(out=b_[:], in_=x1_d[(2 * n + 1) * 128:(2 * n + 2) * 128, :]), b_, reads=[x1_d], writes=[b_])
                return (a_, b_)

            def combine(g, pbank, x1n, rh):
                a_, b_ = rh
                sl = slice(g * 512, (g + 1) * 512)
                P.op("vector", lambda h: h.scalar_tensor_tensor(out=x1n[:, sl], in0=a_[:, sl], scalar=jwt[:, 0:1], in1=pbank[:],
                                                                op0=ALU.mult, op1=ALU.add), reads=[a_, jwt, pbank], writes=[x1n])
                P.op("vector", lambda h: h.scalar_tensor_tensor(out=x1n[:, sl], in0=b_[:, sl], scalar=jwt[:, 1:2], in1=x1n[:, sl],
                                                                op0=ALU.mult, op1=ALU.add), reads=[b_, jwt, x1n], writes=[x1n])

            def emit(n, x2, xnn, junk, pst):
                s_ = stats[n % 2]
                P.op("scalar", lambda h: h.activation(out=junk[:], in_=x2[:], func=AF.Square, accum_out=s_[0][:]), reads=[x2], writes=[junk, s_[0]])
                P.op("vector", lambda h: h.tensor_scalar(out=s_[1][:], in0=s_[0][:], scalar1=1.0 / D, scalar2=EPS, op0=ALU.mult, op1=ALU.add),
                     reads=[s_[0]], writes=[s_[1]])
                P.op("scalar", lambda h: h.activation(out=s_[2][:], in_=s_[1][:], func=AF.Sqrt), reads=[s_[1]], writes=[s_[2]])
                P.op("vector", lambda h: h.reciprocal(out=s_[3][:], in_=s_[2][:]), reads=[s_[2]], writes=[s_[3]])
                o = x2
                P.op("vector", lambda h: h.scalar_tensor_tensor(out=o[:], in0=x2[:], scalar=s_[3][:], in1=gfb[:], op0=ALU.mult, op1=ALU.mult),
                     reads=[x2, s_[3], gfb], writes=[o])
                P.dma("sync", lambda h: h.dma_start(out=out[n * 128:(n + 1) * 128, :], in_=o[:]), o, reads=[o], writes=[out])
            outproj_phase(wo1, ytf, NTo, resid, combine, emit, need_xn=False)

    P.full_barrier()
    return P.finish(), I


_CACHE = {}


def run_full(inputs, TR=4096):
    inputs = {k: np.asarray(v) for k, v in inputs.items()}
    if TR not in _CACHE:
        _CACHE[TR] = build(TR=TR)
    nc, I = _CACHE[TR]
    ins = []
    for c in range(8):
        d = prep_inputs(inputs, c, TR)
        ins.append({k: v for k, v in d.items() if k in I})
    res = run_bass_kernel_spmd(nc, ins, core_ids=list(range(8)))
    B = inputs["x"].shape[0]
    outp = np.zeros((B, TR, D), np.float32)
    NOWN = TR // 256
    for c in range(8):
        b, j = c // 2, c % 2
        o = np.asarray(res.results[c]["out"])
        for i in range(NOWN):
            g = 2 * i + j
            outp[b, g * 128:(g + 1) * 128] = o[i * 128:(i + 1) * 128]
    return outp


def kernel(**inputs):
    return run_full(inputs, 4096)
```

```python
import numpy as np
from concourse.bass_utils import run_bass_kernel_spmd
from contextlib import ExitStack
import concourse.bass as bass
import concourse.mybir as mybir

F32 = mybir.dt.float32
BF16 = mybir.dt.bfloat16
I32 = mybir.dt.int32
U32 = mybir.dt.uint32
AF = mybir.ActivationFunctionType
ALU = mybir.AluOpType
AX = mybir.AxisListType


class Buf:
    def __init__(self, t, name, multi=False):
        self.t = t
        self.name = name
        self.w = {}
        self.r = {}
        self.multi = multi
        self.dsem = None
        self.dcount = 0

    def __getitem__(self, k):
        return self.t[k]


class Eng:
    def __init__(self, name, h, sem):
        self.name = name
        self.h = h
        self.sem = sem
        self.count = 0
        self.seen = {}
        self.stream = []


class Prog:
    def __init__(self):
        self.nc = bass.Bass("TRN2", target_bir_lowering=False)
        self.es = ExitStack()
        nc = self.nc
        self.E = {}
        for name in ["tensor", "vector", "scalar", "gpsimd", "sync"]:
            sem = self.es.enter_context(nc.semaphore("s_" + name))
            self.E[name] = Eng(name, getattr(nc, name), sem)
        self.nbuf = 0
        self.n_inst = 0
        self.dsems_all = {}
        self.dsem_free = []
        self.stack = [self.es]
        self.scope_bufs = [[]]

    def sb(self, name, shape, dtype):
        self.nbuf += 1
        name = f"{name}_{self.nbuf}"
        t = self.stack[-1].enter_context(self.nc.sbuf_tensor(name, list(shape), dtype))
        b = Buf(t, name)
        self.scope_bufs[-1].append(b)
        return b

    def ps(self, name, shape, dtype):
        self.nbuf += 1
        name = f"{name}_{self.nbuf}"
        t = self.stack[-1].enter_context(self.nc.psum_tensor(name, list(shape), dtype))
        b = Buf(t, name)
        self.scope_bufs[-1].append(b)
        return b

    def dram(self, name, shape, dtype, kind="Internal"):
        t = self.nc.dram_tensor(name, list(shape), dtype, kind=kind).ap()
        return Buf(t, name, multi=True)

    def _wait(self, eng, evs):
        for key, (sem, val) in evs.items():
            if sem is eng.sem:
                if val > eng.count:
                    continue
                if eng.name in ("tensor", "sync"):
                    continue
            if eng.seen.get(key, 0) >= val:
                continue
            eng.seen[key] = val
            eng.h.wait_ge(sem, val)

    @staticmethod
    def _merge(d, key, sem, val):
        if key not in d or d[key][1] < val:
            d[key] = (sem, val)

    def _deps(self, eng, reads, writes):
        evs = {}
        for b in reads:
            for k, (s, v) in b.w.items():
                self._merge(evs, k, s, v)
        for b in writes:
            if b.multi:
                continue
            for k, (s, v) in b.w.items():
                self._merge(evs, k, s, v)
            for k, (s, v) in b.r.items():
                self._merge(evs, k, s, v)
        self._wait(eng, evs)

    def _record(self, reads, writes, key, sem, val):
        for b in reads:
            self._merge(b.r, key, sem, val)
        for b in writes:
            if b.multi:
                self._merge(b.w, key, sem, val)
            else:
                b.w = {key: (sem, val)}
                b.r = {}

    def op(self, engname, fn, reads=(), writes=(), signal=True):
        eng = self.E[engname]
        self._deps(eng, reads, writes)
        self.n_inst += 1
        if signal:
            eng.count += 1
            val = eng.count
            sem = eng.sem
            fn(eng.h).then_inc(sem, 1)
        else:
            val = eng.count + 1
            fn(eng.h)
        self._record(reads, writes, id(eng.sem), eng.sem, val)

    def dma(self, qname, fn, owner, reads=(), writes=(), n=1):
        eng = self.E[qname]
        fresh = qname == "gpsimd" and getattr(self, "fresh_swdge", False)
        self._deps(eng, reads, writes)
        self.n_inst += n
        if fresh:
            sems = []
            for _ in range(n):
                sm = self.es.enter_context(self.nc.semaphore("f%d" % len(self.dsems_all)))
                self.dsems_all[id(sm)] = (sm, [16])
                sems.append(sm)
            r = fn(eng.h)
            if not isinstance(r, (list, tuple)):
                r = [r]
            assert len(r) == n
            for ins, sm in zip(r, sems):
                ins.then_inc(sm, 16)
            for i, sm in enumerate(sems):
                if i == 0:
                    self._record(reads, writes, id(sm), sm, 16)
                else:
                    for b in reads:
                        self._merge(b.r, id(sm), sm, 16)
                    for b in writes:
                        self._merge(b.w, id(sm), sm, 16)
            return
        if owner.dsem is None:
            if self.dsem_free:
                owner.dsem = self.dsem_free.pop()
            else:
                owner.dsem = self.es.enter_context(self.nc.semaphore("d%d" % len(self.dsems_all)))
                self.dsems_all[id(owner.dsem)] = (owner.dsem, [0])
        cnt = self.dsems_all[id(owner.dsem)][1]
        cnt[0] += 16 * n
        sem = owner.dsem
        val = cnt[0]
        r = fn(eng.h)
        if not isinstance(r, (list, tuple)):
            r = [r]
        assert len(r) == n
        for ins in r:
            ins.then_inc(sem, 16)
        self._record(reads, writes, id(sem), sem, val)

    def barrier_all_to(self, engname, bufs):
        eng = self.E[engname]
        evs = {}
        for b in bufs:
            for k, (s, v) in b.w.items():
                self._merge(evs, k, s, v)
        self._wait(eng, evs)

    def full_barrier(self):
        evs = {}
        for e in self.E.values():
            if e.count > 0:
                evs[id(e.sem)] = (e.sem, e.count)
        for k, (sem, cnt) in self.dsems_all.items():
            if cnt[0] > 0:
                evs[k] = (sem, cnt[0])
        for e in self.E.values():
            ev2 = {k: v for k, v in evs.items() if v[0] is not e.sem}
            self._wait(e, ev2)

    def scope(self):
        return _Scope(self)

    def finish(self):
        self.es.close()
        return self.nc


class _Scope:
    def __init__(self, P):
        self.P = P

    def __enter__(self):
        self.P.stack.append(ExitStack())
        self.P.scope_bufs.append([])
        return self

    def __exit__(self, *a):
        P = self.P
        P.full_barrier()
        for b in P.scope_bufs.pop():
            if b.dsem is not None:
                P.dsem_free.append(b.dsem)
                b.dsem = None
        P.stack.pop().close()
        return False


D = 2048
FC = 16
EPS = 1e-6


def chunk_cols(w, cols):
    K = w.shape[0]
    sub = w[:, cols]
    return np.ascontiguousarray(sub.reshape(K // 128, 128, len(cols)).transpose(1, 0, 2))


def prep_inputs(inp, core, TR):
    b, j = core // 2, core % 2
    NB = TR // 128
    NOWN = NB // 2
    ar = np.arange
    d = {}
    d["x"] = np.ascontiguousarray(inp["x"][b, :TR])
    d["g0"] = np.ascontiguousarray(inp["norm_gains"][0].reshape(FC, 128).T)
    d["g1"] = np.ascontiguousarray(inp["norm_gains"][1].reshape(FC, 128).T)
    d["gf"] = np.ascontiguousarray(inp["final_norm_gain"].reshape(1, D))
    w = inp["l0_w_in"]
    ch = []
    for n in range(8):
        ch.append(ar(n * 128, (n + 1) * 128))
    for n in range(8):
        ch.append(1024 + ar(n * 128, (n + 1) * 128))
    for n in range(8):
        ch.append(2048 + ar(n * 128, (n + 1) * 128))
    ch.append(3072 + np.concatenate([ar(64), ar(64)]))
    ch.append(3072 + 64 + np.concatenate([ar(64), ar(64)]))
    for n in range(8):
        ch.append(3328 + ar(n * 128, (n + 1) * 128))
    ch.append(3200 + ar(128))
    d["w0v"] = np.stack([chunk_cols(w, c) for c in ch])
    lv = np.zeros((128, 8, 8), np.float32)
    for n in range(8):
        sl = slice(n * 128, (n + 1) * 128)
        lv[:, n, 0:4] = inp["l0_conv_w"][:, sl].T
        lv[:, n, 4] = inp["l0_conv_b"][sl]
        lv[:, n, 5] = inp["l0_gate_x_b"][n]
        lv[:, n, 6] = inp["l0_gate_a_b"][n]
        lv[:, n, 7] = inp["l0_lru_lambda"][sl]
    d["lruvec"] = lv
    d["gxw"] = np.ascontiguousarray(inp["l0_gate_x_w"])
    d["gaw"] = np.ascontiguousarray(inp["l0_gate_a_w"])
    d["sinks"] = np.ascontiguousarray(inp["l0_sinks"].reshape(1, 16))
    d["wo0"] = chunk_cols(inp["l0_w_out"], ar(2048))
    w1 = inp["l1_w_in"]
    o_cq, o_ckv, o_kr, o_mg, o_dq, o_dk, o_dv, o_dg = 0, 768, 1280, 1344, 2368, 3392, 4416, 5440
    chq = [o_cq + ar(n * 128, (n + 1) * 128) for n in range(6)]
    chq += [o_mg + ar(n * 128, (n + 1) * 128) for n in range(8)]
    chq += [o_dq + ar(n * 128, (n + 1) * 128) for n in range(8)]
    chq += [o_dg + ar(n * 128, (n + 1) * 128) for n in range(8)]
    d["w1q"] = np.stack([chunk_cols(w1, c) for c in chq])
    chk = [o_kr + np.concatenate([ar(64), ar(32, 64), ar(0, 32)])]
    chk += [o_dk + ar(n * 128, (n + 1) * 128) for n in range(8)]
    chk += [o_ckv + ar(n * 128, (n + 1) * 128) for n in range(4)]
    d["w1k"] = np.stack([chunk_cols(w1, c) for c in chk])
    d["wdv"] = np.stack([chunk_cols(w1, o_dv + ar(g * 512, (g + 1) * 512)) for g in range(2)])
    d["qn"] = np.ascontiguousarray(inp["l1_q_norm"].reshape(6, 128).T)
    d["kvn"] = np.ascontiguousarray(inp["l1_kv_norm"].reshape(4, 128).T)
    hd = np.arange(8)[:, None] * 256 + np.arange(128)[None, :]
    d["wukv_k"] = chunk_cols(inp["l1_w_ukv"], hd.reshape(-1))
    d["wukv_v"] = chunk_cols(inp["l1_w_ukv"], (hd + 128).reshape(-1))
    fr = (10000.0 ** (-np.arange(32, dtype=np.float32) / 32)).astype(np.float32)
    cst = np.zeros((64, 2), np.float32)
    cst[:, 0] = np.concatenate([fr, fr])
    cst[:, 1] = np.concatenate([-np.ones(32), np.ones(32)])
    d["ropec"] = cst
    d["wuq"] = chunk_cols(inp["l1_w_uq"], ar(1536))
    sw = np.concatenate([h * 192 + 128 + np.concatenate([ar(32, 64), ar(0, 32)]) for h in range(8)])
    d["wuqs"] = chunk_cols(inp["l1_w_uq"], sw)
    d["lams"] = np.ascontiguousarray(np.concatenate([inp["l1_lambda_q1"], inp["l1_lambda_k1"],
                                                    inp["l1_lambda_q2"], inp["l1_lambda_k2"]]).reshape(1, 256))
    d["subln"] = np.ascontiguousarray(inp["l1_subln"].reshape(128, 1))
    d["wo1"] = chunk_cols(inp["l1_w_out"], ar(2048))
    jwv = np.zeros((128, 2), np.float32)
    jwv[:, j] = 1.0
    d["jw"] = jwv
    kk, qq = np.meshgrid(ar(128), ar(128), indexing="ij")
    tri = (kk <= qq).astype(np.float32)
    d["maskab"] = np.ascontiguousarray(np.concatenate(
        [tri if j == 0 else np.ones_like(tri), np.zeros_like(tri) if j == 0 else tri], axis=1))
    d["pos_all"] = ar(TR, dtype=np.float32).reshape(1, TR)
    d["pos_own"] = np.concatenate([(2 * i + j) * 128 + ar(128) for i in range(NOWN)]).astype(np.float32).reshape(1, NOWN * 128)
    return d


def rmsnorm_to_T(P, xb, ncol, gain_bc, xn, ident, pst, dst_fn, stats, junk, gain_cols=None, evac_eng="vector"):
    s = stats
    P.op("scalar", lambda h: h.activation(out=junk[:, 0:ncol], in_=xb[:, 0:ncol], func=AF.Square, accum_out=s[0][:]),
         reads=[xb], writes=[junk, s[0]])
    P.op("vector", lambda h: h.tensor_scalar(out=s[1][:], in0=s[0][:], scalar1=1.0 / ncol, scalar2=EPS,
                                             op0=ALU.mult, op1=ALU.add), reads=[s[0]], writes=[s[1]])
    P.op("scalar", lambda h: h.activation(out=s[2][:], in_=s[1][:], func=AF.Sqrt), reads=[s[1]], writes=[s[2]])
    P.op("vector", lambda h: h.reciprocal(out=s[3][:], in_=s[2][:]), reads=[s[2]], writes=[s[3]])
    if gain_bc is None:
        P.op("scalar", lambda h: h.activation(out=xn[:, 0:ncol], in_=xb[:, 0:ncol], func=AF.Copy, scale=s[3][:]),
             reads=[xb, s[3]], writes=[xn])
    else:
        P.op("vector", lambda h: h.scalar_tensor_tensor(out=xn[:, 0:ncol], in0=xb[:, 0:ncol], scalar=s[3][:],
                                                        in1=gain_bc[:, 0:ncol], op0=ALU.mult, op1=ALU.mult),
             reads=[xb, s[3], gain_bc], writes=[xn])
    nk = ncol // 128
    k0 = 0
    pi = 0
    while k0 < nk:
        n = min(8, nk - k0)
        pt = pst[pi % len(pst)]
        pi += 1
        for kk in range(n):
            k = k0 + kk
            P.op("tensor", lambda h, pt=pt, kk=kk, k=k: h.transpose(out=pt[:, kk, :], in_=xn[:, k * 128:(k + 1) * 128],
                                                                    identity=ident[:]),
                 reads=[xn, ident], writes=[pt], signal=(kk == n - 1))
        dst_fn(k0, n, pt)
        k0 += n


def build(TR=4096, stop=99, dbg=None):
    NB = TR // 128
    NT = TR // 512
    NOWN = NB // 2
    TRo = NOWN * 128
    NTo = TRo // 512
    P = Prog()
    import os
    P.fresh_swdge = bool(os.environ.get('FRESH_SWDGE'))
    nc = P.nc
    I = {}

    def din(name, shape, dt=F32):
        I[name] = P.dram(name, shape, dt, kind="ExternalInput")
        return I[name]

    x = din("x", [TR, D]); g0 = din("g0", [128, FC]); g1 = din("g1", [128, FC]); gf = din("gf", [1, D])
    w0v = din("w0v", [35, 128, 16, 128]); lruvec = din("lruvec", [128, 8, 8])
    gxw = din("gxw", [8, 128, 128]); gaw = din("gaw", [8, 128, 128]); sinks = din("sinks", [1, 16])
    wo0 = din("wo0", [128, 16, 2048])
    w1q = din("w1q", [30, 128, 16, 128]); w1k = din("w1k", [13, 128, 16, 128])
    wdv = din("wdv", [2, 128, 16, 512]); qn = din("qn", [128, 6]); kvn = din("kvn", [128, 4])
    wuq = din("wuq", [128, 6, 1536]); wuqs = din("wuqs", [128, 6, 512])
    wukv_k = din("wukv_k", [128, 4, 1024]); wukv_v = din("wukv_v", [128, 4, 1024]); ropec = din("ropec", [64, 2])
    lams = din("lams", [1, 256]); subln = din("subln", [128, 1]); wo1 = din("wo1", [128, 16, 2048])
    jw = din("jw", [128, 2]); maskab = din("maskab", [128, 256])
    pos_all = din("pos_all", [1, TR]); pos_own = din("pos_own", [1, TRo])
    out = P.dram("out", [TRo, D], F32, kind="ExternalOutput")

    dkind = "ExternalOutput" if dbg else "Internal"
    yT0_d = P.dram("yT0_d", [16, 128, TR], BF16, kind="ExternalOutput" if dbg == "yT0" else "Internal")
    x1_d = P.dram("x1_d", [TR, D], F32, kind="ExternalOutput" if dbg == "x1" else "Internal")
    h1T_d = P.dram("h1T_d", [TR, 16, 128], BF16, kind="ExternalOutput" if dbg == "x1" else "Internal")

    ident = P.sb("ident", [128, 128], BF16)
    io = P.sb("io", [128, 128], I32)
    P.op("gpsimd", lambda h: h.iota(io[:], pattern=[[1, 128]], base=0, channel_multiplier=-1), writes=[io])
    P.op("vector", lambda h: h.tensor_single_scalar(out=ident[:], in_=io[:], scalar=0.0, op=ALU.is_equal),
         reads=[io], writes=[ident])
    m2 = P.sb("m2", [128, 256], BF16)
    P.op("vector", lambda h: h.tensor_single_scalar(out=m2[:, 0:128], in_=io[:], scalar=0.0, op=ALU.is_ge),
         reads=[io], writes=[m2])
    P.op("vector", lambda h: h.tensor_single_scalar(out=m2[:, 128:256], in_=io[:], scalar=0.0, op=ALU.is_lt),
         reads=[io], writes=[m2])
    ones_bf = P.sb("ones_bf", [128, 128], BF16)
    P.op("vector", lambda h: h.memset(ones_bf[:], 1.0), writes=[ones_bf])
    stats = [[P.sb(f"st{i}_{q}", [128, 1], F32) for q in range(4)] for i in range(2)]
    cstage = [P.sb(f"cst{i}", [128, 1024], F32) for i in range(3)]
    cst_i = [0]
    cast_eng = ["gpsimd", "vector"]

    def cast_load(dst, dst_ap, src_ap, nel, a=None):
        st = cstage[cst_i[0] % len(cstage)]
        cst_i[0] += 1
        sv = st[:, 0:nel] if a is None else st[:, 0:nel].rearrange("p (a b) -> p a b", a=a)
        P.dma("sync", lambda h: h.dma_start(out=sv, in_=src_ap), st, writes=[st])
        ce = cast_eng[cst_i[0] % len(cast_eng)]
        if ce == "scalar":
            P.op(ce, lambda h: h.activation(out=dst_ap, in_=sv, func=AF.Copy), reads=[st], writes=[dst])
        else:
            P.op(ce, lambda h: h.tensor_copy(out=dst_ap, in_=sv), reads=[st], writes=[dst])

    with P.scope():
        hT = P.sb("hT", [128, FC, TR], BF16)
        with P.scope():
            gt = P.sb("gt", [128, FC], F32)
            P.dma("sync", lambda h: h.dma_start(out=gt[:], in_=g0[:]), gt, writes=[gt])
            xb = [P.sb(f"xb{i}", [128, D], F32) for i in range(2)]
            xn = [P.sb(f"xn{i}", [128, D], BF16) for i in range(2)]
            junk = P.sb("junk", [128, D], BF16)
            pst = [P.ps(f"pst{i}", [128, 8, 128], BF16) for i in range(2)]
            for n in range(NB):
                b = xb[n % 2]
                P.dma("sync", lambda h, b=b, n=n: h.dma_start(out=b[:], in_=x[n * 128:(n + 1) * 128, :]), b, writes=[b])

                def dst(k0, nk, pt, n=n):
                    P.op("vector", lambda h: h.tensor_tensor(
                        out=hT[:, k0:k0 + nk, n * 128:(n + 1) * 128], in0=pt[:, 0:nk, :],
                        in1=gt[:, k0:k0 + nk].unsqueeze(2).to_broadcast([128, nk, 128]), op=ALU.mult),
                        reads=[pt, gt], writes=[hT])
                rmsnorm_to_T(P, b, D, None, xn[n % 2], ident, pst, dst, stats[n % 2], junk)

        wch = [P.sb(f"wch{i}", [128, 16, 128], BF16) for i in range(4)]
        wch_i = [0]
        pref = {}

        def _load(src, c):
            t = wch[wch_i[0] % len(wch)]
            wch_i[0] += 1
            cast_load(t, t[:, 0:8, :], src[c, :, 0:8, :], 1024, a=8)
            cast_load(t, t[:, 8:16, :], src[c, :, 8:16, :], 1024, a=8)
            return t

        def prefetch(src, c):
            if (src.name, c) not in pref:
                pref[(src.name, c)] = _load(src, c)

        def load_chunk(src, c):
            t = pref.pop((src.name, c), None)
            return t if t is not None else _load(src, c)

        pz = [P.ps(f"pz{i}", [128, 512], F32) for i in range(4)]
        pz_i = [0]

        def nextpz():
            t = pz[pz_i[0] % len(pz)]
            pz_i[0] += 1
            return t

        def proj_fm(wt, tt, ps, m0=0, m1=128):
            for k in range(16):
                P.op("tensor", lambda h, k=k: h.matmul(ps[0:m1 - m0, :], lhsT=wt[:, k, m0:m1],
                                                       rhs=hT[:, k, tt * 512:(tt + 1) * 512],
                                                       start=(k == 0), stop=(k == 15)),
                     reads=[wt, hT], writes=[ps], signal=(k == 15))

        if True:
            with P.scope():
                lv = P.sb("lv", [128, 8, 8], F32)
                P.dma("sync", lambda h: h.dma_start(out=lv[:], in_=lruvec[:]), lv, writes=[lv])
                cv = P.sb("cv", [128, 8, 2], F32)
                tmpv = P.sb("tmpv", [128, 8], F32)
                P.op("scalar", lambda h: h.activation(out=tmpv[:], in_=lv[:, :, 7], func=AF.Exp, scale=-1.0),
                     reads=[lv], writes=[tmpv])
                P.op("vector", lambda h: h.tensor_scalar_add(out=tmpv[:], in0=tmpv[:], scalar1=1.0), reads=[tmpv], writes=[tmpv])
                P.op("scalar", lambda h: h.activation(out=tmpv[:], in_=tmpv[:], func=AF.Ln), reads=[tmpv], writes=[tmpv])
                P.op("vector", lambda h: h.tensor_scalar_mul(out=cv[:, :, 0], in0=tmpv[:], scalar1=-8.0), reads=[tmpv], writes=[cv])
                P.op("vector", lambda h: h.tensor_scalar_mul(out=cv[:, :, 1], in0=tmpv[:], scalar1=-16.0), reads=[tmpv], writes=[cv])
                gw = [[P.sb(f"gw{i}_{q}", [128, 128], BF16) for q in range(2)] for i in range(2)]
                xbuf = [P.sb(f"xbuf{i}", [128, 515], F32) for i in range(2)]
                hs = [P.sb(f"hs{i}", [128, 512], F32) for i in range(2)]
                ya = [P.sb(f"ya{i}", [128, 512], BF16) for i in range(2)]

                def T(name, dt=F32):
                    return [P.sb(f"{name}{i}", [128, 512], dt) for i in range(2)]
                xc, xcb, gi, gr, av, a2, uu, sg = T("xc"), T("xcb", BF16), T("gi"), T("gr"), T("av"), T("a2"), T("uu"), T("sg")
                zxb = [nextpz(), nextpz()]
                zgb = [nextpz(), nextpz()]
                pgi, pgr = P.ps("pgi_x", [128, 512], F32), P.ps("pgr_x", [128, 512], F32)
                tiles = [(n, tt) for n in range(8) for tt in range(NT)]
                wcur = {}

                def stA(i):
                    n, tt = tiles[i]
                    q = i % 2
                    if tt == 0:
                        wcur["wx"] = load_chunk(w0v, n)
                        wcur["wg"] = load_chunk(w0v, 8 + n)
                        if n == 0:
                            cast_load(gw[0][0], gw[0][0][:], gxw[0], 128)
                            cast_load(gw[0][1], gw[0][1][:], gaw[0], 128)
                    if tt == 1 and n + 1 < 8:
                        prefetch(w0v, n + 1)
                        prefetch(w0v, 8 + n + 1)
                        gwx = gw[(n + 1) % 2]
                        cast_load(gwx[0], gwx[0][:], gxw[n + 1], 128)
                        cast_load(gwx[1], gwx[1][:], gaw[n + 1], 128)
                    zx, zg = zxb[q], zgb[q]
                    proj_fm(wcur["wx"], tt, zx)
                    proj_fm(wcur["wg"], tt, zg)
                    xbq, xbp = xbuf[q], xbuf[1 - q]
                    P.op("scalar", lambda h: h.activation(out=xbq[:, 3:515], in_=zx[:], func=AF.Copy),
                         reads=[zx], writes=[xbq])
                    if tt == 0:
                        P.op("gpsimd", lambda h: h.memset(xbq[:, 0:3], 0.0), writes=[xbq])
                    else:
                        P.op("gpsimd", lambda h: h.tensor_copy(out=xbq[:, 0:3], in_=xbp[:, 512:515]),
                             reads=[xbp], writes=[xbq])
                    xcq = xc[q]
                    P.op("vector", lambda h: h.tensor_scalar(out=xcq[:], in0=xbq[:, 3:515], scalar1=lv[:, n, 3:4],
                                                             scalar2=lv[:, n, 4:5], op0=ALU.mult, op1=ALU.add),
                         reads=[xbq, lv], writes=[xcq])
                    for k in range(3):
                        P.op("vector", lambda h, k=k: h.scalar_tensor_tensor(
                            out=xcq[:], in0=xbq[:, k:k + 512], scalar=lv[:, n, k:k + 1], in1=xcq[:],
                            op0=ALU.mult, op1=ALU.add), reads=[xbq, lv, xcq], writes=[xcq])
                    xcbq = xcb[q]
                    P.op("scalar", lambda h: h.activation(out=xcbq[:], in_=xcq[:], func=AF.Copy), reads=[xcq], writes=[xcbq])
                    sgq = sg[q]
                    P.op("scalar", lambda h: h.activation(out=sgq[:], in_=zg[:], func=AF.Silu), reads=[zg], writes=[sgq])

                def stB(i):
                    n, tt = tiles[i]
                    q = i % 2
                    gwn = gw[n % 2]
                    xcq, xcbq = xc[q], xcb[q]
                    P.op("tensor", lambda h: h.matmul(pgi[:], lhsT=gwn[0][:], rhs=xcbq[:], start=True, stop=True),
                         reads=[gwn[0], xcbq], writes=[pgi])
                    P.op("tensor", lambda h: h.matmul(pgr[:], lhsT=gwn[1][:], rhs=xcbq[:], start=True, stop=True),
                         reads=[gwn[1], xcbq], writes=[pgr])
                    giq, grq, avq, a2q, uq, sgq = gi[q], gr[q], av[q], a2[q], uu[q], sg[q]
                    P.op("scalar", lambda h: h.activation(out=giq[:], in_=pgi[:], func=AF.Sigmoid, bias=lv[:, n, 5:6]),
                         reads=[pgi, lv], writes=[giq])
                    P.op("scalar", lambda h: h.activation(out=grq[:], in_=pgr[:], func=AF.Sigmoid, bias=lv[:, n, 6:7]),
                         reads=[pgr, lv], writes=[grq])
                    P.op("scalar", lambda h: h.activation(out=avq[:], in_=grq[:], func=AF.Exp, scale=cv[:, n, 0:1]),
                         reads=[grq, cv], writes=[avq])
                    P.op("scalar", lambda h: h.activation(out=a2q[:], in_=grq[:], func=AF.Exp, scale=cv[:, n, 1:2]),
                         reads=[grq, cv], writes=[a2q])
                    P.op("gpsimd", lambda h: h.tensor_scalar(out=a2q[:], in0=a2q[:], scalar1=-1.0, scalar2=1.0,
                                                             op0=ALU.mult, op1=ALU.add), reads=[a2q], writes=[a2q])
                    P.op("scalar", lambda h: h.activation(out=a2q[:], in_=a2q[:], func=AF.Sqrt), reads=[a2q], writes=[a2q])
                    P.op("gpsimd", lambda h: h.tensor_tensor(out=uq[:], in0=a2q[:], in1=giq[:], op=ALU.mult),
                         reads=[a2q, giq], writes=[uq])
                    P.op("vector", lambda h: h.tensor_tensor(out=uq[:], in0=uq[:], in1=xcq[:], op=ALU.mult),
                         reads=[uq, xcq], writes=[uq])
                    hq, hp = hs[q], hs[1 - q]
                    if tt == 0:
                        P.op("vector", lambda h: h.tensor_tensor_scan(out=hq[:], data0=avq[:], data1=uq[:], initial=0.0,
                                                                      op0=ALU.mult, op1=ALU.add),
                             reads=[avq, uq], writes=[hq])
                    else:
                        P.op("vector", lambda h: h.tensor_tensor_scan(out=hq[:], data0=avq[:], data1=uq[:],
                                                                      initial=hp[:, 511:512], op0=ALU.mult, op1=ALU.add),
                             reads=[avq, uq, hp], writes=[hq])
                    yan = ya[q]
                    P.op("gpsimd", lambda h: h.tensor_tensor(out=yan[:], in0=hq[:], in1=sgq[:],
                                                             op=ALU.mult), reads=[hq, sgq], writes=[yan])
                    P.dma("sync", lambda h: h.dma_start(out=yT0_d[n, :, tt * 512:(tt + 1) * 512], in_=yan[:]), yan,
                          reads=[yan], writes=[yT0_d])
                stA(0)
                for i in range(len(tiles)):
                    if i + 1 < len(tiles):
                        stA(i + 1)
                    stB(i)

        if stop >= 2:
            with P.scope():
                kd = [P.sb(f"kd{i}", [128, TR], BF16) for i in range(2)]
                vx = P.sb("vx", [128, NB, 2, 65], BF16)
                P.op("gpsimd", lambda h: h.memset(vx[:], 1.0), writes=[vx])
                sk = P.sb("sk", [128, 16], F32)
                P.dma("sync", lambda h: h.dma_start(out=sk[:], in_=sinks[:].partition_broadcast(128)), sk, writes=[sk])
                P.op("scalar", lambda h: h.activation(out=sk[:], in_=sk[:], func=AF.Exp), reads=[sk], writes=[sk])
                for i in range(2):
                    wk = load_chunk(w0v, 24 + i)
                    for tt in range(NT):
                        ps = nextpz()
                        proj_fm(wk, tt, ps)
                        P.op("scalar", lambda h: h.activation(out=kd[i][:, tt * 512:(tt + 1) * 512], in_=ps[:], func=AF.Copy),
                             reads=[ps], writes=[kd[i]])
                wv = load_chunk(w0v, 34)
                for n in range(NB):
                    ps = nextpz()
                    for k in range(16):
                        P.op("tensor", lambda h, k=k: h.matmul(ps[:, 0:128], lhsT=hT[:, k, n * 128:(n + 1) * 128], rhs=wv[:, k, :],
                                                               start=(k == 0), stop=(k == 15)),
                             reads=[hT, wv], writes=[ps], signal=(k == 15))
                    P.op("vector", lambda h: h.tensor_copy(out=vx[:, n, :, 0:64], in_=ps[:, 0:128].rearrange("p (a b) -> p a b", a=2)),
                         reads=[ps], writes=[vx])
                qT = [P.sb(f"qT{i}", [128, TR], BF16) for i in range(1)]
                sgF = P.sb("sgF", [128, TR], BF16)
                yb = [P.sb(f"yb{i}", [128, 512], BF16) for i in range(2)]
                ptr = [P.ps(f"ptr{i}", [128, 128], BF16) for i in range(2)]
                PT = [[P.sb(f"PT{i}_{q}", [128, 256], BF16) for q in range(2)] for i in range(4)]
                den = [P.sb(f"den{i}", [128, 2], F32) for i in range(2)]
                ybt = [P.sb(f"ybt{i}", [128, 2, 64], BF16) for i in range(3)]
                pso = [P.ps(f"pso{i}", [128, 2, 65], F32) for i in range(2)]
                for c in range(8):
                    wq = load_chunk(w0v, 16 + c)
                    wg = load_chunk(w0v, 26 + c)
                    qTc = qT[0]
                    for tt in range(NT):
                        ps = nextpz()
                        proj_fm(wq, tt, ps)
                        P.op("scalar", lambda h: h.activation(out=qTc[:, tt * 512:(tt + 1) * 512], in_=ps[:], func=AF.Copy),
                             reads=[ps], writes=[qTc])
                    for tt in range(NT):
                        ps2 = nextpz()
                        proj_fm(wg, tt, ps2)
                        P.op("scalar", lambda h: h.activation(out=sgF[:, tt * 512:(tt + 1) * 512], in_=ps2[:], func=AF.Silu),
                             reads=[ps2], writes=[sgF])
                    if c + 1 < 8:
                        prefetch(w0v, 16 + c + 1)
                        prefetch(w0v, 26 + c + 1)
                    K = kd[c // 4]
                    kvh = c // 4

                    def stA(m):
                        ncol = 256 if m < NB - 1 else 128
                        PTm = PT[m % 4]
                        pss_ = [nextpz(), nextpz()]
                        for hh in range(2):
                            P.op("tensor", lambda h: h.matmul(pss_[hh][:, 0:ncol], lhsT=K[64 * hh:64 * hh + 64, m * 128:(m + 1) * 128],
                                                              rhs=qTc[64 * hh:64 * hh + 64, m * 128:m * 128 + ncol],
                                                              start=True, stop=True), reads=[K, qTc], writes=[pss_[hh]])
                        for hh in range(2):
                            P.op("scalar", lambda h: h.activation(out=PTm[hh][:, 0:ncol], in_=pss_[hh][:, 0:ncol], func=AF.Exp, scale=0.125),
                                 reads=[pss_[hh]], writes=[PTm[hh]])
                            P.op("gpsimd", lambda h: h.tensor_tensor(out=PTm[hh][:, 0:ncol], in0=PTm[hh][:, 0:ncol],
                                                                     in1=m2[:, 0:ncol], op=ALU.mult),
                                 reads=[PTm[hh], m2], writes=[PTm[hh]])

                    def stB(m):
                        PTm = PT[m % 4]
                        PTp = PT[(m - 1) % 4]
                        po = pso[m % 2]
                        for hh in range(2):
                            if m > 0:
                                P.op("tensor", lambda h: h.matmul(po[:, hh, :], lhsT=PTp[hh][:, 128:256], rhs=vx[:, m - 1, kvh, :],
                                                                  start=True, stop=False), reads=[PTp[hh], vx], writes=[po], signal=False)
                            P.op("tensor", lambda h: h.matmul(po[:, hh, :], lhsT=PTm[hh][:, 0:128], rhs=vx[:, m, kvh, :],
                                                              start=(m == 0), stop=True), reads=[PTm[hh], vx], writes=[po])
                        dn = den[m % 2]
                        ybm = ybt[m % 3]
                        P.op("vector", lambda h: h.tensor_tensor(out=dn[:], in0=po[:, :, 64], in1=sk[:, 2 * c:2 * c + 2], op=ALU.add),
                             reads=[po, sk], writes=[dn])
                        P.op("vector", lambda h: h.reciprocal(out=dn[:], in_=dn[:]), reads=[dn], writes=[dn])
                        P.op("vector", lambda h: h.tensor_tensor(out=ybm[:], in0=po[:, :, 0:64],
                                                                 in1=dn[:].unsqueeze(2).to_broadcast([128, 2, 64]), op=ALU.mult),
                             reads=[po, dn], writes=[ybm])

                    def stC(m):
                        ybm = ybt[m % 3]
                        tt = m // 4
                        ybc = yb[tt % 2]
                        pt = ptr[m % 2]
                        mm = m % 4
                        P.op("tensor", lambda h: h.transpose(out=pt[:], in_=ybm[:].rearrange("p a b -> p (a b)"), identity=ident[:]),
                             reads=[ybm, ident], writes=[pt])
                        P.op("vector", lambda h: h.tensor_tensor(out=ybc[:, mm * 128:(mm + 1) * 128], in0=pt[:],
                                                                 in1=sgF[:, m * 128:(m + 1) * 128], op=ALU.mult),
                             reads=[pt, sgF], writes=[ybc])
                        if mm == 3:
                            P.dma("sync", lambda h: h.dma_start(out=yT0_d[8 + c, :, (m - 3) * 128:(m + 1) * 128], in_=ybc[:]), ybc,
                                  reads=[ybc], writes=[yT0_d])
                    for st in range(NB + 2):
                        if st < NB:
                            stA(st)
                        if 1 <= st <= NB:
                            stB(st - 1)
                        if st >= 2:
                            stC(st - 2)

    if stop <= 2:
        P.full_barrier()
        return P.finish(), I

    def outproj_phase(wo_d, yT_tile_fn, ntile, resid_fn, combine_fn, emit_fn, need_xn=True):
        with P.scope():
            wo = P.sb("wo", [128, 16, 2048], BF16)
            cast_eng[:] = ["vector", "scalar", "gpsimd", "vector", "scalar"]
            for k in range(16):
                for hf in range(2):
                    cast_load(wo, wo[:, k, hf * 1024:(hf + 1) * 1024], wo_d[:, k, hf * 1024:(hf + 1) * 1024], 1024)
            cast_eng[:] = ["gpsimd", "vector"]
            pb = [P.ps(f"pb{i}", [128, 512], F32) for i in range(4)]
            pst = [P.ps(f"pstB{i}", [128, 8, 128], BF16) for i in range(2)] if need_xn else None
            x1 = [P.sb(f"x1{i}", [128, D], F32) for i in range(2)]
            xn = [P.sb(f"xnB{i}", [128, D], BF16) for i in range(2)] if need_xn else [None, None]
            junk = P.sb("junkB", [128, D], BF16)
            hc = 0
            for t4 in range(ntile):
                yt = yT_tile_fn(t4)
                for bi in range(4):
                    n = t4 * 4 + bi
                    rh = resid_fn(n)
                    x1n = x1[n % 2]
                    for half in range(2):
                        banks = pb[0:2] if hc % 2 == 0 else pb[2:4]
                        hc += 1
                        for k in range(16):
                            for g2 in range(2):
                                g = half * 2 + g2
                                P.op("tensor", lambda h, k=k, g=g, g2=g2: h.matmul(banks[g2][:], lhsT=yt[:, k, bi * 128:(bi + 1) * 128],
                                                                                  rhs=wo[:, k, g * 512:(g + 1) * 512],
                                                                                  start=(k == 0), stop=(k == 15)),
                                     reads=[yt, wo], writes=[banks[g2]], signal=(k == 15))
                        for g2 in range(2):
                            combine_fn(half * 2 + g2, banks[g2], x1n, rh)
                    emit_fn(n, x1n, xn[n % 2], junk, pst)

    with P.scope():
        ytile = [P.sb(f"ytile{i}", [128, 16, 512], BF16) for i in range(2)]
        g1t = P.sb("g1t", [128, FC], F32)
        P.dma("sync", lambda h: h.dma_start(out=g1t[:], in_=g1[:]), g1t, writes=[g1t])
        hst = [P.sb(f"hst{i}", [128, 16, 128], BF16) for i in range(2)]

        def ytf(t4):
            yt = ytile[t4 % 2]
            P.dma("sync", lambda h: h.dma_start(out=yt[:], in_=yT0_d[:, :, t4 * 512:(t4 + 1) * 512].rearrange("c p t -> p c t")),
                  yt, reads=[yT0_d], writes=[yt])
            return yt

        xrB = [P.sb(f"xrB{i}", [128, D], F32) for i in range(2)]

        def resid(n):
            xrn = xrB[n % 2]
            P.dma("sync", lambda h: h.dma_start(out=xrn[:], in_=x[n * 128:(n + 1) * 128, :]), xrn, writes=[xrn])
            return xrn

        def combine(g, pbank, x1n, xrn):
            P.op("vector", lambda h: h.tensor_tensor(out=x1n[:, g * 512:(g + 1) * 512], in0=pbank[:],
                                                     in1=xrn[:, g * 512:(g + 1) * 512], op=ALU.add),
                 reads=[pbank, xrn], writes=[x1n])

        def emit(n, x1n, xnn, junk, pst):
            P.dma("sync", lambda h: h.dma_start(out=x1_d[n * 128:(n + 1) * 128, :], in_=x1n[:]), x1n, reads=[x1n], writes=[x1_d])
            hs_ = hst[n % 2]

            def dst(k0, nk, pt):
                P.op("vector", lambda h: h.tensor_tensor(
                    out=hs_[:, k0:k0 + nk, :], in0=pt[:, 0:nk, :],
                    in1=g1t[:, k0:k0 + nk].unsqueeze(2).to_broadcast([128, nk, 128]), op=ALU.mult),
                    reads=[pt, g1t], writes=[hs_])
            rmsnorm_to_T(P, x1n, D, None, xnn, ident, pst, dst, stats[n % 2], junk)
            P.dma("sync", lambda h: h.dma_start(out=h1T_d[n * 128:(n + 1) * 128, :, :], in_=hs_[:]),
                  hs_, reads=[hs_], writes=[h1T_d])
        outproj_phase(wo0, ytf, NT, resid, combine, emit)

    if stop <= 3:
        P.full_barrier()
        return P.finish(), I

    ckvnT_d = P.dram("ckvnT_d", [4, 128, TR], BF16, kind=dkind)
    krT_d = P.dram("krT_d", [64, TR], BF16, kind=dkind)
    dkT_d = P.dram("dkT_d", [8, 128, TR], BF16, kind=dkind)
    dv_d = P.dram("dv_d", [8, 128, NB, 128], BF16, kind=dkind)
    KT_d = P.dram("KT_d", [8, 128, TR], BF16, kind=dkind)
    V_d = P.dram("V_d", [8, 128, NB, 128], BF16, kind=dkind)
    qT_d = P.dram("qT_d", [8, 128, TRo], BF16, kind=dkind)
    qrT_d = P.dram("qrT_d", [8, 64, TRo], BF16, kind=dkind)
    mg_d = P.dram("mg_d", [8, 128, TRo], BF16, kind=dkind)
    dq_d = P.dram("dq_d", [8, 128, TRo], BF16, kind=dkind)
    dg_d = P.dram("dg_d", [8, 128, TRo], BF16, kind=dkind)

    ones_f = P.sb("ones_f", [128, 128], F32)
    P.op("vector", lambda h: h.memset(ones_f[:], 1.0), writes=[ones_f])
    rc = P.sb("rc", [64, 2], F32)
    P.dma("sync", lambda h: h.dma_start(out=rc[:], in_=ropec[:]), rc, writes=[rc])
    TWO_PI = 6.283185307179586
    MAGIC = 12582912.0

    def rope_tables(pos_d, t0, pos_t, ang, kf, cosT, sinT):
        P.dma("sync", lambda h: h.dma_start(out=pos_t[:], in_=pos_d[0:1, t0:t0 + 512].partition_broadcast(64)), pos_t, writes=[pos_t])
        for which, dst in ((0, sinT), (1, cosT)):
            P.op("vector", lambda h: h.tensor_scalar(out=ang[:], in0=pos_t[:], scalar1=rc[:, 0:1],
                                                     scalar2=(1.5707963267948966 if which else 0.0),
                                                     op0=ALU.mult, op1=ALU.add), reads=[pos_t, rc], writes=[ang])
            P.op("vector", lambda h: h.tensor_scalar(out=kf[:], in0=ang[:], scalar1=1.0 / TWO_PI, scalar2=MAGIC,
                                                     op0=ALU.mult, op1=ALU.add), reads=[ang], writes=[kf])
            P.op("vector", lambda h: h.tensor_scalar_add(out=kf[:], in0=kf[:], scalar1=-MAGIC), reads=[kf], writes=[kf])
            P.op("vector", lambda h: h.scalar_tensor_tensor(out=ang[:], in0=kf[:], scalar=-TWO_PI, in1=ang[:],
                                                            op0=ALU.mult, op1=ALU.add), reads=[kf, ang], writes=[ang])
            P.op("vector", lambda h: h.tensor_scalar(out=ang[:], in0=ang[:], scalar1=3.14159, scalar2=-3.14159,
                                                     op0=ALU.min, op1=ALU.max), reads=[ang], writes=[ang])
            P.op("scalar", lambda h: h.activation(out=dst[:], in_=ang[:], func=AF.Sin), reads=[ang], writes=[dst])
        P.op("vector", lambda h: h.tensor_scalar_mul(out=sinT[:], in0=sinT[:], scalar1=rc[:, 1:2]), reads=[sinT, rc], writes=[sinT])

    def rope_apply(pa, pb_, cosT, sinT, t1, t2, dst):
        P.op("vector", lambda h: h.tensor_tensor(out=t1[:], in0=pa[0:64, :], in1=cosT[:], op=ALU.mult), reads=[pa, cosT], writes=[t1])
        P.op("vector", lambda h: h.tensor_tensor(out=t2[:], in0=pb_[0:64, :], in1=sinT[:], op=ALU.mult), reads=[pb_, sinT], writes=[t2])
        P.op("gpsimd", lambda h: h.tensor_tensor(out=dst[0:64, :], in0=t1[:], in1=t2[:], op=ALU.add), reads=[t1, t2], writes=[dst])

    def make_loader(nslots):
        wch = [P.sb(f"wc{P.n_inst}_{i}", [128, 16, 128], BF16) for i in range(nslots)]
        ctr = [0]

        pref = {}

        def _load(src, c):
            t = wch[ctr[0] % len(wch)]
            ctr[0] += 1
            cast_load(t, t[:, 0:8, :], src[c, :, 0:8, :], 1024, a=8)
            cast_load(t, t[:, 8:16, :], src[c, :, 8:16, :], 1024, a=8)
            return t

        def load_chunk(src, c, pre=None):
            t = pref.pop((src.name, c), None)
            if t is None:
                t = _load(src, c)
            if pre is not None and (src.name, pre) not in pref:
                pref[(src.name, pre)] = _load(src, pre)
            return t
        return load_chunk

    def make_pz(n):
        pz = [P.ps(f"pq{P.n_inst}_{i}", [128, 512], F32) for i in range(n)]
        ctr = [0]

        def nxt():
            t = pz[ctr[0] % len(pz)]
            ctr[0] += 1
            return t
        return nxt

    def proj(wt, src, c0, ps, m0=0, m1=128, nk=16):
        for k in range(nk):
            P.op("tensor", lambda h, k=k: h.matmul(ps[0:m1 - m0, :], lhsT=wt[:, k, m0:m1], rhs=src[:, k, c0:c0 + 512],
                                                   start=(k == 0), stop=(k == nk - 1)),
                 reads=[wt, src], writes=[ps], signal=(k == nk - 1))

    norm_pss = [None]

    def norm_fm(src, nch, nxt, wts, gcol, ncols_total, cf, sq, rst, dst_fn, tcol, post_fn=None):
        pss = norm_pss[0]
        for c in range(nch):
            ps = nxt()
            proj(wts[c], src, tcol, ps)
            P.op("scalar", lambda h: h.activation(out=cf[:, c, :], in_=ps[:], func=AF.Copy), reads=[ps], writes=[cf])
            sqc = sq[c % 2]
            P.op("vector", lambda h: h.tensor_tensor(out=sqc[:], in0=cf[:, c, :], in1=cf[:, c, :], op=ALU.mult), reads=[cf], writes=[sqc])
            P.op("tensor", lambda h: h.matmul(pss[:], lhsT=ones_f[:], rhs=sqc[:], start=(c == 0), stop=(c == nch - 1)),
                 reads=[ones_f, sqc], writes=[pss], signal=(c == nch - 1))
        P.op("vector", lambda h: h.tensor_scalar(out=rst[:], in0=pss[:], scalar1=1.0 / ncols_total, scalar2=EPS,
                                                 op0=ALU.mult, op1=ALU.add), reads=[pss], writes=[rst])
        P.op("scalar", lambda h: h.activation(out=rst[:], in_=rst[:], func=AF.Ln), reads=[rst], writes=[rst])
        P.op("scalar", lambda h: h.activation(out=rst[:], in_=rst[:], func=AF.Exp, scale=-0.5), reads=[rst], writes=[rst])
        for c in range(nch):
            o_ap, ob = dst_fn(c)
            P.op("vector", lambda h: h.scalar_tensor_tensor(out=o_ap, in0=cf[:, c, :], scalar=gcol[:, c:c + 1], in1=rst[:],
                                                            op0=ALU.mult, op1=ALU.mult), reads=[cf, gcol, rst], writes=[ob])
            if post_fn is not None:
                post_fn(c, ob)

    with P.scope():
        h1T = P.sb("h1T", [128, 16, TR], BF16)
        for n in range(NB):
            P.dma("sync", lambda h: h.dma_start(out=h1T[:, :, n * 128:(n + 1) * 128], in_=h1T_d[n * 128:(n + 1) * 128, :, :]),
                  h1T, reads=[h1T_d], writes=[h1T])
        nxt = make_pz(5)
        norm_pss[0] = P.ps("npss1", [128, 512], F32)
        stg = [P.sb(f"stg{i}", [128, 512], BF16) for i in range(3)]
        stg_i = [0]

        def stage():
            t = stg[stg_i[0] % 3]
            stg_i[0] += 1
            return t
        with P.scope():
            ld = make_loader(4)
            wts = [ld(w1k, 9 + c) for c in range(4)]
            kvg = P.sb("kvg", [128, 4], F32)
            P.dma("sync", lambda h: h.dma_start(out=kvg[:], in_=kvn[:]), kvg, writes=[kvg])
            cf = P.sb("cf", [128, 4, 512], F32)
            sq = [P.sb(f"sq{i}", [128, 512], F32) for i in range(2)]
            rst = P.sb("rst", [128, 512], F32)
            for tt in range(NT):
                def dst(c):
                    t = stage()
                    return t[:], t

                def post(c, t):
                    P.dma("sync", lambda h: h.dma_start(out=ckvnT_d[c, :, tt * 512:(tt + 1) * 512], in_=t[:]), t, reads=[t], writes=[ckvnT_d])
                norm_fm(h1T, 4, nxt, wts, kvg, 512.0, cf, sq, rst, dst, tt * 512, post)
        with P.scope():
            ld = make_loader(3)
            wkr = ld(w1k, 0, pre=1)
            pos_t = P.sb("pos_t", [64, 512], F32); ang = P.sb("ang", [64, 512], F32); kf = P.sb("kf", [64, 512], F32)
            cosT = P.sb("cosT", [64, 512], F32); sinT = P.sb("sinT", [64, 512], F32)
            t1 = P.sb("t1", [64, 512], F32); t2 = P.sb("t2", [64, 512], F32)
            for tt in range(NT):
                rope_tables(pos_all, tt * 512, pos_t, ang, kf, cosT, sinT)
                pa, pb_ = nxt(), nxt()
                proj(wkr, h1T, tt * 512, pa, 0, 64)
                proj(wkr, h1T, tt * 512, pb_, 64, 128)
                t = stage()
                rope_apply(pa, pb_, cosT, sinT, t1, t2, t)
                P.dma("sync", lambda h: h.dma_start(out=krT_d[:, tt * 512:(tt + 1) * 512], in_=t[0:64, :]), t, reads=[t], writes=[krT_d])
            for hh in range(8):
                wt = ld(w1k, 1 + hh, pre=(2 + hh if hh < 7 else None))
                for tt in range(NT):
                    ps = nxt()
                    proj(wt, h1T, tt * 512, ps)
                    t = stage()
                    P.op("scalar", lambda h: h.activation(out=t[:], in_=ps[:], func=AF.Copy), reads=[ps], writes=[t])
                    P.dma("scalar", lambda h: h.dma_start(out=dkT_d[hh, :, tt * 512:(tt + 1) * 512], in_=t[:]), t, reads=[t], writes=[dkT_d])
        with P.scope():
            wdvt = P.sb("wdvt", [128, 16, 512], BF16)
            for g in range(2):
                for k2 in range(8):
                    cast_load(wdvt, wdvt[:, k2 * 2:(k2 + 1) * 2, :], wdv[g, :, k2 * 2:(k2 + 1) * 2, :], 1024, a=2)
                for n in range(NB):
                    ps = nxt()
                    for k in range(16):
                        P.op("tensor", lambda h, k=k: h.matmul(ps[:], lhsT=h1T[:, k, n * 128:(n + 1) * 128], rhs=wdvt[:, k, :],
                                                               start=(k == 0), stop=(k == 15)), reads=[h1T, wdvt], writes=[ps], signal=(k == 15))
                    t = stage()
                    P.op("scalar", lambda h: h.activation(out=t[:], in_=ps[:], func=AF.Copy), reads=[ps], writes=[t])
                    P.dma("sync", lambda h: h.dma_start(out=dv_d[g * 4:(g + 1) * 4, :, n, :].rearrange("h p d -> p h d"),
                                                        in_=t[:].rearrange("p (h d) -> p h d", h=4)), t, reads=[t], writes=[dv_d])

    with P.scope():
        ckT = P.sb("ckT", [128, 4, TR], BF16)
        P.dma("sync", lambda h: h.dma_start(out=ckT[:], in_=ckvnT_d[:].rearrange("c p t -> p c t")), ckT, reads=[ckvnT_d], writes=[ckT])
        wk_ = P.sb("wk_", [128, 4, 1024], BF16); wv_ = P.sb("wv_", [128, 4, 1024], BF16)
        for kc in range(4):
            cast_load(wk_, wk_[:, kc, :], wukv_k[:, kc, :], 1024)
            cast_load(wv_, wv_[:, kc, :], wukv_v[:, kc, :], 1024)
        nxt = make_pz(4)
        stg = [P.sb(f"stgc{i}", [128, 512], BF16) for i in range(3)]
        si = 0
        for hh in range(8):
            for tt in range(NT):
                ps = nxt()
                proj(wk_, ckT, tt * 512, ps, hh * 128, (hh + 1) * 128, nk=4)
                t = stg[si % 3]; si += 1
                P.op("scalar", lambda h: h.activation(out=t[:], in_=ps[:], func=AF.Copy), reads=[ps], writes=[t])
                P.dma("scalar", lambda h: h.dma_start(out=KT_d[hh, :, tt * 512:(tt + 1) * 512], in_=t[:]), t, reads=[t], writes=[KT_d])
        for g in range(2):
            for n in range(NB):
                ps = nxt()
                for k in range(4):
                    P.op("tensor", lambda h, k=k: h.matmul(ps[:], lhsT=ckT[:, k, n * 128:(n + 1) * 128], rhs=wv_[:, k, g * 512:(g + 1) * 512],
                                                           start=(k == 0), stop=(k == 3)), reads=[ckT, wv_], writes=[ps], signal=(k == 3))
                t = stg[si % 3]; si += 1
                P.op("scalar", lambda h: h.activation(out=t[:], in_=ps[:], func=AF.Copy), reads=[ps], writes=[t])
                P.dma("sync", lambda h: h.dma_start(out=V_d[g * 4:(g + 1) * 4, :, n, :].rearrange("h p d -> p h d"),
                                                    in_=t[:].rearrange("p (h d) -> p h d", h=4)), t, reads=[t], writes=[V_d])

    jwt = P.sb("jwt", [128, 2], F32)
    P.dma("sync", lambda h: h.dma_start(out=jwt[:], in_=jw[:]), jwt, writes=[jwt])
    with P.scope():
        h1o = P.sb("h1o", [128, 16, TRo], BF16)
        with P.scope():
            ga = [P.sb(f"ga{i}", [128, 16, 128], BF16) for i in range(2)]
            gb = [P.sb(f"gb{i}", [128, 16, 128], BF16) for i in range(2)]
            for i in range(NOWN):
                a_, b_ = ga[i % 2], gb[i % 2]
                P.dma("sync", lambda h: h.dma_start(out=a_[:], in_=h1T_d[(2 * i) * 128:(2 * i + 1) * 128, :, :]), a_, reads=[h1T_d], writes=[a_])
                P.dma("sync", lambda h: h.dma_start(out=b_[:], in_=h1T_d[(2 * i + 1) * 128:(2 * i + 2) * 128, :, :]), b_, reads=[h1T_d], writes=[b_])
                P.op("vector", lambda h: h.tensor_scalar_mul(out=a_[:], in0=a_[:], scalar1=jwt[:, 0:1]), reads=[a_, jwt], writes=[a_])
                P.op("vector", lambda h: h.scalar_tensor_tensor(out=h1o[:, :, i * 128:(i + 1) * 128], in0=b_[:], scalar=jwt[:, 1:2], in1=a_[:],
                                                                op0=ALU.mult, op1=ALU.add), reads=[b_, jwt, a_], writes=[h1o])
        nxt = make_pz(5)
        norm_pss[0] = P.ps("npss2", [128, 512], F32)
        stg = [P.sb(f"stgq{i}", [128, 512], BF16) for i in range(3)]
        stg_i = [0]

        def stage():
            t = stg[stg_i[0] % 3]
            stg_i[0] += 1
            return t
        with P.scope():
            cqn = P.sb("cqn", [128, 6, TRo], BF16)
            with P.scope():
                ld = make_loader(6)
                wts = [ld(w1q, c) for c in range(6)]
                qg = P.sb("qg", [128, 6], F32)
                P.dma("sync", lambda h: h.dma_start(out=qg[:], in_=qn[:]), qg, writes=[qg])
                cf = P.sb("cfq", [128, 6, 512], F32)
                sq = [P.sb(f"sqq{i}", [128, 512], F32) for i in range(2)]
                rst = P.sb("rstq", [128, 512], F32)
                for tt in range(NTo):
                    norm_fm(h1o, 6, nxt, wts, qg, 768.0, cf, sq, rst, lambda c: (cqn[:, c, tt * 512:(tt + 1) * 512], cqn), tt * 512)
            with P.scope():
                wq_ = P.sb("wq_", [128, 6, 1536], BF16); wqs_ = P.sb("wqs_", [128, 6, 512], BF16)
                for kc in range(6):
                    cast_load(wq_, wq_[:, kc, 0:768], wuq[:, kc, 0:768], 768)
                    cast_load(wq_, wq_[:, kc, 768:1536], wuq[:, kc, 768:1536], 768)
                    cast_load(wqs_, wqs_[:, kc, :], wuqs[:, kc, :], 512)
                pos_t = P.sb("pos_tq", [64, 512], F32); ang = P.sb("angq", [64, 512], F32); kf = P.sb("kfq", [64, 512], F32)
                cosT = P.sb("cosTq", [64, 512], F32); sinT = P.sb("sinTq", [64, 512], F32)
                t1 = P.sb("t1q", [64, 512], F32); t2 = P.sb("t2q", [64, 512], F32)
                for tt in range(NTo):
                    rope_tables(pos_own, tt * 512, pos_t, ang, kf, cosT, sinT)
                    for hh in range(8):
                        ps = nxt()
                        proj(wq_, cqn, tt * 512, ps, hh * 192, hh * 192 + 128, nk=6)
                        t = stage()
                        P.op("scalar", lambda h: h.activation(out=t[:], in_=ps[:], func=AF.Copy), reads=[ps], writes=[t])
                        P.dma("scalar", lambda h: h.dma_start(out=qT_d[hh, :, tt * 512:(tt + 1) * 512], in_=t[:]), t, reads=[t], writes=[qT_d])
                        pa, pb_ = nxt(), nxt()
                        proj(wq_, cqn, tt * 512, pa, hh * 192 + 128, hh * 192 + 192, nk=6)
                        proj(wqs_, cqn, tt * 512, pb_, hh * 64, hh * 64 + 64, nk=6)
                        t = stage()
                        rope_apply(pa, pb_, cosT, sinT, t1, t2, t)
                        P.dma("sync", lambda h: h.dma_start(out=qrT_d[hh, :, tt * 512:(tt + 1) * 512], in_=t[0:64, :]), t, reads=[t], writes=[qrT_d])
        with P.scope():
            ld = make_loader(3)
            for (base, dstd, fn) in ((6, mg_d, AF.Silu), (14, dq_d, AF.Copy), (22, dg_d, AF.Silu)):
                for hh in range(8):
                    wt = ld(w1q, base + hh, pre=(base + hh + 1 if base + hh + 1 < 30 else None))
                    for tt in range(NTo):
                        ps = nxt()
                        proj(wt, h1o, tt * 512, ps)
                        t = stage()
                        P.op("scalar", lambda h: h.activation(out=t[:], in_=ps[:], func=fn), reads=[ps], writes=[t])
                        P.dma("scalar", lambda h: h.dma_start(out=dstd[hh, :, tt * 512:(tt + 1) * 512], in_=t[:]), t, reads=[t], writes=[dstd])

    with P.scope():
        yT1 = P.sb("yT1", [128, 16, TRo], BF16)
        mab_f = P.sb("mab_f", [128, 256], F32)
        mab = P.sb("mab", [128, 256], BF16)
        P.dma("sync", lambda h: h.dma_start(out=mab_f[:], in_=maskab[:]), mab_f, writes=[mab_f])
        P.op("vector", lambda h: h.tensor_copy(out=mab[:], in_=mab_f[:]), reads=[mab_f], writes=[mab])

        lacc = [P.sb(f"lacc{i}", [128, 512], F32) for i in range(2)]

        def attn_tile(qt, qk_fn, Vt, scale, ps2, po, pl, PTs):
            i0 = 4 * qt
            mlast = 2 * i0 + 7
            NPT = len(PTs)

            def A(m):
                c0 = max(0, m // 2 - i0) * 128
                ps = ps2[m % len(ps2)]
                PT = PTs[m % NPT]
                qk_fn(ps, m, qt * 512 + c0, c0)
                P.op("scalar", lambda h: h.activation(out=PT[:, c0:512], in_=ps[:, c0:512], func=AF.Exp, scale=scale),
                     reads=[ps], writes=[PT])
                if m >= 2 * i0:
                    mo = (m % 2) * 128
                    P.op("gpsimd", lambda h: h.tensor_tensor(out=PT[:, c0:c0 + 128], in0=PT[:, c0:c0 + 128],
                                                             in1=mab[:, mo:mo + 128], op=ALU.mult), reads=[PT, mab], writes=[PT])

            def B(m):
                c0 = max(0, m // 2 - i0) * 128
                PT = PTs[m % NPT]
                P.op("tensor", lambda h: h.matmul(po[:, c0:512], lhsT=Vt[:, m, :], rhs=PT[:, c0:512], start=(m == 0), stop=(m == mlast)),
                     reads=[Vt, PT], writes=[po], signal=(m == mlast))
                if m == 0:
                    P.op("vector", lambda h: h.tensor_copy(out=lacc[0][:], in_=PT[:]), reads=[PT], writes=[lacc[0]])
                elif m % 3 == 0:
                    P.op("vector", lambda h: h.tensor_tensor(out=lacc[0][:, c0:512], in0=lacc[0][:, c0:512], in1=PT[:, c0:512], op=ALU.add),
                         reads=[lacc[0], PT], writes=[lacc[0]])
                elif m % 3 == 1:
                    P.op("tensor", lambda h: h.matmul(pl[:, c0:512], lhsT=ones_bf[:], rhs=PT[:, c0:512], start=(m == 1), stop=False),
                         reads=[ones_bf, PT], writes=[pl], signal=False)
                else:
                    P.op("gpsimd", lambda h: h.tensor_tensor(out=lacc[1][:, c0:512], in0=lacc[1][:, c0:512], in1=PT[:, c0:512], op=ALU.add),
                         reads=[lacc[1], PT], writes=[lacc[1]])
            P.op("gpsimd", lambda h: h.memset(lacc[1][:], 0.0), writes=[lacc[1]])
            A(0)
            A(1)
            for m in range(2, mlast + 1):
                A(m)
                B(m - 2)
            B(mlast - 1)
            B(mlast)
            P.op("tensor", lambda h: h.matmul(pl[:], lhsT=ones_f[:], rhs=lacc[0][:], start=False, stop=False), reads=[ones_f, lacc[0]], writes=[pl], signal=False)
            P.op("tensor", lambda h: h.matmul(pl[:], lhsT=ones_f[:], rhs=lacc[1][:], start=False, stop=True), reads=[ones_f, lacc[1]], writes=[pl])

        with P.scope():
            krT = P.sb("krT", [128, TR], BF16)
            P.op("gpsimd", lambda h: h.memset(krT[64:128, :], 0.0), writes=[krT])
            P.dma("sync", lambda h: h.dma_start(out=krT[0:64, :], in_=krT_d[:]), krT, reads=[krT_d], writes=[krT])
            KT = [P.sb(f"KT{i}", [128, TR], BF16) for i in range(2)]
            Vt = [P.sb(f"Vt{i}", [128, NB, 128], BF16) for i in range(2)]
            qT = [P.sb(f"qTh{i}", [128, TRo], BF16) for i in range(2)]
            qr = [P.sb(f"qrh{i}", [128, TRo], BF16) for i in range(2)]
            for qq_ in qr:
                P.op("gpsimd", lambda h: h.memset(qq_[64:128, :], 0.0), writes=[qq_])
            mg = [P.sb(f"mgh{i}", [128, TRo], BF16) for i in range(2)]
            PTs = [P.sb(f"PTd{i}", [128, 512], BF16) for i in range(5)]
            rl = [P.sb(f"rl{i}", [128, 512], F32) for i in range(2)]
            ps2 = [P.ps(f"psS{i}", [128, 512], F32) for i in range(3)]
            poo = [P.ps(f"poo{i}", [128, 512], F32) for i in range(2)]
            pll = [P.ps(f"pll{i}", [128, 512], F32) for i in range(2)]
            it = 0
            for hh in range(8):
                q = hh % 2
                P.dma("sync", lambda h: h.dma_start(out=KT[q][:], in_=KT_d[hh]), KT[q], reads=[KT_d], writes=[KT[q]])
                P.dma("sync", lambda h: h.dma_start(out=Vt[q][:], in_=V_d[hh]), Vt[q], reads=[V_d], writes=[Vt[q]])
                P.dma("sync", lambda h: h.dma_start(out=qT[q][:], in_=qT_d[hh]), qT[q], reads=[qT_d], writes=[qT[q]])
                P.dma("sync", lambda h: h.dma_start(out=qr[q][0:64, :], in_=qrT_d[hh]), qr[q], reads=[qrT_d], writes=[qr[q]])
                P.dma("sync", lambda h: h.dma_start(out=mg[q][:], in_=mg_d[hh]), mg[q], reads=[mg_d], writes=[mg[q]])
                for qt in range(NTo):
                    po, pl = poo[it % 2], pll[it % 2]
                    rlt = rl[it % 2]
                    it += 1

                    def qk(ps, m, qc, c0):
                        P.op("tensor", lambda h: h.matmul(ps[:, c0:512], lhsT=KT[q][:, m * 128:(m + 1) * 128], rhs=qT[q][:, qc:qt * 512 + 512],
                                                          start=True, stop=False), reads=[KT[q], qT[q]], writes=[ps], signal=False)
                        P.op("tensor", lambda h: h.matmul(ps[:, c0:512], lhsT=krT[:, m * 128:(m + 1) * 128], rhs=qr[q][:, qc:qt * 512 + 512],
                                                          start=False, stop=True), reads=[krT, qr[q]], writes=[ps])
                    attn_tile(qt, qk, Vt[q], 192.0 ** -0.5, ps2, po, pl, PTs)
                    P.op("scalar", lambda h: h.activation(out=rlt[:], in_=pl[:], func=AF.Ln), reads=[pl], writes=[rlt])
                    P.op("scalar", lambda h: h.activation(out=rlt[:], in_=rlt[:], func=AF.Exp, scale=-1.0), reads=[rlt], writes=[rlt])
                    P.op("vector", lambda h: h.tensor_tensor(out=rlt[:], in0=po[:], in1=rlt[:], op=ALU.mult), reads=[po, rlt], writes=[rlt])
                    P.op("gpsimd", lambda h: h.tensor_tensor(out=yT1[:, hh, qt * 512:(qt + 1) * 512], in0=rlt[:],
                                                             in1=mg[q][:, qt * 512:(qt + 1) * 512], op=ALU.mult),
                         reads=[rlt, mg[q]], writes=[yT1])

        with P.scope():
            LINIT = 0.8 - 0.6 * float(np.exp(-0.3 * 1))
            lm = P.sb("lm", [128, 256], F32)
            P.dma("sync", lambda h: h.dma_start(out=lm[:], in_=lams[:].partition_broadcast(128)), lm, writes=[lm])
            lp = P.sb("lp", [128, 2, 64], F32)
            ls = P.sb("ls", [128, 2], F32)
            nlam = P.sb("nlam", [128, 1], F32)
            for a in range(2):
                P.op("vector", lambda h: h.tensor_tensor(out=lp[:, a, :], in0=lm[:, a * 128:a * 128 + 64], in1=lm[:, a * 128 + 64:a * 128 + 128],
                                                         op=ALU.mult), reads=[lm], writes=[lp])
            P.op("vector", lambda h: h.tensor_reduce(out=ls[:], in_=lp[:], axis=AX.X, op=ALU.add), reads=[lp], writes=[ls])
            P.op("scalar", lambda h: h.activation(out=ls[:], in_=ls[:], func=AF.Exp), reads=[ls], writes=[ls])
            P.op("vector", lambda h: h.tensor_tensor(out=nlam[:], in0=ls[:, 1:2], in1=ls[:, 0:1], op=ALU.subtract), reads=[ls], writes=[nlam])
            P.op("vector", lambda h: h.tensor_scalar_add(out=nlam[:], in0=nlam[:], scalar1=-LINIT), reads=[nlam], writes=[nlam])
            sln = P.sb("sln", [128, 1], F32)
            P.dma("sync", lambda h: h.dma_start(out=sln[:], in_=subln[:]), sln, writes=[sln])
            P.op("vector", lambda h: h.tensor_scalar_mul(out=sln[:], in0=sln[:], scalar1=1.0 - LINIT), reads=[sln], writes=[sln])
            dk = [P.sb(f"dk{i}", [128, TR], BF16) for i in range(2)]
            dvt = [P.sb(f"dvt{i}", [128, NB, 128], BF16) for i in range(2)]
            dq = [[P.sb(f"dq{i}_{c}", [128, TRo], BF16) for c in range(2)] for i in range(2)]
            for i_ in range(2):
                P.op("gpsimd", lambda h: h.memset(dq[i_][0][64:128, :], 0.0), writes=[dq[i_][0]])
                P.op("gpsimd", lambda h: h.memset(dq[i_][1][0:64, :], 0.0), writes=[dq[i_][1]])
            dg = [P.sb(f"dg{i}", [128, TRo], BF16) for i in range(2)]
            PTs = [P.sb(f"PTe{i}", [128, 512], BF16) for i in range(5)]
            r0t = P.sb("r0t", [128, 512], F32); r1t = P.sb("r1t", [128, 512], F32); sqt = P.sb("sqt", [128, 512], F32)
            ps2 = [P.ps(f"peS{i}", [128, 512], F32) for i in range(3)]
            poo = [P.ps(f"peo{i}", [128, 512], F32) for i in range(2)]
            pll = [P.ps(f"pel{i}", [128, 512], F32) for i in range(2)]
            pss = P.ps("pess", [128, 512], F32)
            for hh in range(8):
                q = hh % 2
                P.dma("sync", lambda h: h.dma_start(out=dk[q][:], in_=dkT_d[hh]), dk[q], reads=[dkT_d], writes=[dk[q]])
                P.dma("sync", lambda h: h.dma_start(out=dvt[q][:], in_=dv_d[hh]), dvt[q], reads=[dv_d], writes=[dvt[q]])
                P.dma("sync", lambda h: h.dma_start(out=dq[q][0][0:64, :], in_=dq_d[hh, 0:64, :]), dq[q][0], reads=[dq_d], writes=[dq[q][0]])
                P.dma("sync", lambda h: h.dma_start(out=dq[q][1][64:128, :], in_=dq_d[hh, 64:128, :]), dq[q][1], reads=[dq_d], writes=[dq[q][1]])
                P.dma("sync", lambda h: h.dma_start(out=dg[q][:], in_=dg_d[hh]), dg[q], reads=[dg_d], writes=[dg[q]])
                for qt in range(NTo):
                    for c in range(2):
                        def qk(ps, m, qc, c0):
                            P.op("tensor", lambda h: h.matmul(ps[:, c0:512], lhsT=dk[q][:, m * 128:(m + 1) * 128],
                                                              rhs=dq[q][c][:, qc:qt * 512 + 512], start=True, stop=True),
                                 reads=[dk[q], dq[q][c]], writes=[ps])
                        attn_tile(qt, qk, dvt[q], 0.125, ps2, poo[c], pll[c], PTs)
                    P.op("scalar", lambda h: h.activation(out=r0t[:], in_=pll[0][:], func=AF.Ln), reads=[pll[0]], writes=[r0t])
                    P.op("scalar", lambda h: h.activation(out=r0t[:], in_=r0t[:], func=AF.Exp, scale=-1.0), reads=[r0t], writes=[r0t])
                    P.op("vector", lambda h: h.tensor_tensor(out=r0t[:], in0=poo[0][:], in1=r0t[:], op=ALU.mult), reads=[poo[0], r0t], writes=[r0t])
                    P.op("scalar", lambda h: h.activation(out=r1t[:], in_=pll[1][:], func=AF.Ln), reads=[pll[1]], writes=[r1t])
                    P.op("scalar", lambda h: h.activation(out=r1t[:], in_=r1t[:], func=AF.Exp, scale=-1.0), reads=[r1t], writes=[r1t])
                    P.op("vector", lambda h: h.tensor_tensor(out=r1t[:], in0=poo[1][:], in1=r1t[:], op=ALU.mult), reads=[poo[1], r1t], writes=[r1t])
                    P.op("vector", lambda h: h.scalar_tensor_tensor(out=r0t[:], in0=r1t[:], scalar=nlam[:, 0:1], in1=r0t[:],
                                                                    op0=ALU.mult, op1=ALU.add), reads=[r1t, nlam, r0t], writes=[r0t])
                    P.op("scalar", lambda h: h.activation(out=sqt[:], in_=r0t[:], func=AF.Square), reads=[r0t], writes=[sqt])
                    P.op("tensor", lambda h: h.matmul(pss[:], lhsT=ones_f[:], rhs=sqt[:], start=True, stop=True), reads=[ones_f, sqt], writes=[pss])
                    P.op("vector", lambda h: h.tensor_scalar(out=r1t[:], in0=pss[:], scalar1=1.0 / 128, scalar2=EPS, op0=ALU.mult, op1=ALU.add),
                         reads=[pss], writes=[r1t])
                    P.op("scalar", lambda h: h.activation(out=r1t[:], in_=r1t[:], func=AF.Ln), reads=[r1t], writes=[r1t])
                    P.op("scalar", lambda h: h.activation(out=r1t[:], in_=r1t[:], func=AF.Exp, scale=-0.5), reads=[r1t], writes=[r1t])
                    P.op("vector", lambda h: h.tensor_tensor(out=r0t[:], in0=r0t[:], in1=r1t[:], op=ALU.mult), reads=[r0t, r1t], writes=[r0t])
                    P.op("vector", lambda h: h.scalar_tensor_tensor(out=yT1[:, 8 + hh, qt * 512:(qt + 1) * 512], in0=r0t[:], scalar=sln[:, 0:1],
                                                                    in1=dg[q][:, qt * 512:(qt + 1) * 512], op0=ALU.mult, op1=ALU.mult),
                         reads=[r0t, sln, dg[q]], writes=[yT1])

        with P.scope():
            gfb = P.sb("gfb", [128, D], F32)
            P.dma("sync", lambda h: h.dma_start(out=gfb[:], in_=gf[:].partition_broadcast(128)), gfb, writes=[gfb])

            def ytf(t4):
                return Buf(yT1.t[:, :, t4 * 512:(t4 + 1) * 512], "ytv")

            xra = [P.sb(f"xra{i}", [128, D], F32) for i in range(2)]
            xrb = [P.sb(f"xrb{i}", [128, D], F32) for i in range(2)]

            def resid(n):
                a_, b_ = xra[n % 2], xrb[n % 2]
                P.dma("sync", lambda h: h.dma_start(out=a_[:], in_=x1_d[(2 * n) * 128:(2 * n + 1) * 128, :]), a_, reads=[x1_d], writes=[a_])
                P.dma("sync", lambda h: h.dma_start(out=b_[:], in_=x1_d[(2 * n + 1) * 128:(2 * n + 2) * 128, :]), b_, reads=[x1_d], writes=[b_])
                return (a_, b_)

            def combine(g, pbank, x1n, rh):
                a_, b_ = rh
                sl = slice(g * 512, (g + 1) * 512)
                P.op("vector", lambda h: h.scalar_tensor_tensor(out=x1n[:, sl], in0=a_[:, sl], scalar=jwt[:, 0:1], in1=pbank[:],
                                                                op0=ALU.mult, op1=ALU.add), reads=[a_, jwt, pbank], writes=[x1n])
                P.op("vector", lambda h: h.scalar_tensor_tensor(out=x1n[:, sl], in0=b_[:, sl], scalar=jwt[:, 1:2], in1=x1n[:, sl],
                                                                op0=ALU.mult, op1=ALU.add), reads=[b_, jwt, x1n], writes=[x1n])

            def emit(n, x2, xnn, junk, pst):
                s_ = stats[n % 2]
                P.op("scalar", lambda h: h.activation(out=junk[:], in_=x2[:], func=AF.Square, accum_out=s_[0][:]), reads=[x2], writes=[junk, s_[0]])
                P.op("vector", lambda h: h.tensor_scalar(out=s_[1][:], in0=s_[0][:], scalar1=1.0 / D, scalar2=EPS, op0=ALU.mult, op1=ALU.add),
                     reads=[s_[0]], writes=[s_[1]])
                P.op("scalar", lambda h: h.activation(out=s_[2][:], in_=s_[1][:], func=AF.Sqrt), reads=[s_[1]], writes=[s_[2]])
                P.op("vector", lambda h: h.reciprocal(out=s_[3][:], in_=s_[2][:]), reads=[s_[2]], writes=[s_[3]])
                o = x2
                P.op("vector", lambda h: h.scalar_tensor_tensor(out=o[:], in0=x2[:], scalar=s_[3][:], in1=gfb[:], op0=ALU.mult, op1=ALU.mult),
                     reads=[x2, s_[3], gfb], writes=[o])
                P.dma("sync", lambda h: h.dma_start(out=out[n * 128:(n + 1) * 128, :], in_=o[:]), o, reads=[o], writes=[out])
            outproj_phase(wo1, ytf, NTo, resid, combine, emit, need_xn=False)

    P.full_barrier()
    return P.finish(), I


_CACHE = {}


def run_full(inputs, TR=4096):
    inputs = {k: np.asarray(v) for k, v in inputs.items()}
    if TR not in _CACHE:
        _CACHE[TR] = build(TR=TR)
    nc, I = _CACHE[TR]
    ins = []
    for c in range(8):
        d = prep_inputs(inputs, c, TR)
        ins.append({k: v for k, v in d.items() if k in I})
    res = run_bass_kernel_spmd(nc, ins, core_ids=list(range(8)))
    B = inputs["x"].shape[0]
    outp = np.zeros((B, TR, D), np.float32)
    NOWN = TR // 256
    for c in range(8):
        b, j = c // 2, c % 2
        o = np.asarray(res.results[c]["out"])
        for i in range(NOWN):
            g = 2 * i + j
            outp[b, g * 128:(g + 1) * 128] = o[i * 128:(i + 1) * 128]
    return outp


def kernel(**inputs):
    return run_full(inputs, 4096)
```
